# Optimizing a Trainium2 kernel written in Bass

```python
import jax, jax.numpy as jnp
from jax import lax
import numpy as np

D_MODEL = 2048
BATCH = 16
SEQ = 2048
DEPTH = 4

N_MIXERS = 3
RMS_EPS = 1e-6

RWKV_HEAD_DIM = 64
RWKV_HEADS = D_MODEL // RWKV_HEAD_DIM
DECAY_LORA = 96
AAA_LORA = 96
MV_LORA = 64
RWKV_LN_EPS = 64e-5

ATT_HEAD_DIM = 128
DILATED_CONFIG = ((128, 1), (512, 4), (2048, 16))
ATT_GROUPS = len(DILATED_CONFIG)
ATT_HEADS_PER_GROUP = D_MODEL // (2 * ATT_HEAD_DIM)
ATT_BRANCH = ATT_HEADS_PER_GROUP * ATT_HEAD_DIM
ATT_QKV = 3 * ATT_GROUPS * ATT_BRANCH
ATT_BLOCK = 128
ROPE_THETA = 500000.0
ROPE_DIM = ATT_HEAD_DIM // 4

POOL_WINDOWS = (2, 4, 8, 16)
POOL_WIDTH = D_MODEL
POOL_GROUP = POOL_WIDTH // len(POOL_WINDOWS)

kernel_name = "hybrid_rwkv7_dilated_attn_pool_trunk"


def _rms_norm(x, w):
    xf = x.astype(jnp.float32)
    y = xf * lax.rsqrt(jnp.mean(xf * xf, axis=-1, keepdims=True) + RMS_EPS)
    return (y * w.astype(jnp.float32)).astype(x.dtype)


def _rwkv7_scan(r, w, k, v, a, b):
    B, S, H, N = r.shape

    def step(state, inp):
        r_t, w_t, k_t, v_t, a_t, b_t = inp
        sa = jnp.einsum('bhij,bhj->bhi', state, a_t)
        state = (state * w_t[:, :, None, :] + sa[..., None] * b_t[:, :, None, :]
                 + v_t[..., None] * k_t[:, :, None, :])
        y_t = jnp.einsum('bhij,bhj->bhi', state, r_t)
        return state, y_t

    xs = tuple(jnp.moveaxis(t, 1, 0) for t in (r, w, k, v, a, b))
    init = jnp.zeros((B, H, N, N), jnp.float32)
    _, ys = lax.scan(step, init, xs)
    return jnp.moveaxis(ys, 0, 1)


def _head_group_norm(y, w, b):
    B, S, H, N = y.shape
    mu = jnp.mean(y, axis=-1, keepdims=True)
    yc = y - mu
    var = jnp.mean(yc * yc, axis=-1, keepdims=True)
    yn = (yc * lax.rsqrt(var + RWKV_LN_EPS)).reshape(B, S, H * N)
    return yn * w.astype(jnp.float32) + b.astype(jnp.float32)


def _rwkv7_mixer(h, mu, w_in, w0, w1, w2, a0, a1, a2, k_k, k_a, r_k, lnx_w, lnx_b,
                 w_out, v_first, vres):
    B, S, D = h.shape
    H, N = RWKV_HEADS, RWKV_HEAD_DIM
    heads = lambda t: t.reshape(B, S, H, N)
    xx = jnp.pad(h, ((0, 0), (1, 0), (0, 0)))[:, :S] - h
    r, k, v, g = [(h + xx * mu[j]) @ w_in[j] for j in range(4)]
    xw = h + xx * mu[4]
    xa = h + xx * mu[5]
    w_log = -jax.nn.softplus(-(w0 + jnp.tanh(xw @ w1) @ w2).astype(jnp.float32)) - 0.5
    decay = jnp.exp(-jnp.exp(w_log))
    if vres is None:
        v_first = v
    else:
        v0, v1, v2 = vres
        xv = h + xx * mu[2]
        v = v + (v_first - v) * jax.nn.sigmoid(v0 + (xv @ v1) @ v2)
    a = jax.nn.sigmoid(a0 + (xa @ a1) @ a2)
    kk = heads(k * k_k).astype(jnp.float32)
    kk = kk / jnp.maximum(jnp.sqrt(jnp.sum(kk * kk, axis=-1, keepdims=True)), 1e-12)
    k = k * (1 + (a - 1) * k_a)
    af = heads(a).astype(jnp.float32)
    vf = heads(v).astype(jnp.float32)
    y = _rwkv7_scan(heads(r).astype(jnp.float32), heads(decay), heads(k).astype(jnp.float32),
                    vf, -kk, kk * af)
    y = _head_group_norm(y, lnx_w, lnx_b)
    bonus = jnp.sum(heads(r * k).astype(jnp.float32) * r_k.astype(jnp.float32),
                    axis=-1, keepdims=True) * vf
    y = (y + bonus.reshape(B, S, D)).astype(h.dtype) * jax.nn.silu(g)
    return y @ w_out, v_first


def _rope_tables(positions):
    inv_freq = ROPE_THETA ** (-jnp.arange(0, ROPE_DIM, 2, dtype=jnp.float32) / ROPE_DIM)
    ang = positions.astype(jnp.float32)[..., None] * inv_freq
    return jnp.cos(ang), jnp.sin(ang)


def _apply_partial_rope(x, cos, sin):
    half = ROPE_DIM // 2
    c = cos[:, :, None, None, :].astype(x.dtype)
    s = sin[:, :, None, None, :].astype(x.dtype)
    x1 = x[..., :half]
    x2 = x[..., half:ROPE_DIM]
    return jnp.concatenate([x1 * c - x2 * s, x1 * s + x2 * c, x[..., ROPE_DIM:]], axis=-1)


def _dilated_window_attention(q, k, v, window, dilation):
    B, S, H, Dh = q.shape
    steps = window // dilation
    L = S // dilation
    nblk = -(-L // ATT_BLOCK)
    pad = nblk * ATT_BLOCK - L

    def gather(t):
        t = t.reshape(B, L, dilation, H, Dh).transpose(0, 2, 3, 1, 4)
        t = jnp.pad(t, ((0, 0), (0, 0), (0, 0), (0, pad), (0, 0)))
        return t.reshape(B, dilation, H, nblk, ATT_BLOCK, Dh)

    def with_prev(t):
        prev = jnp.pad(t[:, :, :, :-1], ((0, 0), (0, 0), (0, 0), (1, 0), (0, 0), (0, 0)))
        return jnp.concatenate([prev, t], axis=4)

    qb = gather(q)
    kc = with_prev(gather(k))
    vc = with_prev(gather(v))
    qi = jnp.arange(ATT_BLOCK)[:, None]
    kj = jnp.arange(2 * ATT_BLOCK)[None, :]
    dist = ATT_BLOCK + qi - kj
    key_idx = (jnp.arange(nblk)[:, None, None] - 1) * ATT_BLOCK + kj[None]
    mask = (dist >= 0)[None] & (dist <= steps)[None] & (key_idx >= 0)
    s = jnp.einsum('bghnqe,bghnke->bghnqk', qb, kc).astype(jnp.float32) * (Dh ** -0.5)
    s = jnp.where(mask, s, -jnp.inf)
    m = jnp.max(s, axis=-1, keepdims=True)
    p = jnp.exp(s - m)
    l = jnp.sum(p, axis=-1, keepdims=True)
    o = jnp.einsum('bghnqk,bghnke->bghnqe', p, vc.astype(jnp.float32)) / l
    lse = (m + jnp.log(l))[..., 0]
    o = o.reshape(B, dilation, H, nblk * ATT_BLOCK, Dh)[:, :, :, :L]
    o = o.transpose(0, 3, 1, 2, 4).reshape(B, S, H, Dh)
    lse = lse.reshape(B, dilation, H, nblk * ATT_BLOCK)[:, :, :, :L]
    lse = lse.transpose(0, 3, 1, 2).reshape(B, S, H)
    return o, lse


def _dilated_attn_mixer(h, positions, w_in, qn_w, kn_w, w_out):
    B, S, _ = h.shape
    proj = h @ w_in
    qkv = proj[..., :ATT_QKV].reshape(B, S, 3, ATT_GROUPS, ATT_HEADS_PER_GROUP, ATT_HEAD_DIM)
    gate = proj[..., ATT_QKV:]
    cos, sin = _rope_tables(positions)
    q = _apply_partial_rope(_rms_norm(qkv[:, :, 0], qn_w[:, None, :]), cos, sin)
    k = _apply_partial_rope(_rms_norm(qkv[:, :, 1], kn_w[:, None, :]), cos, sin)
    v = qkv[:, :, 2]
    outs, lses = [], []
    for gi, (window, dilation) in enumerate(DILATED_CONFIG):
        o_g, lse_g = _dilated_window_attention(q[:, :, gi], k[:, :, gi], v[:, :, gi], window, dilation)
        outs.append(o_g)
        lses.append(lse_g)
    wts = jax.nn.softmax(jnp.stack(lses, axis=0), axis=0)
    o = jnp.sum(wts[..., None] * jnp.stack(outs, axis=0), axis=0)
    o = o.astype(h.dtype).reshape(B, S, ATT_BRANCH)
    return (o * jax.nn.silu(gate)) @ w_out


def _pool_mixer(h, w_in, w_grp, scale, w_out):
    B, S, _ = h.shape
    u, gate = jnp.split(h @ w_in, 2, axis=-1)
    c = jnp.pad(jnp.cumsum(u.astype(jnp.float32), axis=1), ((0, 0), (1, 0), (0, 0)))
    t = jnp.arange(S)
    pooled = []
    for j, win in enumerate(POOL_WINDOWS):
        cg = c[..., j * POOL_GROUP:(j + 1) * POOL_GROUP]
        lower = jnp.pad(cg[:, :S + 1 - win], ((0, 0), (win - 1, 0), (0, 0)))
        cnt = jnp.minimum(t + 1, win).astype(jnp.float32)[None, :, None]
        pooled.append((cg[:, 1:] - lower) / cnt)
    pooled = jnp.stack(pooled, axis=2).astype(h.dtype)
    diff = pooled - u.reshape(B, S, len(POOL_WINDOWS), POOL_GROUP)
    y = jnp.einsum('bsgc,gce->bsge', diff, w_grp).reshape(B, S, POOL_WIDTH) * scale
    return (y * jax.nn.silu(gate)) @ w_out


def setup_inputs(seed: int = 0) -> dict:
    key = jax.random.key(seed)
    ks = iter(jax.random.split(key, 40))
    f32 = jnp.float32
    n_a = (DEPTH + 2) // 3
    n_b = (DEPTH + 1) // 3
    n_c = DEPTH // 3
    D, H, N = D_MODEL, RWKV_HEADS, RWKV_HEAD_DIM

    def nrm(shape, scale):
        return jax.random.normal(next(ks), shape, f32) * scale

    x = jax.random.normal(next(ks), (BATCH, SEQ, D), f32)
    offset = jax.random.randint(next(ks), (BATCH, 1), 0, 4096, dtype=jnp.int32)
    positions = offset + jnp.arange(SEQ, dtype=jnp.int32)[None, :]
    norm_w = 1.0 + nrm((DEPTH, D), 0.1)
    lin = jnp.linspace(0.0, 1.0, D, dtype=f32)
    inp = {
        'x': x,
        'positions': positions,
        'norm_w': norm_w,
        'a_mu': jax.random.uniform(next(ks), (n_a, 6, D), f32),
        'a_w_in': nrm((n_a, 4, D, D), D ** -0.5),
        'a_w0': (-6.5 + 5.0 * lin ** 1.5)[None, :] + nrm((n_a, D), 0.1),
        'a_w1': nrm((n_a, D, DECAY_LORA), D ** -0.5),
        'a_w2': nrm((n_a, DECAY_LORA, D), 0.1 * DECAY_LORA ** -0.5),
        'a_a0': nrm((n_a, D), 0.3),
        'a_a1': nrm((n_a, D, AAA_LORA), D ** -0.5),
        'a_a2': nrm((n_a, AAA_LORA, D), 0.3 * AAA_LORA ** -0.5),
        'a_k_k': 0.85 + nrm((n_a, D), 0.05),
        'a_k_a': 1.0 + nrm((n_a, D), 0.05),
        'a_r_k': nrm((n_a, H, N), 0.1),
        'a_lnx_w': 1.0 + nrm((n_a, D), 0.1),
        'a_lnx_b': nrm((n_a, D), 0.02),
        'a_v0': 1.0 + nrm((n_a - 1, D), 0.1),
        'a_v1': nrm((n_a - 1, D, MV_LORA), D ** -0.5),
        'a_v2': nrm((n_a - 1, MV_LORA, D), 0.3 * MV_LORA ** -0.5),
        'a_w_out': nrm((n_a, D, D), 0.5 * D ** -0.5),
        'b_w_in': nrm((n_b, D, ATT_QKV + ATT_BRANCH), D ** -0.5),
        'b_qn_w': 1.0 + nrm((n_b, ATT_GROUPS, ATT_HEAD_DIM), 0.1),
        'b_kn_w': 1.0 + nrm((n_b, ATT_GROUPS, ATT_HEAD_DIM), 0.1),
        'b_w_out': nrm((n_b, ATT_BRANCH, D), 0.5 * ATT_BRANCH ** -0.5),
        'c_w_in': nrm((n_c, D, 2 * POOL_WIDTH), D ** -0.5),
        'c_w_grp': nrm((n_c, len(POOL_WINDOWS), POOL_GROUP, POOL_GROUP), POOL_GROUP ** -0.5),
        'c_scale': 1.0 + nrm((n_c, POOL_WIDTH), 0.1),
        'c_w_out': nrm((n_c, POOL_WIDTH, D), 0.5 * POOL_WIDTH ** -0.5),
    }
    return inp


def reference(x, positions, norm_w, a_mu, a_w_in, a_w0, a_w1, a_w2, a_a0, a_a1, a_a2,
              a_k_k, a_k_a, a_r_k, a_lnx_w, a_lnx_b, a_v0, a_v1, a_v2, a_w_out,
              b_w_in, b_qn_w, b_kn_w, b_w_out, c_w_in, c_w_grp, c_scale, c_w_out):
    ia = ib = ic = 0
    v_first = None
    for i in range(DEPTH):
        h = _rms_norm(x, norm_w[i])
        kind = i % N_MIXERS
        if kind == 0:
            vres = None if ia == 0 else (a_v0[ia - 1], a_v1[ia - 1], a_v2[ia - 1])
            y, vf = _rwkv7_mixer(h, a_mu[ia], a_w_in[ia], a_w0[ia], a_w1[ia], a_w2[ia],
                                 a_a0[ia], a_a1[ia], a_a2[ia], a_k_k[ia], a_k_a[ia], a_r_k[ia],
                                 a_lnx_w[ia], a_lnx_b[ia], a_w_out[ia], v_first, vres)
            if v_first is None:
                v_first = vf
            ia += 1
        elif kind == 1:
            y = _dilated_attn_mixer(h, positions, b_w_in[ib], b_qn_w[ib], b_kn_w[ib], b_w_out[ib])
            ib += 1
        else:
            y = _pool_mixer(h, c_w_in[ic], c_w_grp[ic], c_scale[ic], c_w_out[ic])
            ic += 1
        x = x + y
    return x
```

```python
import numpy as np
from contextlib import ExitStack
import concourse.bass as bass
import concourse.mybir as mybir
from concourse.bass_utils import run_bass_kernel_spmd

F32 = mybir.dt.float32
BF16 = mybir.dt.bfloat16
I32 = mybir.dt.int32
AF = mybir.ActivationFunctionType
ALU = mybir.AluOpType
AX = mybir.AxisListType

D = 2048
NFT = 16
SEQ = 2048
NB_CORE = 2
TP = 512
NPB = SEQ // TP
RMS_EPS = 1e-6
NDMA = 24
NDMA_SW = 8
SAME_ENGINE_SYNC = True


class Buf:
    __slots__ = ("w", "r")

    def __init__(self):
        self.w = None
        self.r = {}


class TT:
    def __init__(self, t):
        self.t = t
        self.b = Buf()

    def __getitem__(self, k):
        return self.t[k]


class Sched:
    def __init__(self, nc, es):
        self.nc = nc
        self.eng = {"pe": nc.tensor, "act": nc.scalar, "dve": nc.vector, "pool": nc.gpsimd, "sp": nc.sync}
        self.sem = {}
        self.cnt = {}
        for e in ("pe", "act", "dve", "pool"):
            self.sem[e] = es.enter_context(nc.semaphore("s_" + e))
            self.cnt[e] = 0
        for i in range(NDMA):
            self.sem[("dma", i)] = es.enter_context(nc.semaphore("s_dma%d" % i))
            self.cnt[("dma", i)] = 0
        self.seen = {e: {} for e in self.eng}
        self.rr = 0
        self.rr_sw = 0
        self.out_tokens = []
        self.nwait = 0

    def wait(self, e, tok):
        if tok is None:
            return
        key, val = tok
        if key == e and (e == "pe" or not SAME_ENGINE_SYNC):
            return
        if self.seen[e].get(key, 0) >= val:
            return
        self.eng[e].wait_ge(self.sem[key], val)
        self.seen[e][key] = val
        self.nwait += 1

    def _deps(self, e, reads, writes):
        for b in reads:
            self.wait(e, b.b.w)
        for b in writes:
            self.wait(e, b.b.w)
            for k, v in b.b.r.items():
                self.wait(e, (k, v))

    def _commit(self, tok, reads, writes):
        k, v = tok
        for b in reads:
            if b.b.r.get(k, 0) < v:
                b.b.r[k] = v
        for b in writes:
            b.b.w = tok
            b.b.r = {}

    def op(self, e, fn, reads=(), writes=(), signal=True):
        self._deps(e, reads, writes)
        ins = fn(self.eng[e])
        if signal:
            self.cnt[e] += 1
            ins.then_inc(self.sem[e], 1)
            tok = (e, self.cnt[e])
        else:
            tok = (e, self.cnt[e] + 1)
        self._commit(tok, reads, writes)
        return tok

    def dma(self, q, out, in_, reads=(), writes=(), is_output=False):
        self._deps(q, reads, writes)
        if q == "pool":
            i = self.rr_sw
            self.rr_sw = (self.rr_sw + 1) % NDMA_SW
        else:
            i = NDMA_SW + self.rr
            self.rr = (self.rr + 1) % (NDMA - NDMA_SW)
        key = ("dma", i)
        if self.cnt[key] > 0:
            self.wait(q, (key, self.cnt[key]))
        self.cnt[key] += 16
        self.eng[q].dma_start(out=out, in_=in_).then_inc(self.sem[key], 16)
        tok = (key, self.cnt[key])
        self._commit(tok, reads, writes)
        if is_output:
            self.out_tokens.append(tok)
        return tok

    def barrier(self):
        for e in ("pe", "act", "dve", "pool", "sp"):
            for k in list(self.cnt.keys()):
                if self.cnt[k] > 0 and k != e:
                    self.wait(e, (k, self.cnt[k]))

    def finish(self):
        for tok in self.out_tokens:
            self.wait("sp", tok)
        for e in ("pe", "act", "dve", "pool"):
            if self.cnt[e] > 0:
                self.wait("sp", (e, self.cnt[e]))


def bcast_rows(ap_row, nparts):
    t = ap_row.tensor
    pairs = list(ap_row.ap)
    return bass.AP(t, ap_row.offset, [[0, nparts]] + [list(p) for p in pairs[1:]])


class Prog:
    def __init__(self, layers=(0, 1, 2, 3), n_panels=NB_CORE * NPB):
        self.layers = layers
        self.n_panels = n_panels
        self.nc = bass.Bass("TRN2", target_bir_lowering=False)
        self.es = ExitStack()
        self.inputs = {}
        self.uid = 0
        self.scopes = []

    def dram_in(self, name, shape, dtype=F32):
        if name in self.inputs:
            return self.inputs[name]
        t = self.nc.dram_tensor(name, list(shape), dtype, kind="ExternalInput")
        self.inputs[name] = t
        return t

    def sb(self, name, shape, dtype):
        self.uid += 1
        st = self.scopes[-1] if self.scopes else self.es
        return TT(st.enter_context(self.nc.sbuf_tensor("sb%d_%s" % (self.uid, name), list(shape), dtype)))

    def scope(self):
        prog = self

        class _Scope:
            def __enter__(self_):
                self_.st = ExitStack()
                prog.scopes.append(self_.st)
                return self_

            def __exit__(self_, *a):
                prog.S.barrier()
                prog.scopes.pop()
                self_.st.close()
                return False
        return _Scope()

    def build(self):
        nc = self.nc
        es = self.es
        S = self.S = Sched(nc, es)
        x_in = self.dram_in("x", [NB_CORE * SEQ, D])
        self.ident_d = self.dram_in("ident", [128, 128])
        self.normw_d = self.dram_in("norm_w", [4, D])
        out_d = nc.dram_tensor("out", [NB_CORE * SEQ, D], F32, kind="ExternalOutput")
        nl = len(self.layers)
        scr = [nc.dram_tensor("xs%d" % i, [NB_CORE * SEQ, D], F32, kind="Internal") for i in range(2)]
        chain = [x_in]
        for i in range(nl - 1):
            chain.append(scr[i % 2])
        chain.append(out_d)
        self.xbufs = {}

        def xbuf(t, tile):
            k = (t.name, tile)
            if k not in self.xbufs:
                self.xbufs[k] = TT(None)
            return self.xbufs[k]
        self.xbuf = xbuf

        self.ident = self.sb("ident", [128, 128], BF16)
        S.dma("pool", self.ident[:], self.ident_d.ap()[:, :], writes=[self.ident])
        self.ps = [TT(es.enter_context(nc.psum_tensor("ps%d" % i, [128, 512], F32))) for i in range(8)]
        self.ps_i = 0

        for li, L in enumerate(self.layers):
            src, dst = chain[li], chain[li + 1]
            last = li == nl - 1
            with self.scope():
                if L == 2:
                    self.layer_pool(src, dst, last)
                elif L == 1:
                    self.layer_attn(src, dst, last)
                else:
                    self.layer_rwkv(L, src, dst, last)
        S.finish()
        return nc

    def psum(self):
        p = self.ps[self.ps_i]
        self.ps_i = (self.ps_i + 1) % 7
        return p

    def alloc_wsl(self, n=2):
        self.wsl = [self.sb("wsl%d" % i, [128, NFT, 512], BF16) for i in range(n)]
        self.wsl_i = 0

    def wslab(self):
        w = self.wsl[self.wsl_i]
        self.wsl_i = (self.wsl_i + 1) % len(self.wsl)
        return w

    def load_wslab(self, w_ap2d, c0, ncols=512, nrows_t=NFT):
        w = self.wslab()
        src = w_ap2d[:, c0:c0 + ncols].rearrange("(a p) c -> p a c", p=128)
        self.S.dma("pool", w[:, 0:nrows_t, 0:ncols], src, writes=[w])
        return w

    def alloc_norm(self, layer, nxt):
        self.normw = self.sb("normw", [128, D], F32)
        row = self.normw_d.ap()[layer:layer + 1, :]
        self.S.dma("sp", self.normw[:], bcast_rows(row, 128), writes=[self.normw])
        self.xt = [self.sb("xt%d" % i, [128, D], F32) for i in range(nxt)]
        self.hb = [self.sb("hb%d" % i, [128, D], BF16) for i in range(2)]
        self.junk = self.sb("junk", [128, D], BF16)
        self.stat = [self.sb("stat%d" % i, [128, 4], F32) for i in range(2)]
        self.epsb = self.sb("epsb", [128, 1], F32)
        self.S.op("dve", lambda e: e.memset(self.epsb[:], RMS_EPS), writes=[self.epsb])

    def norm_tiles(self, src, tile0, ntiles, hT):
        S = self.S
        nx = len(self.xt)
        for i in range(ntiles):
            tile = tile0 + i
            xt = self.xt[i % nx]
            st = self.stat[i % 2]
            S.dma("sp", xt[:], src.ap()[tile * 128:(tile + 1) * 128, :], reads=[self.xbuf(src, tile)], writes=[xt])
            S.op("act", lambda e: e.activation(out=self.junk[:], in_=xt[:], func=AF.Square, accum_out=st[:, 0:1]),
                 reads=[xt], writes=[self.junk, st])
            S.op("act", lambda e: e.activation(out=st[:, 1:2], in_=st[:, 0:1], func=AF.Sqrt, scale=1.0 / D, bias=self.epsb[:, 0:1]),
                 reads=[st, self.epsb], writes=[st])
            S.op("dve", lambda e: e.reciprocal(out=st[:, 2:3], in_=st[:, 1:2]), reads=[st], writes=[st])
            hb = self.hb[i % 2]
            S.op("dve", lambda e: e.scalar_tensor_tensor(out=hb[:], in0=xt[:], scalar=st[:, 2:3], in1=self.normw[:],
                                                         op0=ALU.mult, op1=ALU.mult),
                 reads=[xt, st, self.normw], writes=[hb])
            for half in range(2):
                p = self.psum()
                pv = p.t[:].bitcast(BF16)
                for j in range(8):
                    ft = half * 8 + j
                    S.op("pe", lambda e: e.transpose(pv[:, j * 128:(j + 1) * 128], hb[:, ft * 128:(ft + 1) * 128],
                                                     self.ident[:]),
                         reads=[hb, self.ident], writes=[p], signal=(j == 7))
                S.op("act", lambda e: e.copy(out=hT[:, half * 8:half * 8 + 8, i * 128:(i + 1) * 128],
                                             in_=pv.rearrange("p (a c) -> p a c", a=8)),
                     reads=[p], writes=[hT])

    def layer_pool(self, src, dst, last):
        S = self.S
        w_in = self.dram_in("c_w_in", [D, 2 * D]).ap()
        w_grp = self.dram_in("c_w_grp", [4 * 512, 512]).ap()
        w_out = self.dram_in("c_w_out", [D, D]).ap()
        scale_d = self.dram_in("c_scale_pm", [128, NFT]).ap()
        rc_d = self.dram_in("pool_rc", [128, 4 * 16]).ap()
        self.alloc_norm(2, 4)
        self.alloc_wsl()
        hT = self.sb("hT", [128, NFT, TP], BF16)
        scale = self.sb("c_scale", [128, NFT], F32)
        S.dma("sp", scale[:], scale_d[:, :], writes=[scale])
        rc = self.sb("pool_rc", [128, 4, 16], F32)
        S.dma("sp", rc[:].rearrange("p a b -> p (a b)"), rc_d[:, :], writes=[rc])
        wg = self.sb("wgrp", [128, 16, 512], BF16)
        S.dma("pool", wg[:], w_grp.rearrange("(a p) c -> p a c", p=128), writes=[wg])
        hist = self.sb("hist", [128, NFT, 16], F32)
        ub = [self.sb("ub%d" % i, [128, 16 + TP], F32) for i in range(2)]
        tmp = [self.sb("ptmp%d" % i, [128, 16 + TP], F32) for i in range(2)]
        dT = self.sb("dT", [128, NFT, TP], BF16)
        sg = self.sb("sg", [128, NFT, TP], BF16)
        zT = self.sb("zT", [128, NFT, TP], BF16)
        for panel in range(self.n_panels):
            first = (panel % NPB == 0)
            self.norm_tiles(src, panel * 4, 4, hT)
            if first:
                S.op("dve", lambda e: e.memset(hist[:], 0.0), writes=[hist])
            for s in range(8):
                w = self.load_wslab(w_in, s * 512)
                for j in range(4):
                    fo = s * 4 + j
                    p = self.psum()
                    for fi in range(NFT):
                        S.op("pe", lambda e: e.matmul(p[:], w[:, fi, j * 128:(j + 1) * 128], hT[:, fi, :],
                                                      start=(fi == 0), stop=(fi == NFT - 1)),
                             reads=[w, hT], writes=[p], signal=(fi == NFT - 1))
                    if fo < NFT:
                        g = fo // 4
                        win = 2 << g
                        u = ub[fo % 2]
                        S.op("act", lambda e: e.copy(out=u[:, 16:16 + TP], in_=p[:]), reads=[p], writes=[u])
                        S.op("pool", lambda e: e.tensor_copy(out=u[:, 0:16], in_=hist[:, fo, :]), reads=[hist], writes=[u])
                        S.op("pool", lambda e: e.tensor_copy(out=hist[:, fo, :], in_=u[:, TP:TP + 16]), reads=[u], writes=[hist])
                        cur = u
                        sh = 1
                        k = 0
                        while sh < win:
                            nxt = tmp[k % 2]
                            lo = 2 * sh - 1
                            S.op("dve", lambda e: e.tensor_tensor(out=nxt[:, lo:16 + TP], in0=cur[:, lo:16 + TP],
                                                                  in1=cur[:, lo - sh:16 + TP - sh], op=ALU.add),
                                 reads=[cur], writes=[nxt])
                            cur = nxt
                            sh *= 2
                            k += 1
                        S.op("dve", lambda e: e.scalar_tensor_tensor(out=dT[:, fo, :], in0=cur[:, 16:16 + TP], scalar=1.0 / win,
                                                                     in1=u[:, 16:16 + TP], op0=ALU.mult, op1=ALU.subtract),
                             reads=[cur, u], writes=[dT])
                        if first:
                            t2 = tmp[k % 2]
                            S.op("dve", lambda e: e.tensor_tensor(out=t2[:, 0:16], in0=cur[:, 16:32], in1=rc[:, g, :], op=ALU.mult),
                                 reads=[cur, rc], writes=[t2])
                            S.op("dve", lambda e: e.tensor_tensor(out=dT[:, fo, 0:16], in0=t2[:, 0:16], in1=u[:, 16:32], op=ALU.subtract),
                                 reads=[t2, u], writes=[dT])
                    else:
                        S.op("act", lambda e: e.activation(out=sg[:, fo - NFT, :], in_=p[:], func=AF.Silu), reads=[p], writes=[sg])
            for fo in range(NFT):
                g = fo // 4
                jo = fo % 4
                p = self.psum()
                for fi in range(4):
                    S.op("pe", lambda e: e.matmul(p[:], wg[:, g * 4 + fi, jo * 128:(jo + 1) * 128], dT[:, g * 4 + fi, :],
                                                  start=(fi == 0), stop=(fi == 3)),
                         reads=[wg, dT], writes=[p], signal=(fi == 3))
                S.op("dve", lambda e: e.scalar_tensor_tensor(out=zT[:, fo, :], in0=p[:], scalar=scale[:, fo:fo + 1],
                                                             in1=sg[:, fo, :], op0=ALU.mult, op1=ALU.mult),
                     reads=[p, scale, sg], writes=[zT])
            self.out_proj(w_out, zT, NFT, src, dst, panel * 4, last, reload_x=False)

    def out_proj(self, w_out, zT, nfi, src, dst, tile0, last, reload_x):
        S = self.S
        if reload_x:
            for i in range(4):
                tile = tile0 + i
                S.dma("sp", self.xt[i][:], src.ap()[tile * 128:(tile + 1) * 128, :], reads=[self.xbuf(src, tile)],
                      writes=[self.xt[i]])
        for c in range(4):
            w = self.load_wslab(w_out, c * 512, nrows_t=nfi)
            for i in range(4):
                p = self.psum()
                for fi in range(nfi):
                    S.op("pe", lambda e: e.matmul(p[:], zT[:, fi, i * 128:(i + 1) * 128], w[:, fi, :],
                                                  start=(fi == 0), stop=(fi == nfi - 1)),
                         reads=[zT, w], writes=[p], signal=(fi == nfi - 1))
                xt = self.xt[i]
                S.op("dve", lambda e: e.tensor_tensor(out=xt[:, c * 512:(c + 1) * 512], in0=xt[:, c * 512:(c + 1) * 512],
                                                      in1=p[:], op=ALU.add), reads=[xt, p], writes=[xt])
        for i in range(4):
            tile = tile0 + i
            S.dma("sp", dst.ap()[tile * 128:(tile + 1) * 128, :], self.xt[i][:], reads=[self.xt[i]],
                  writes=[self.xbuf(dst, tile)], is_output=last)

    def layer_attn(self, src, dst, last):
        S = self.S
        nc = self.nc
        w_in = self.dram_in("b_w_in_r", [D, 8 * 1280]).ap()
        w_out = self.dram_in("b_w_out", [1024, D]).ap()
        qkn_d = self.dram_in("b_qkn_pm", [128, 6]).ap()
        pos_d = self.dram_in("positions", [NB_CORE, SEQ], I32).ap()
        invf_d = self.dram_in("rope_invf", [32, 1]).ap()
        rot_d = self.dram_in("rope_rot", [32, 32]).ap()
        mask_d = self.dram_in("attn_mask", [128, 256]).ap()
        zscr = [nc.dram_tensor("zscr%d" % b, [1024, SEQ], BF16, kind="Internal") for b in range(NB_CORE)]
        zbuf = [[TT(None) for hd in range(8)] for b in range(NB_CORE)]
        hTf = self.sb("hTf", [128, NFT, SEQ], BF16)
        mask = self.sb("amask", [128, 256], BF16)
        S.dma("pool", mask[:], mask_d[:, :], writes=[mask])
        rot = self.sb("rot", [32, 32], BF16)
        S.dma("pool", rot[:], rot_d[:, :], writes=[rot])
        qkn = self.sb("qkn", [128, 6], F32)
        S.dma("sp", qkn[:], qkn_d[:, :], writes=[qkn])
        invf = self.sb("invf", [32, 1], F32)
        S.dma("sp", invf[:], invf_d[:, :], writes=[invf])
        ones = self.sb("ones", [128, 128], BF16)
        S.op("dve", lambda e: e.memset(ones[:], 1.0), writes=[ones])
        DILS = (1, 4, 16)
        SCALE = 128.0 ** -0.5
        PI = float(np.pi)
        C1 = 6.28125
        C2 = float(2 * np.pi - 6.28125)
        nb = self.n_panels // NPB
        for b in range(nb):
            with self.scope():
                self.alloc_norm(1, 2)
                self.norm_tiles(src, b * 16, 16, hTf)
            with self.scope():
                self.alloc_wsl()
                epsb = self.sb("epsb2", [128, 1], F32)
                S.op("dve", lambda e: e.memset(epsb[:], RMS_EPS), writes=[epsb])
                Ctab = self.sb("Ctab", [32, SEQ], F32)
                Stab = self.sb("Stab", [32, SEQ], F32)
                posi = self.sb("posi", [32, 512], I32)
                ra = self.sb("ra", [32, 512], F32)
                rb = self.sb("rb", [32, 512], F32)
                rc_ = self.sb("rc", [32, 512], F32)
                rqi = self.sb("rqi", [32, 512], I32)
                for c in range(4):
                    cs = slice(c * 512, (c + 1) * 512)
                    S.dma("sp", posi[:], bcast_rows(pos_d[b:b + 1, cs], 32), writes=[posi])
                    S.op("dve", lambda e: e.tensor_copy(out=ra[:], in_=posi[:]), reads=[posi], writes=[ra])
                    S.op("dve", lambda e: e.tensor_scalar(out=ra[:], in0=ra[:], scalar1=invf[:, 0:1], scalar2=None, op0=ALU.mult),
                         reads=[ra, invf], writes=[ra])
                    S.op("dve", lambda e: e.tensor_scalar(out=rb[:], in0=ra[:], scalar1=1.0 / (2 * PI), scalar2=None, op0=ALU.mult),
                         reads=[ra], writes=[rb])
                    S.op("dve", lambda e: e.tensor_copy(out=rqi[:], in_=rb[:]), reads=[rb], writes=[rqi])
                    S.op("dve", lambda e: e.tensor_copy(out=rb[:], in_=rqi[:]), reads=[rqi], writes=[rb])
                    S.op("dve", lambda e: e.scalar_tensor_tensor(out=rc_[:], in0=rb[:], scalar=-C1, in1=ra[:], op0=ALU.mult, op1=ALU.add),
                         reads=[rb, ra], writes=[rc_])
                    S.op("dve", lambda e: e.scalar_tensor_tensor(out=ra[:], in0=rb[:], scalar=-C2, in1=rc_[:], op0=ALU.mult, op1=ALU.add),
                         reads=[rb, rc_], writes=[ra])
                    S.op("dve", lambda e: e.tensor_scalar(out=rb[:], in0=ra[:], scalar1=PI, scalar2=-PI, op0=ALU.min, op1=ALU.max),
                         reads=[ra], writes=[rb])
                    S.op("act", lambda e: e.activation(out=Stab[:, cs], in_=rb[:], func=AF.Sin), reads=[rb], writes=[Stab])
                    S.op("dve", lambda e: e.tensor_scalar(out=rc_[:], in0=ra[:], scalar1=PI / 2, scalar2=None, op0=ALU.add),
                         reads=[ra], writes=[rc_])
                    S.op("dve", lambda e: e.tensor_scalar(out=rb[:], in0=rc_[:], scalar1=PI, scalar2=-2 * PI, op0=ALU.is_gt, op1=ALU.mult),
                         reads=[rc_], writes=[rb])
                    S.op("dve", lambda e: e.tensor_tensor(out=rc_[:], in0=rc_[:], in1=rb[:], op=ALU.add), reads=[rc_, rb], writes=[rc_])
                    S.op("dve", lambda e: e.tensor_scalar(out=rc_[:], in0=rc_[:], scalar1=PI, scalar2=-PI, op0=ALU.min, op1=ALU.max),
                         reads=[rc_], writes=[rc_])
                    S.op("act", lambda e: e.activation(out=Ctab[:, cs], in_=rc_[:], func=AF.Sin), reads=[rc_], writes=[Ctab])
                qT = self.sb("qT", [128, SEQ], BF16)
                kT = self.sb("kT", [128, SEQ], BF16)
                V = self.sb("V", [128, 16, 128], BF16)
                qf = [self.sb("qf%d" % i, [128, 512], F32) for i in range(2)]
                sq = [self.sb("sq%d" % i, [128, 512], BF16) for i in range(2)]
                rs = [self.sb("rs%d" % i, [128, 512], F32) for i in range(2)]
                rt1 = self.sb("rt1", [32, 512], F32)
                rt2 = self.sb("rt2", [32, 512], F32)
                PT = [self.sb("PT%d" % i, [128, 1024], BF16) for i in range(2)]
                acc = self.sb("acc", [128, SEQ], F32)
                lacc = self.sb("lacc", [128, SEQ], F32)
                sgate = self.sb("sgate", [128, SEQ], BF16)
                zst = self.sb("zst", [128, SEQ], BF16)
                cnt = [0]

                def proj_qk(w, coff, dstT, ncol):
                    for c in range(4):
                        cs = slice(c * 512, (c + 1) * 512)
                        p = self.psum()
                        for fi in range(NFT):
                            S.op("pe", lambda e: e.matmul(p[:], w[:, fi, coff:coff + 128], hTf[:, fi, cs],
                                                          start=(fi == 0), stop=(fi == NFT - 1)),
                                 reads=[w, hTf], writes=[p], signal=(fi == NFT - 1))
                        k = cnt[0] % 2
                        cnt[0] += 1
                        S.op("act", lambda e: e.copy(out=qf[k][:], in_=p[:]), reads=[p], writes=[qf[k]])
                        S.op("act", lambda e: e.activation(out=sq[k][:], in_=p[:], func=AF.Square), reads=[p], writes=[sq[k]])
                        p2 = self.psum()
                        S.op("pe", lambda e: e.matmul(p2[:], ones[:], sq[k][:], start=True, stop=True),
                             reads=[ones, sq[k]], writes=[p2])
                        S.op("act", lambda e: e.activation(out=rs[k][:], in_=p2[:], func=AF.Sqrt, scale=1.0 / 128, bias=epsb[:, 0:1]),
                             reads=[p2, epsb], writes=[rs[k]])
                        S.op("dve", lambda e: e.reciprocal(out=rs[k][:], in_=rs[k][:]), reads=[rs[k]], writes=[rs[k]])
                        S.op("dve", lambda e: e.scalar_tensor_tensor(out=dstT[:, cs], in0=qf[k][:], scalar=qkn[:, ncol:ncol + 1],
                                                                     in1=rs[k][:], op0=ALU.mult, op1=ALU.mult),
                             reads=[qf[k], qkn, rs[k]], writes=[dstT])
                        p3 = self.psum()
                        S.op("pe", lambda e: e.matmul(p3[0:32, :], rot[:], dstT[0:32, cs], start=True, stop=True),
                             reads=[rot, dstT], writes=[p3])
                        S.op("dve", lambda e: e.tensor_tensor(out=rt1[:], in0=p3[0:32, :], in1=Stab[:, cs], op=ALU.mult),
                             reads=[p3, Stab], writes=[rt1])
                        S.op("pool", lambda e: e.tensor_tensor(out=rt2[:], in0=dstT[0:32, cs], in1=Ctab[:, cs], op=ALU.mult),
                             reads=[dstT, Ctab], writes=[rt2])
                        S.op("dve", lambda e: e.tensor_tensor(out=dstT[0:32, cs], in0=rt1[:], in1=rt2[:], op=ALU.add),
                             reads=[rt1, rt2], writes=[dstT])

                def view(ap2d, g):
                    dil = DILS[g]
                    nblk = SEQ // dil // 128
                    return ap2d.rearrange("p (n i r) -> p r n i", n=nblk, i=128, r=dil)

                def blocks(g):
                    dil = DILS[g]
                    nblk = SEQ // dil // 128
                    return [(r, n) for r in range(dil) for n in range(nblk)]

                for hd in range(8):
                    for g in range(3):
                        if g == 0:
                            w = self.load_wslab(w_in, hd * 1280, ncols=512)
                        else:
                            w = self.load_wslab(w_in, hd * 1280 + 512 + (g - 1) * 384, ncols=384)
                        voff = 256
                        proj_qk(w, 0, qT, g)
                        proj_qk(w, 128, kT, 3 + g)
                        bl = blocks(g)
                        for tg in range(4):
                            p = self.psum()
                            for tl in range(4):
                                r, n = bl[tg * 4 + tl]
                                for fi in range(NFT):
                                    S.op("pe", lambda e: e.matmul(p[:, tl * 128:(tl + 1) * 128], view(hTf[:, fi, :], g)[:, r, n, :],
                                                                  w[:, fi, voff:voff + 128], start=(fi == 0), stop=(fi == NFT - 1)),
                                         reads=[w, hTf], writes=[p], signal=(fi == NFT - 1 and tl == 3))
                            S.op("act", lambda e: e.copy(out=V[:, tg * 4:tg * 4 + 4, :], in_=p[:].rearrange("p (a c) -> p a c", a=4)),
                                 reads=[p], writes=[V])
                        if g == 0:
                            for c in range(4):
                                cs = slice(c * 512, (c + 1) * 512)
                                p = self.psum()
                                for fi in range(NFT):
                                    S.op("pe", lambda e: e.matmul(p[:], w[:, fi, 384:512], hTf[:, fi, cs],
                                                                  start=(fi == 0), stop=(fi == NFT - 1)),
                                         reads=[w, hTf], writes=[p], signal=(fi == NFT - 1))
                                S.op("act", lambda e: e.activation(out=sgate[:, cs], in_=p[:], func=AF.Silu), reads=[p], writes=[sgate])
                        qv = view(qT[:, :], g)
                        kv = view(kT[:, :], g)
                        av = view(acc[:, :], g)
                        lv = view(lacc[:, :], g)
                        for j in range(4):
                            pt = PT[j % 2]
                            pss = [self.psum(), self.psum()]
                            for bi in range(4):
                                r, n = bl[j * 4 + bi]
                                bank = pss[bi // 2]
                                off = (bi % 2) * 256
                                if n > 0:
                                    S.op("pe", lambda e: e.matmul(bank[:, off:off + 128], kv[:, r, n - 1, :], qv[:, r, n, :],
                                                                  start=True, stop=True),
                                         reads=[kT, qT], writes=[bank], signal=False)
                                S.op("pe", lambda e: e.matmul(bank[:, off + 128:off + 256], kv[:, r, n, :], qv[:, r, n, :],
                                                              start=True, stop=True),
                                     reads=[kT, qT], writes=[bank], signal=(bi % 2 == 1))
                            for h2 in range(2):
                                S.op("act", lambda e: e.activation(out=pt[:, h2 * 512:(h2 + 1) * 512], in_=pss[h2][:], func=AF.Exp, scale=SCALE),
                                     reads=[pss[h2]], writes=[pt])
                            mb = bass.AP(mask.t, 0, [[256, 128], [0, 4], [1, 256]])
                            S.op("pool", lambda e: e.tensor_tensor(out=pt[:].rearrange("p (a c) -> p a c", a=4),
                                                                   in0=pt[:].rearrange("p (a c) -> p a c", a=4), in1=mb, op=ALU.mult),
                                 reads=[pt, mask], writes=[pt])
                            po = self.psum()
                            pl = self.psum()
                            for (bank, isl) in ((po, False), (pl, True)):
                                for bi in range(4):
                                    r, n = bl[j * 4 + bi]
                                    bidx = j * 4 + bi
                                    osl = slice(bi * 128, (bi + 1) * 128)
                                    lh_cur = ones[:] if isl else V[:, bidx, :]
                                    S.op("pe", lambda e: e.matmul(bank[:, osl], lh_cur, pt[:, bi * 256 + 128:bi * 256 + 256],
                                                                  start=True, stop=(n == 0)),
                                         reads=[V, ones, pt], writes=[bank], signal=(n == 0 and bi == 3))
                                    if n > 0:
                                        lh_prev = ones[:] if isl else V[:, bidx - 1, :]
                                        S.op("pe", lambda e: e.matmul(bank[:, osl], lh_prev, pt[:, bi * 256:bi * 256 + 128],
                                                                      start=False, stop=True),
                                             reads=[V, ones, pt], writes=[bank], signal=(bi == 3))
                            if g == 0:
                                osel = av[:, 0, 4 * j:4 * j + 4, :]
                                lsel = lv[:, 0, 4 * j:4 * j + 4, :]
                            elif g == 1:
                                osel = av[:, j, :, :]
                                lsel = lv[:, j, :, :]
                            else:
                                osel = av[:, 4 * j:4 * j + 4, 0, :]
                                lsel = lv[:, 4 * j:4 * j + 4, 0, :]
                            pov = po[:].rearrange("p (a c) -> p a c", a=4)
                            plv = pl[:].rearrange("p (a c) -> p a c", a=4)
                            if g == 0:
                                S.op("act", lambda e: e.copy(out=osel, in_=pov), reads=[po], writes=[acc])
                                S.op("dve", lambda e: e.tensor_copy(out=lsel, in_=plv), reads=[pl], writes=[lacc])
                            else:
                                S.op("dve", lambda e: e.tensor_tensor(out=osel, in0=osel, in1=pov, op=ALU.add), reads=[po, acc], writes=[acc])
                                S.op("dve", lambda e: e.tensor_tensor(out=lsel, in0=lsel, in1=plv, op=ALU.add), reads=[pl, lacc], writes=[lacc])
                    S.op("dve", lambda e: e.reciprocal(out=lacc[:], in_=lacc[:]), reads=[lacc], writes=[lacc])
                    S.op("pool", lambda e: e.tensor_tensor(out=acc[:], in0=acc[:], in1=lacc[:], op=ALU.mult), reads=[acc, lacc], writes=[acc])
                    S.op("pool", lambda e: e.tensor_tensor(out=zst[:], in0=acc[:], in1=sgate[:], op=ALU.mult), reads=[acc, sgate], writes=[zst])
                    S.dma("sp", zscr[b].ap()[hd * 128:(hd + 1) * 128, :], zst[:], reads=[zst], writes=[zbuf[b][hd]])
            with self.scope():
                self.alloc_wsl()
                self.xt = [self.sb("xt%d" % i, [128, D], F32) for i in range(4)]
                zp = [self.sb("zp%d" % i, [128, 8, 512], BF16) for i in range(2)]
                for pn in range(4):
                    z = zp[pn % 2]
                    S.dma("sp", z[:], zscr[b].ap()[:, pn * 512:(pn + 1) * 512].rearrange("(a p) c -> p a c", p=128),
                          reads=zbuf[b], writes=[z])
                    self.out_proj(w_out, z, 8, src, dst, b * 16 + pn * 4, last, reload_x=True)

    def layer_rwkv(self, L, src, dst, last):
        S = self.S
        nc = self.nc
        ia = 0 if L == 0 else 1
        has_v = (L != 0)
        TR = 256
        NCH = TR // 64
        npan = self.n_panels * TP // TR
        ppb = SEQ // TR
        pre = "a%d_" % ia
        w_in = self.dram_in(pre + "w_in", [4 * D, D]).ap()
        w_out = self.dram_in(pre + "w_out", [D, D]).ap()
        mu_d = self.dram_in(pre + "mu_pm", [128, 6 * NFT]).ap()
        par_d = self.dram_in(pre + "par_pm", [128, 8 * NFT]).ap()
        w1_d = self.dram_in(pre + "w1", [D, 96]).ap()
        a1_d = self.dram_in(pre + "a1", [D, 96]).ap()
        w2_d = self.dram_in(pre + "w2", [96, D]).ap()
        a2_d = self.dram_in(pre + "a2", [96, D]).ap()
        if has_v:
            v1_d = self.dram_in(pre + "v1", [D, 64]).ap()
            v2_d = self.dram_in(pre + "v2", [64, D]).ap()
        cm_d = self.dram_in("rwkv_masks", [128, 512]).ap()
        if not hasattr(self, "vfs"):
            self.vfs_out = False
            if 0 in self.layers:
                self.vfs_out = 3 not in self.layers
                self.vfs = nc.dram_tensor("vfs", [D, NB_CORE * SEQ], F32, kind="ExternalOutput" if self.vfs_out else "Internal")
            else:
                self.vfs = self.dram_in("vfs", [D, NB_CORE * SEQ])
            self.vfs_b = {}
        wsc = nc.dram_tensor(pre + "wsc", [20, 128, NFT * 512], BF16, kind="Internal")
        wsc_b = [TT(None) for _ in range(20)]
        for j in range(5):
            wsrc = w_in[j * D:(j + 1) * D, :] if j < 4 else w_out
            for sl in range(4):
                idx = j * 4 + sl
                S.dma("pool", wsc.ap()[idx].rearrange("p (a c) -> p a c", a=NFT),
                      wsrc[:, sl * 512:(sl + 1) * 512].rearrange("(a p) c -> p a c", p=128), writes=[wsc_b[idx]])
        mu = self.sb("mu", [128, 6 * NFT], F32)
        S.dma("sp", mu[:], mu_d[:, :], writes=[mu])
        par = self.sb("par", [128, 8 * NFT], F32)
        S.dma("sp", par[:], par_d[:, :], writes=[par])
        omka = self.sb("omka", [128, NFT], F32)
        S.op("dve", lambda e: e.tensor_scalar(out=omka[:], in0=par[:, 3 * NFT:4 * NFT], scalar1=-1.0, scalar2=1.0, op0=ALU.mult, op1=ALU.add),
             reads=[par], writes=[omka])
        P_W0, P_A0, P_KK, P_KA, P_RK, P_LW, P_LB, P_V0 = range(8)

        def pcol(which, ft):
            return par[:, which * NFT + ft:which * NFT + ft + 1]
        w1 = self.sb("w1", [128, NFT, 96], BF16)
        S.dma("pool", w1[:], w1_d.rearrange("(a p) c -> p a c", p=128), writes=[w1])
        a1 = self.sb("a1", [128, NFT, 96], BF16)
        S.dma("pool", a1[:], a1_d.rearrange("(a p) c -> p a c", p=128), writes=[a1])
        w2 = self.sb("w2", [96, D], BF16)
        S.dma("pool", w2[:], w2_d[:, :], writes=[w2])
        a2 = self.sb("a2", [96, D], BF16)
        S.dma("pool", a2[:], a2_d[:, :], writes=[a2])
        if has_v:
            v1 = self.sb("v1", [128, NFT, 64], BF16)
            S.dma("pool", v1[:], v1_d.rearrange("(a p) c -> p a c", p=128), writes=[v1])
            v2 = self.sb("v2", [64, D], BF16)
            S.dma("pool", v2[:], v2_d[:, :], writes=[v2])
        cmf = self.sb("cmf", [128, 512], F32)
        S.dma("sp", cmf[:], cm_d[:, :], writes=[cmf])
        cmb = self.sb("cmb", [128, 4 * 64], BF16)
        S.op("dve", lambda e: e.tensor_copy(out=cmb[:], in_=cmf[:, 0:256]), reads=[cmf], writes=[cmb])
        m_lt = cmf[:, 0:64]
        m_le = cmf[:, 64:128]
        m_gt = cmf[:, 128:192]
        id2b = cmb[:, 192:256]
        id2f = cmf[:, 192:256]
        resetm = cmf[:, 256:256 + TR]
        bo = self.sb("bo", [128, 128], BF16)
        S.op("dve", lambda e: e.memset(bo[:], 0.0), writes=[bo])
        S.op("dve", lambda e: e.memset(bo[0:64, 0:64], 1.0), writes=[bo])
        S.op("dve", lambda e: e.memset(bo[64:128, 64:128], 1.0), writes=[bo])
        epsl = self.sb("epsl", [128, 1], F32)
        S.op("dve", lambda e: e.memset(epsl[:], 64e-5), writes=[epsl])
        self.alloc_norm(L, 2)
        hT = self.sb("hT", [128, NFT, TR], BF16)
        xx = self.sb("xx", [128, NFT, TR], BF16)
        zT = self.sb("zT", [128, NFT, TR], BF16)
        carry = self.sb("carry", [128, NFT], BF16)
        tw = self.sb("tw", [96, TR], BF16)
        la = self.sb("la", [96, TR], BF16)
        lv = self.sb("lv", [64, TR], BF16)
        wsl = [self.sb("rwsl%d" % i, [128, NFT, 512], BF16) for i in range(2)]
        mix = [self.sb("mix%d" % i, [128, NFT, TR], BF16) for i in range(2)]
        mixc = [0]
        wslc = [0]
        Rs = self.sb("Rs", [128, 4, TR], F32)
        Ks = self.sb("Ks", [128, 4, TR], F32)
        Vs = self.sb("Vs", [128, 4, TR], F32)
        Gs = self.sb("Gs", [128, 4, TR], BF16)
        Sf = [self.sb("Sf%d" % i, [128, 64], F32) for i in range(NFT)]
        Sb = [self.sb("Sb%d" % i, [128, 64], BF16) for i in range(NFT)]

        def f32t(name):
            return self.sb(name, [128, TR], F32)

        def bft(name):
            return self.sb(name, [128, TR], BF16)
        sw, aa, sv, logw, cum, d1, d2 = [f32t(n) for n in ("sw", "aa", "sv", "logw", "cum", "d1", "d2")]
        Wt, Winv, Wprev, Wend = [f32t(n) for n in ("Wt", "Winv", "Wprev", "Wend")]
        kk, nrm, kkn, tt, kp, bs = [f32t(n) for n in ("kk", "nrm", "kkn", "tt", "kp", "bs")]
        vf_t = f32t("vf_t")
        ksq = bft("ksq")
        AR = self.sb("AR", [128, 2, TR], BF16)
        bT, kTl, bh, kh, vb = [bft(n) for n in ("bT", "kTl", "bh", "kh", "vb")]
        TM = self.sb("TM", [128, 3, NCH, 64], BF16)
        MBs = self.sb("MBs", [128, NCH, 2, 64], BF16)
        MKs = self.sb("MKs", [128, NCH, 2, 64], BF16)
        Bp = [self.sb("Bp%d" % i, [128, NCH, 64], BF16) for i in range(2)]
        Ap = [self.sb("Ap%d" % i, [128, NCH, 64], BF16) for i in range(2)]
        TTt = self.sb("TTt", [128, NCH, 64], BF16)
        TTf = self.sb("TTf", [128, NCH, 64], F32)
        Xs = self.sb("Xs", [128, 64], BF16)
        Us = self.sb("Us", [128, 64], BF16)
        WC = self.sb("WC", [128, NCH], F32)
        ysb, yc, sd, yn, bon = [f32t(n) for n in ("ysb", "yc", "sd", "yn", "bon")]
        ybf, sqb, rkb = [bft(n) for n in ("ybf", "sqb", "rkb")]
        EXPM05 = float(np.exp(-0.5))
        PRS = (slice(0, 64), slice(64, 128))

        def gen_mix(j):
            m = mix[mixc[0] % 2]
            mixc[0] += 1
            for fi in range(NFT):
                eng = "dve"
                S.op(eng, lambda e: e.scalar_tensor_tensor(out=m[:, fi, :], in0=xx[:, fi, :], scalar=mu[:, j * NFT + fi:j * NFT + fi + 1],
                                                           in1=hT[:, fi, :], op0=ALU.mult, op1=ALU.add),
                     reads=[xx, mu, hT], writes=[m])
            return m

        def lora_down(m, wl, dst_t, nlr, func):
            p = self.psum()
            for fi in range(NFT):
                S.op("pe", lambda e: e.matmul(p[0:nlr, 0:TR], wl[:, fi, :], m[:, fi, :], start=(fi == 0), stop=(fi == NFT - 1)),
                     reads=[wl, m], writes=[p], signal=(fi == NFT - 1))
            S.op("act", lambda e: e.activation(out=dst_t[:], in_=p[0:nlr, 0:TR], func=func), reads=[p], writes=[dst_t])

        for pn in range(npan):
            first = (pn % ppb == 0)
            tok0 = pn * TR
            self.norm_tiles(src, pn * 2, 2, hT)
            if first:
                S.op("dve", lambda e: e.memset(carry[:], 0.0), writes=[carry])
                for ft in range(NFT):
                    S.op("pool", lambda e: e.memset(Sf[ft][:], 0.0), writes=[Sf[ft]])
                    S.op("pool", lambda e: e.memset(Sb[ft][:], 0.0), writes=[Sb[ft]])
            S.op("dve", lambda e: e.tensor_tensor(out=xx[:, :, 1:TR], in0=hT[:, :, 0:TR - 1], in1=hT[:, :, 1:TR], op=ALU.subtract),
                 reads=[hT], writes=[xx])
            S.op("dve", lambda e: e.tensor_tensor(out=xx[:, :, 0], in0=carry[:], in1=hT[:, :, 0], op=ALU.subtract),
                 reads=[hT, carry], writes=[xx])
            S.op("dve", lambda e: e.tensor_copy(out=carry[:], in_=hT[:, :, TR - 1]), reads=[hT], writes=[carry])
            m = gen_mix(4)
            lora_down(m, w1, tw, 96, AF.Tanh)
            m = gen_mix(5)
            lora_down(m, a1, la, 96, AF.Copy)
            if has_v:
                m = gen_mix(2)
                lora_down(m, v1, lv, 64, AF.Copy)
            for sl in range(4):
                for j in range(4):
                    m = gen_mix(j)
                    w = wsl[wslc[0] % 2]
                    wslc[0] += 1
                    idx = j * 4 + sl
                    S.dma("sp", w[:].rearrange("p a c -> p (a c)"), wsc.ap()[idx], reads=[wsc_b[idx]], writes=[w])
                    for ftl in range(4):
                        p = self.psum()
                        for fi in range(NFT):
                            S.op("pe", lambda e: e.matmul(p[:, 0:TR], w[:, fi, ftl * 128:(ftl + 1) * 128], m[:, fi, :],
                                                          start=(fi == 0), stop=(fi == NFT - 1)),
                                 reads=[w, m], writes=[p], signal=(fi == NFT - 1))
                        if j == 3:
                            S.op("act", lambda e: e.activation(out=Gs[:, ftl, :], in_=p[:, 0:TR], func=AF.Silu), reads=[p], writes=[Gs])
                        else:
                            dstt = (Rs, Ks, Vs)[j]
                            S.op("act", lambda e: e.copy(out=dstt[:, ftl, :], in_=p[:, 0:TR]), reads=[p], writes=[dstt])
                for ftl in range(4):
                    ft = sl * 4 + ftl
                    r_ = Rs[:, ftl, :]
                    k_ = Ks[:, ftl, :]
                    v_ = Vs[:, ftl, :]
                    g_ = Gs[:, ftl, :]
                    fs = slice(ft * 128, (ft + 1) * 128)
                    p = self.psum()
                    S.op("pe", lambda e: e.matmul(p[:, 0:TR], w2[:, fs], tw[:], start=True, stop=True), reads=[w2, tw], writes=[p])
                    S.op("act", lambda e: e.activation(out=sw[:], in_=p[:, 0:TR], func=AF.Sigmoid, bias=pcol(P_W0, ft)),
                         reads=[p, par], writes=[sw])
                    p = self.psum()
                    S.op("pe", lambda e: e.matmul(p[:, 0:TR], a2[:, fs], la[:], start=True, stop=True), reads=[a2, la], writes=[p])
                    S.op("act", lambda e: e.activation(out=aa[:], in_=p[:, 0:TR], func=AF.Sigmoid, bias=pcol(P_A0, ft)),
                         reads=[p, par], writes=[aa])
                    vkey = (ft, pn)
                    if has_v:
                        p = self.psum()
                        S.op("pe", lambda e: e.matmul(p[:, 0:TR], v2[:, fs], lv[:], start=True, stop=True), reads=[v2, lv], writes=[p])
                        S.op("act", lambda e: e.activation(out=sv[:], in_=p[:, 0:TR], func=AF.Sigmoid, bias=pcol(P_V0, ft)),
                             reads=[p, par], writes=[sv])
                        rd = [self.vfs_b[vkey]] if vkey in self.vfs_b else []
                        S.dma("sp", vf_t[:], self.vfs.ap()[fs, tok0:tok0 + TR], reads=rd, writes=[vf_t])
                        S.op("dve", lambda e: e.tensor_tensor(out=vf_t[:], in0=vf_t[:], in1=v_, op=ALU.subtract), reads=[vf_t, Vs], writes=[vf_t])
                        S.op("dve", lambda e: e.tensor_tensor(out=vf_t[:], in0=vf_t[:], in1=sv[:], op=ALU.mult), reads=[vf_t, sv], writes=[vf_t])
                        S.op("dve", lambda e: e.tensor_tensor(out=v_, in0=v_, in1=vf_t[:], op=ALU.add), reads=[vf_t, Vs], writes=[Vs])
                    else:
                        if vkey not in self.vfs_b:
                            self.vfs_b[vkey] = TT(None)
                        S.dma("sp", self.vfs.ap()[fs, tok0:tok0 + TR], v_, reads=[Vs], writes=[self.vfs_b[vkey]], is_output=self.vfs_out)
                    S.op("dve", lambda e: e.tensor_scalar(out=logw[:], in0=sw[:], scalar1=-EXPM05, scalar2=None, op0=ALU.mult),
                         reads=[sw], writes=[logw])
                    S.op("dve", lambda e: e.tensor_tensor_scan(out=cum[:], data0=resetm, data1=logw[:], initial=0.0, op0=ALU.mult, op1=ALU.add),
                         reads=[logw, cmf], writes=[cum])
                    cumC_bc = bass.AP(cum.t, 63, [[TR, 128], [64, NCH], [0, 64]])
                    cumC = bass.AP(cum.t, 63, [[TR, 128], [64, NCH]])
                    c3 = cum[:].rearrange("p (a c) -> p a c", a=NCH)
                    S.op("pool", lambda e: e.tensor_tensor(out=d1[:].rearrange("p (a c) -> p a c", a=NCH), in0=cumC_bc, in1=c3, op=ALU.subtract),
                         reads=[cum], writes=[d1])
                    S.op("pool", lambda e: e.tensor_tensor(out=d2[:], in0=cum[:], in1=logw[:], op=ALU.subtract), reads=[cum, logw], writes=[d2])
                    S.op("act", lambda e: e.activation(out=Wt[:], in_=cum[:], func=AF.Exp), reads=[cum], writes=[Wt])
                    S.op("act", lambda e: e.activation(out=Winv[:], in_=cum[:], func=AF.Exp, scale=-1.0), reads=[cum], writes=[Winv])
                    S.op("act", lambda e: e.activation(out=WC[:], in_=cumC, func=AF.Exp), reads=[cum], writes=[WC])
                    S.op("act", lambda e: e.activation(out=Wprev[:], in_=d2[:], func=AF.Exp), reads=[d2], writes=[Wprev])
                    S.op("act", lambda e: e.activation(out=Wend[:], in_=d1[:], func=AF.Exp), reads=[d1], writes=[Wend])
                    S.op("dve", lambda e: e.tensor_scalar(out=kk[:], in0=k_, scalar1=pcol(P_KK, ft), scalar2=None, op0=ALU.mult),
                         reads=[Ks, par], writes=[kk])
                    S.op("pool", lambda e: e.tensor_tensor(out=ksq[:], in0=kk[:], in1=kk[:], op=ALU.mult), reads=[kk], writes=[ksq])
                    p = self.psum()
                    S.op("pe", lambda e: e.matmul(p[:, 0:TR], bo[:], ksq[:], start=True, stop=True), reads=[bo, ksq], writes=[p])
                    S.op("act", lambda e: e.activation(out=nrm[:], in_=p[:, 0:TR], func=AF.Sqrt), reads=[p], writes=[nrm])
                    S.op("dve", lambda e: e.tensor_scalar(out=nrm[:], in0=nrm[:], scalar1=1e-12, scalar2=None, op0=ALU.max), reads=[nrm], writes=[nrm])
                    S.op("dve", lambda e: e.reciprocal(out=nrm[:], in_=nrm[:]), reads=[nrm], writes=[nrm])
                    S.op("dve", lambda e: e.tensor_tensor(out=kkn[:], in0=kk[:], in1=nrm[:], op=ALU.mult), reads=[kk, nrm], writes=[kkn])
                    S.op("dve", lambda e: e.tensor_scalar(out=tt[:], in0=aa[:], scalar1=pcol(P_KA, ft), scalar2=omka[:, ft:ft + 1],
                                                          op0=ALU.mult, op1=ALU.add), reads=[aa, par, omka], writes=[tt])
                    S.op("dve", lambda e: e.tensor_tensor(out=kp[:], in0=k_, in1=tt[:], op=ALU.mult), reads=[Ks, tt], writes=[kp])
                    S.op("pool", lambda e: e.tensor_tensor(out=bs[:], in0=kkn[:], in1=aa[:], op=ALU.mult), reads=[kkn, aa], writes=[bs])
                    S.op("dve", lambda e: e.scalar_tensor_tensor(out=AR[:, 0, :], in0=kkn[:], scalar=-1.0, in1=Wprev[:], op0=ALU.mult, op1=ALU.mult),
                         reads=[kkn, Wprev], writes=[AR])
                    S.op("pool", lambda e: e.tensor_tensor(out=AR[:, 1, :], in0=r_, in1=Wt[:], op=ALU.mult), reads=[Rs, Wt], writes=[AR])
                    S.op("dve", lambda e: e.tensor_tensor(out=bT[:], in0=bs[:], in1=Winv[:], op=ALU.mult), reads=[bs, Winv], writes=[bT])
                    S.op("pool", lambda e: e.tensor_tensor(out=kTl[:], in0=kp[:], in1=Winv[:], op=ALU.mult), reads=[kp, Winv], writes=[kTl])
                    S.op("dve", lambda e: e.tensor_tensor(out=bh[:], in0=bs[:], in1=Wend[:], op=ALU.mult), reads=[bs, Wend], writes=[bh])
                    S.op("pool", lambda e: e.tensor_tensor(out=kh[:], in0=kp[:], in1=Wend[:], op=ALU.mult), reads=[kp, Wend], writes=[kh])
                    S.op("act", lambda e: e.copy(out=vb[:], in_=v_), reads=[Vs], writes=[vb])
                    S.op("dve", lambda e: e.scalar_tensor_tensor(out=rkb[:], in0=r_, scalar=pcol(P_RK, ft), in1=kp[:], op0=ALU.mult, op1=ALU.mult),
                         reads=[Rs, par, kp], writes=[rkb])
                    pT = self.psum()
                    pTb = pT.t[:].bitcast(BF16)
                    for wi, srct in enumerate((vb, bh, kh)):
                        for c in range(NCH):
                            for hh in range(2):
                                pr = PRS[hh]
                                o0 = (wi * NCH + c) * 64
                                S.op("pe", lambda e: e.transpose(pTb[pr, o0:o0 + 64], srct[pr, c * 64:(c + 1) * 64], id2b[pr, :]),
                                     reads=[srct, cmb], writes=[pT], signal=(wi == 2 and c == NCH - 1 and hh == 1))
                    S.op("act", lambda e: e.copy(out=TM[:].rearrange("p a b c -> p (a b c)"), in_=pTb[:, 0:3 * NCH * 64]), reads=[pT], writes=[TM])
                    pMB = self.psum()
                    pMK = self.psum()
                    pN = self.psum()
                    for c in range(NCH):
                        cs = slice(c * 64, (c + 1) * 64)
                        for hh in range(2):
                            pr = PRS[hh]
                            lastm = (c == NCH - 1 and hh == 1)
                            S.op("pe", lambda e: e.matmul(pMB[pr, c * 128:(c + 1) * 128].rearrange("p (a b) -> p a b", a=2), bT[pr, cs], AR[pr, :, cs],
                                                          start=True, stop=True), reads=[bT, AR], writes=[pMB], signal=lastm)
                            S.op("pe", lambda e: e.matmul(pMK[pr, c * 128:(c + 1) * 128].rearrange("p (a b) -> p a b", a=2), kTl[pr, cs], AR[pr, :, cs],
                                                          start=True, stop=True), reads=[kTl, AR], writes=[pMK], signal=lastm)
                            S.op("pe", lambda e: e.matmul(pN[pr, c * 64:(c + 1) * 64], AR[pr, 0, cs], bT[pr, cs],
                                                          start=True, stop=True), reads=[bT, AR], writes=[pN], signal=lastm)
                    mk2 = bass.AP(cmf.t, 0, [[512, 128], [0, NCH], [1, 128]])
                    mgt = bass.AP(cmf.t, 128, [[512, 128], [0, NCH], [1, 64]])
                    S.op("dve", lambda e: e.tensor_tensor(out=MBs[:].rearrange("p c a b -> p c (a b)"),
                                                          in0=pMB[:, 0:NCH * 128].rearrange("p (c x) -> p c x", c=NCH), in1=mk2, op=ALU.mult),
                         reads=[pMB, cmf], writes=[MBs])
                    S.op("dve", lambda e: e.tensor_tensor(out=MKs[:].rearrange("p c a b -> p c (a b)"),
                                                          in0=pMK[:, 0:NCH * 128].rearrange("p (c x) -> p c x", c=NCH), in1=mk2, op=ALU.mult),
                         reads=[pMK, cmf], writes=[MKs])
                    S.op("dve", lambda e: e.tensor_tensor(out=Bp[0][:], in0=pN[:, 0:NCH * 64].rearrange("p (c x) -> p c x", c=NCH), in1=mgt, op=ALU.mult),
                         reads=[pN, cmf], writes=[Bp[0]])
                    idb = bass.AP(cmf.t, 192, [[512, 128], [0, NCH], [1, 64]])
                    S.op("pool", lambda e: e.tensor_tensor(out=TTf[:], in0=MBs[:, :, 0, :], in1=idb, op=ALU.add), reads=[MBs, cmf], writes=[TTf])
                    S.op("pool", lambda e: e.tensor_copy(out=TTt[:], in_=TTf[:]), reads=[TTf], writes=[TTt])
                    Acur = None
                    Bcur = Bp[0]
                    bi = 0

                    def A_ap(c, pr):
                        return MBs[pr, c, 0, :] if Acur is None else Acur[pr, c, :]
                    for lvl in range(1, 6):
                        pB = self.psum()
                        pA = self.psum() if lvl < 5 else None
                        Areads = [MBs] if Acur is None else [Acur]
                        for c in range(NCH):
                            for hh in range(2):
                                pr = PRS[hh]
                                lastm = (c == NCH - 1 and hh == 1)
                                S.op("pe", lambda e: e.matmul(pB[pr, c * 64:(c + 1) * 64], A_ap(c, pr), Bcur[pr, c, :], start=True, stop=True),
                                     reads=Areads + [Bcur], writes=[pB], signal=lastm)
                                if pA is not None:
                                    S.op("pe", lambda e: e.matmul(pA[pr, c * 64:(c + 1) * 64], Bcur[pr, c, :], A_ap(c, pr), start=True, stop=True),
                                         reads=Areads + [Bcur], writes=[pA], signal=lastm)
                        Bn = Bp[(bi + 1) % 2]
                        S.op("act", lambda e: e.copy(out=Bn[:].rearrange("p c x -> p (c x)"), in_=pB[:, 0:NCH * 64]), reads=[pB], writes=[Bn])
                        if pA is not None:
                            An = Ap[bi % 2]
                            S.op("dve", lambda e: e.tensor_copy(out=An[:].rearrange("p c x -> p (c x)"), in_=pA[:, 0:NCH * 64]), reads=[pA], writes=[An])
                        pTu = self.psum()
                        for c in range(NCH):
                            for hh in range(2):
                                pr = PRS[hh]
                                lastm = (c == NCH - 1 and hh == 1)
                                S.op("pe", lambda e: e.matmul(pTu[pr, c * 64:(c + 1) * 64], Bn[pr, c, :], TTt[pr, c, :], start=True, stop=True),
                                     reads=[Bn, TTt], writes=[pTu], signal=lastm)
                        S.op("dve", lambda e: e.tensor_tensor(out=TTf[:].rearrange("p c x -> p (c x)"), in0=TTf[:].rearrange("p c x -> p (c x)"),
                                                              in1=pTu[:, 0:NCH * 64], op=ALU.add), reads=[pTu, TTf], writes=[TTf])
                        S.op("pool", lambda e: e.tensor_copy(out=TTt[:], in_=TTf[:]), reads=[TTf], writes=[TTt])
                        if pA is not None:
                            Acur = An
                        Bcur = Bn
                        bi += 1
                    pY = self.ps[7]
                    for c in range(NCH):
                        cs = slice(c * 64, (c + 1) * 64)
                        pX = self.psum()
                        for hh in range(2):
                            pr = PRS[hh]
                            S.op("pe", lambda e: e.matmul(pX[pr, 0:64], AR[pr, 0, cs], Sb[ft][pr, :], start=True, stop=False),
                                 reads=[AR, Sb[ft]], writes=[pX], signal=False)
                            S.op("pe", lambda e: e.matmul(pX[pr, 0:64], MKs[pr, c, 0, :], TM[pr, 0, c, :], start=False, stop=True),
                                 reads=[MKs, TM], writes=[pX], signal=(hh == 1))
                        S.op("act", lambda e: e.copy(out=Xs[:], in_=pX[:, 0:64]), reads=[pX], writes=[Xs])
                        pU = self.psum()
                        for hh in range(2):
                            pr = PRS[hh]
                            S.op("pe", lambda e: e.matmul(pU[pr, 0:64], TTt[pr, c, :], Xs[pr, :], start=True, stop=True),
                                 reads=[TTt, Xs], writes=[pU], signal=(hh == 1))
                        S.op("dve", lambda e: e.tensor_copy(out=Us[:], in_=pU[:, 0:64]), reads=[pU], writes=[Us])
                        for hh in range(2):
                            pr = PRS[hh]
                            S.op("pe", lambda e: e.matmul(pY[pr, cs], Sb[ft][pr, :], AR[pr, 1, cs], start=True, stop=False),
                                 reads=[Sb[ft], AR], writes=[pY], signal=False)
                            S.op("pe", lambda e: e.matmul(pY[pr, cs], Us[pr, :], MBs[pr, c, 1, :], start=False, stop=False),
                                 reads=[Us, MBs], writes=[pY], signal=False)
                            S.op("pe", lambda e: e.matmul(pY[pr, cs], TM[pr, 0, c, :], MKs[pr, c, 1, :], start=False, stop=True),
                                 reads=[TM, MKs], writes=[pY], signal=(hh == 1))
                        pS = self.psum()
                        for hh in range(2):
                            pr = PRS[hh]
                            S.op("pe", lambda e: e.matmul(pS[pr, 0:64], TM[pr, 1, c, :], Us[pr, :], start=True, stop=False),
                                 reads=[TM, Us], writes=[pS], signal=False)
                            S.op("pe", lambda e: e.matmul(pS[pr, 0:64], TM[pr, 2, c, :], TM[pr, 0, c, :], start=False, stop=True),
                                 reads=[TM], writes=[pS], signal=(hh == 1))
                        S.op("dve", lambda e: e.scalar_tensor_tensor(out=Sf[ft][:], in0=Sf[ft][:], scalar=WC[:, c:c + 1], in1=pS[:, 0:64],
                                                                     op0=ALU.mult, op1=ALU.add), reads=[Sf[ft], WC, pS], writes=[Sf[ft]])
                        S.op("act", lambda e: e.copy(out=Sb[ft][:], in_=Sf[ft][:]), reads=[Sf[ft]], writes=[Sb[ft]])
                    S.op("act", lambda e: e.copy(out=ysb[:], in_=pY[:, 0:TR]), reads=[pY], writes=[ysb])
                    S.op("pool", lambda e: e.tensor_copy(out=ybf[:], in_=ysb[:]), reads=[ysb], writes=[ybf])
                    p = self.psum()
                    S.op("pe", lambda e: e.matmul(p[:, 0:TR], bo[:], ybf[:], start=True, stop=True), reads=[bo, ybf], writes=[p])
                    S.op("dve", lambda e: e.scalar_tensor_tensor(out=yc[:], in0=p[:, 0:TR], scalar=-1.0 / 64, in1=ysb[:], op0=ALU.mult, op1=ALU.add),
                         reads=[p, ysb], writes=[yc])
                    S.op("act", lambda e: e.activation(out=sqb[:], in_=yc[:], func=AF.Square), reads=[yc], writes=[sqb])
                    p = self.psum()
                    S.op("pe", lambda e: e.matmul(p[:, 0:TR], bo[:], sqb[:], start=True, stop=True), reads=[bo, sqb], writes=[p])
                    S.op("act", lambda e: e.activation(out=sd[:], in_=p[:, 0:TR], func=AF.Sqrt, scale=1.0 / 64, bias=epsl[:, 0:1]),
                         reads=[p, epsl], writes=[sd])
                    S.op("dve", lambda e: e.reciprocal(out=sd[:], in_=sd[:]), reads=[sd], writes=[sd])
                    S.op("dve", lambda e: e.tensor_tensor(out=yn[:], in0=yc[:], in1=sd[:], op=ALU.mult), reads=[yc, sd], writes=[yn])
                    S.op("dve", lambda e: e.tensor_scalar(out=yn[:], in0=yn[:], scalar1=pcol(P_LW, ft), scalar2=pcol(P_LB, ft),
                                                          op0=ALU.mult, op1=ALU.add), reads=[yn, par], writes=[yn])
                    p = self.psum()
                    S.op("pe", lambda e: e.matmul(p[:, 0:TR], bo[:], rkb[:], start=True, stop=True), reads=[bo, rkb], writes=[p])
                    S.op("dve", lambda e: e.tensor_tensor(out=bon[:], in0=p[:, 0:TR], in1=v_, op=ALU.mult), reads=[p, Vs], writes=[bon])
                    S.op("pool", lambda e: e.tensor_tensor(out=yn[:], in0=yn[:], in1=bon[:], op=ALU.add), reads=[yn, bon], writes=[yn])
                    S.op("pool", lambda e: e.tensor_tensor(out=zT[:, ft, :], in0=yn[:], in1=g_, op=ALU.mult), reads=[yn, Gs], writes=[zT])
                    if getattr(self, "debug", None) == (pn, ft):
                        items = [("r", Rs, r_), ("k", Ks, k_), ("v", Vs, v_), ("sw", sw, sw[:]), ("aa", aa, aa[:]), ("cum", cum, cum[:]),
                                 ("kkn", kkn, kkn[:]), ("kp", kp, kp[:]), ("at", AR, AR[:, 0, :]), ("rt", AR, AR[:, 1, :]), ("bT", bT, bT[:]),
                                 ("A", MBs, MBs[:, :, 0, :]), ("MrbT", MBs, MBs[:, :, 1, :]), ("MakT", MKs, MKs[:, :, 0, :]),
                                 ("TT", TTf, TTf[:]), ("ysb", ysb, ysb[:]), ("yn", yn, yn[:]), ("z", zT, zT[:, ft, :]),
                                 ("VT", TM, TM[:, 0, :, :]), ("g", Gs, g_), ("Wt", Wt, Wt[:]), ("Wend", Wend, Wend[:]), ("B0", Bp[0], Bp[0][:])]
                        self.dbg_names = [n for n, _, _ in items]
                        dbg = nc.dram_tensor("dbg", [128, len(items) * TR], F32, kind="ExternalOutput")
                        for di, (n, tt_, ap) in enumerate(items):
                            dst_ap = dbg.ap()[:, di * TR:(di + 1) * TR]
                            if len(ap.shape) == 3:
                                dst_ap = dst_ap.rearrange("p (a b) -> p a b", a=ap.shape[1])
                            S.dma("pool", dst_ap, ap, reads=[tt_], writes=[TT(None)], is_output=True)
            for i in range(2):
                tile = pn * 2 + i
                S.dma("sp", self.xt[i][:], src.ap()[tile * 128:(tile + 1) * 128, :], reads=[self.xbuf(src, tile)], writes=[self.xt[i]])
            for c in range(4):
                w = wsl[wslc[0] % 2]
                wslc[0] += 1
                idx = 16 + c
                S.dma("sp", w[:].rearrange("p a c -> p (a c)"), wsc.ap()[idx], reads=[wsc_b[idx]], writes=[w])
                for i in range(2):
                    p = self.psum()
                    for fi in range(NFT):
                        S.op("pe", lambda e: e.matmul(p[:], zT[:, fi, i * 128:(i + 1) * 128], w[:, fi, :],
                                                      start=(fi == 0), stop=(fi == NFT - 1)),
                             reads=[zT, w], writes=[p], signal=(fi == NFT - 1))
                    xt = self.xt[i]
                    S.op("dve", lambda e: e.tensor_tensor(out=xt[:, c * 512:(c + 1) * 512], in0=xt[:, c * 512:(c + 1) * 512],
                                                          in1=p[:], op=ALU.add), reads=[xt, p], writes=[xt])
            for i in range(2):
                tile = pn * 2 + i
                S.dma("sp", dst.ap()[tile * 128:(tile + 1) * 128, :], self.xt[i][:], reads=[self.xt[i]],
                      writes=[self.xbuf(dst, tile)], is_output=last)


def pm(v):
    return np.ascontiguousarray(np.asarray(v, np.float32).reshape(NFT, 128).T)


def host_consts():
    c = {}
    c["ident"] = np.eye(128, dtype=np.float32)
    rc = np.zeros((4, 16), np.float32)
    for g in range(4):
        win = 2 << g
        for t in range(16):
            rc[g, t] = 1.0 / min(t + 1, win)
    invf = (500000.0 ** (-np.arange(0, 32, 2, dtype=np.float32) / 32)).astype(np.float32)
    c["rope_invf"] = np.concatenate([invf, invf]).reshape(32, 1).astype(np.float32)
    rt = np.zeros((32, 32), np.float32)
    for i in range(16):
        rt[i + 16, i] = -1.0
        rt[i, i + 16] = 1.0
    c["rope_rot"] = rt
    kk = np.arange(128)[:, None]
    qq = np.arange(128)[None, :]
    c["attn_mask"] = np.concatenate([(qq <= kk), (kk <= qq)], axis=1).astype(np.float32)
    s_ = np.arange(64)[:, None]
    t_ = np.arange(64)[None, :]
    lt = (s_ < t_).astype(np.float32)
    le = (s_ <= t_).astype(np.float32)
    gt = (s_ > t_).astype(np.float32)
    eye = np.eye(64, dtype=np.float32)
    rm = np.ones((64, 256), np.float32)
    rm[:, ::64] = 0.0
    blk = np.concatenate([lt, le, gt, eye, rm], axis=1)
    c["rwkv_masks"] = np.ascontiguousarray(np.concatenate([blk, blk], axis=0))
    c["pool_rc"] = np.ascontiguousarray(np.broadcast_to(rc.reshape(1, 64), (128, 64)))
    return c


def shared_inputs(inp, layers):
    m = dict(host_consts())
    m["norm_w"] = np.ascontiguousarray(inp["norm_w"], dtype=np.float32)
    for L, ia in ((0, 0), (3, 1)):
        if L not in layers:
            continue
        pre = "a%d_" % ia
        m[pre + "w_in"] = np.ascontiguousarray(np.asarray(inp["a_w_in"][ia]).reshape(4 * D, D))
        m[pre + "w_out"] = np.ascontiguousarray(inp["a_w_out"][ia])
        m[pre + "mu_pm"] = np.ascontiguousarray(np.concatenate([pm(inp["a_mu"][ia][j]) for j in range(6)], axis=1))
        plist = [inp["a_w0"][ia], inp["a_a0"][ia], inp["a_k_k"][ia], inp["a_k_a"][ia], np.asarray(inp["a_r_k"][ia]).reshape(-1),
                 inp["a_lnx_w"][ia], inp["a_lnx_b"][ia], inp["a_v0"][ia - 1] if ia > 0 else np.zeros(D, np.float32)]
        m[pre + "par_pm"] = np.ascontiguousarray(np.concatenate([pm(v) for v in plist], axis=1))
        m[pre + "w1"] = np.ascontiguousarray(inp["a_w1"][ia])
        m[pre + "a1"] = np.ascontiguousarray(inp["a_a1"][ia])
        m[pre + "w2"] = np.ascontiguousarray(inp["a_w2"][ia])
        m[pre + "a2"] = np.ascontiguousarray(inp["a_a2"][ia])
        if ia > 0:
            m[pre + "v1"] = np.ascontiguousarray(inp["a_v1"][ia - 1])
            m[pre + "v2"] = np.ascontiguousarray(inp["a_v2"][ia - 1])
    if 1 in layers:
        w = np.asarray(inp["b_w_in"][0])
        cols = []
        for hd in range(8):
            order = [(0, 0), (1, 0), (2, 0), None, (0, 1), (1, 1), (2, 1), (0, 2), (1, 2), (2, 2)]
            for o in order:
                if o is None:
                    c0 = 9216 + hd * 128
                else:
                    sidx, g = o
                    c0 = ((sidx * 3 + g) * 8 + hd) * 128
                cols.append(np.arange(c0, c0 + 128))
        cols = np.concatenate(cols)
        m["b_w_in_r"] = np.ascontiguousarray(w[:, cols])
        m["b_w_out"] = np.ascontiguousarray(inp["b_w_out"][0])
        m["b_qkn_pm"] = np.ascontiguousarray(np.concatenate([np.asarray(inp["b_qn_w"][0]).T, np.asarray(inp["b_kn_w"][0]).T], axis=1).astype(np.float32))
    if 2 in layers:
        m["c_w_in"] = np.ascontiguousarray(inp["c_w_in"][0])
        m["c_w_grp"] = np.ascontiguousarray(inp["c_w_grp"][0].reshape(4 * 512, 512))
        m["c_w_out"] = np.ascontiguousarray(inp["c_w_out"][0])
        m["c_scale_pm"] = pm(inp["c_scale"][0])
    return m


_CACHE = {}
FUSED = True


def _prog(layers):
    if layers not in _CACHE:
        pr = Prog(layers)
        pr.build()
        _CACHE[layers] = pr
    return _CACHE[layers]


def kernel(**inp):
    all_layers = (0, 1, 2, 3)
    groups = [all_layers] if FUSED else [(0,), (1,), (2,), (3,)]
    x = np.asarray(inp["x"], np.float32)
    xs = [np.ascontiguousarray(x[c * NB_CORE:(c + 1) * NB_CORE].reshape(NB_CORE * SEQ, D)) for c in range(8)]
    pos = np.asarray(inp["positions"])
    vfs = None
    for layers in groups:
        pr = _prog(layers)
        shared = shared_inputs(inp, layers)
        in_maps = []
        for c in range(8):
            m = dict(shared)
            m["x"] = xs[c]
            m["positions"] = np.ascontiguousarray(pos[c * NB_CORE:(c + 1) * NB_CORE].astype(np.int32))
            if vfs is not None:
                m["vfs"] = vfs[c]
            m = {k: v for k, v in m.items() if k in pr.inputs}
            in_maps.append(m)
        res = run_bass_kernel_spmd(pr.nc, in_maps, core_ids=list(range(8)))
        xs = [np.asarray(res.results[c]["out"]) for c in range(8)]
        if "vfs" in res.results[0]:
            vfs = [np.asarray(res.results[c]["vfs"]) for c in range(8)]
    out = np.stack([xs[c].reshape(NB_CORE, SEQ, D) for c in range(8)], axis=0)
    return out.reshape(16, SEQ, D).astype(np.float32)
```

```python
import numpy as np
from contextlib import ExitStack
import concourse.bass as bass
import concourse.mybir as mybir
from concourse.bass_utils import run_bass_kernel_spmd

F32 = mybir.dt.float32
BF16 = mybir.dt.bfloat16
I32 = mybir.dt.int32
AF = mybir.ActivationFunctionType
ALU = mybir.AluOpType
AX = mybir.AxisListType

D = 2048
NFT = 16
SEQ = 2048
NB_CORE = 2
TP = 512
NPB = SEQ // TP
RMS_EPS = 1e-6
NDMA = 24
NDMA_SW = 8
SAME_ENGINE_SYNC = True


class Buf:
    __slots__ = ("w", "r")

    def __init__(self):
        self.w = None
        self.r = {}


class TT:
    def __init__(self, t):
        self.t = t
        self.b = Buf()

    def __getitem__(self, k):
        return self.t[k]


class Sched:
    def __init__(self, nc, es):
        self.nc = nc
        self.eng = {"pe": nc.tensor, "act": nc.scalar, "dve": nc.vector, "pool": nc.gpsimd, "sp": nc.sync}
        self.sem = {}
        self.cnt = {}
        for e in ("pe", "act", "dve", "pool"):
            self.sem[e] = es.enter_context(nc.semaphore("s_" + e))
            self.cnt[e] = 0
        for i in range(NDMA):
            self.sem[("dma", i)] = es.enter_context(nc.semaphore("s_dma%d" % i))
            self.cnt[("dma", i)] = 0
        self.seen = {e: {} for e in self.eng}
        self.rr = 0
        self.rr_sw = 0
        self.out_tokens = []
        self.nwait = 0

    def wait(self, e, tok):
        if tok is None:
            return
        key, val = tok
        if key == e and (e == "pe" or not SAME_ENGINE_SYNC):
            return
        if self.seen[e].get(key, 0) >= val:
            return
        self.eng[e].wait_ge(self.sem[key], val)
        self.seen[e][key] = val
        self.nwait += 1

    def _deps(self, e, reads, writes):
        for b in reads:
            self.wait(e, b.b.w)
        for b in writes:
            self.wait(e, b.b.w)
            for k, v in b.b.r.items():
                self.wait(e, (k, v))

    def _commit(self, tok, reads, writes):
        k, v = tok
        for b in reads:
            if b.b.r.get(k, 0) < v:
                b.b.r[k] = v
        for b in writes:
            b.b.w = tok
            b.b.r = {}

    def op(self, e, fn, reads=(), writes=(), signal=True):
        self._deps(e, reads, writes)
        ins = fn(self.eng[e])
        if signal:
            self.cnt[e] += 1
            ins.then_inc(self.sem[e], 1)
            tok = (e, self.cnt[e])
        else:
            tok = (e, self.cnt[e] + 1)
        self._commit(tok, reads, writes)
        return tok

    def dma(self, q, out, in_, reads=(), writes=(), is_output=False):
        self._deps(q, reads, writes)
        if q == "pool":
            i = self.rr_sw
            self.rr_sw = (self.rr_sw + 1) % NDMA_SW
        else:
            i = NDMA_SW + self.rr
            self.rr = (self.rr + 1) % (NDMA - NDMA_SW)
        key = ("dma", i)
        if self.cnt[key] > 0:
            self.wait(q, (key, self.cnt[key]))
        self.cnt[key] += 16
        self.eng[q].dma_start(out=out, in_=in_).then_inc(self.sem[key], 16)
        tok = (key, self.cnt[key])
        self._commit(tok, reads, writes)
        if is_output:
            self.out_tokens.append(tok)
        return tok

    def barrier(self):
        for e in ("pe", "act", "dve", "pool", "sp"):
            for k in list(self.cnt.keys()):
                if self.cnt[k] > 0 and k != e:
                    self.wait(e, (k, self.cnt[k]))

    def finish(self):
        for tok in self.out_tokens:
            self.wait("sp", tok)
        for e in ("pe", "act", "dve", "pool"):
            if self.cnt[e] > 0:
                self.wait("sp", (e, self.cnt[e]))


def bcast_rows(ap_row, nparts):
    t = ap_row.tensor
    pairs = list(ap_row.ap)
    return bass.AP(t, ap_row.offset, [[0, nparts]] + [list(p) for p in pairs[1:]])


class Prog:
    def __init__(self, layers=(0, 1, 2, 3), n_panels=NB_CORE * NPB):
        self.layers = layers
        self.n_panels = n_panels
        self.nc = bass.Bass("TRN2", target_bir_lowering=False)
        self.es = ExitStack()
        self.inputs = {}
        self.uid = 0
        self.scopes = []

    def dram_in(self, name, shape, dtype=F32):
        if name in self.inputs:
            return self.inputs[name]
        t = self.nc.dram_tensor(name, list(shape), dtype, kind="ExternalInput")
        self.inputs[name] = t
        return t

    def sb(self, name, shape, dtype):
        self.uid += 1
        st = self.scopes[-1] if self.scopes else self.es
        return TT(st.enter_context(self.nc.sbuf_tensor("sb%d_%s" % (self.uid, name), list(shape), dtype)))

    def scope(self):
        prog = self

        class _Scope:
            def __enter__(self_):
                self_.st = ExitStack()
                prog.scopes.append(self_.st)
                return self_

            def __exit__(self_, *a):
                prog.S.barrier()
                prog.scopes.pop()
                self_.st.close()
                return False
        return _Scope()

    def build(self):
        nc = self.nc
        es = self.es
        S = self.S = Sched(nc, es)
        x_in = self.dram_in("x", [NB_CORE * SEQ, D])
        self.ident_d = self.dram_in("ident", [128, 128])
        self.normw_d = self.dram_in("norm_w", [4, D])
        out_d = nc.dram_tensor("out", [NB_CORE * SEQ, D], F32, kind="ExternalOutput")
        nl = len(self.layers)
        scr = [nc.dram_tensor("xs%d" % i, [NB_CORE * SEQ, D], F32, kind="Internal") for i in range(2)]
        chain = [x_in]
        for i in range(nl - 1):
            chain.append(scr[i % 2])
        chain.append(out_d)
        self.xbufs = {}

        def xbuf(t, tile):
            k = (t.name, tile)
            if k not in self.xbufs:
                self.xbufs[k] = TT(None)
            return self.xbufs[k]
        self.xbuf = xbuf

        self.ident = self.sb("ident", [128, 128], BF16)
        S.dma("pool", self.ident[:], self.ident_d.ap()[:, :], writes=[self.ident])
        self.ps = [TT(es.enter_context(nc.psum_tensor("ps%d" % i, [128, 512], F32))) for i in range(8)]
        self.ps_i = 0

        for li, L in enumerate(self.layers):
            src, dst = chain[li], chain[li + 1]
            last = li == nl - 1
            with self.scope():
                if L == 2:
                    self.layer_pool(src, dst, last)
                elif L == 1:
                    self.layer_attn(src, dst, last)
                else:
                    self.layer_rwkv(L, src, dst, last)
        S.finish()
        return nc

    def psum(self):
        p = self.ps[self.ps_i]
        self.ps_i = (self.ps_i + 1) % 6
        return p

    def alloc_wsl(self, n=2):
        self.wsl = [self.sb("wsl%d" % i, [128, NFT, 512], BF16) for i in range(n)]
        self.wsl_i = 0

    def wslab(self):
        w = self.wsl[self.wsl_i]
        self.wsl_i = (self.wsl_i + 1) % len(self.wsl)
        return w

    def load_wslab(self, w_ap2d, c0, ncols=512, nrows_t=NFT):
        w = self.wslab()
        src = w_ap2d[:, c0:c0 + ncols].rearrange("(a p) c -> p a c", p=128)
        self.S.dma("pool", w[:, 0:nrows_t, 0:ncols], src, writes=[w])
        return w

    def alloc_norm(self, layer, nxt):
        self.normw = self.sb("normw", [128, D], F32)
        row = self.normw_d.ap()[layer:layer + 1, :]
        self.S.dma("sp", self.normw[:], bcast_rows(row, 128), writes=[self.normw])
        self.xt = [self.sb("xt%d" % i, [128, D], F32) for i in range(nxt)]
        self.hb = [self.sb("hb%d" % i, [128, D], BF16) for i in range(2)]
        self.stat = [self.sb("stat%d" % i, [128, 4], F32) for i in range(2)]
        self.epsb = self.sb("epsb", [128, 1], F32)
        self.S.op("dve", lambda e: e.memset(self.epsb[:], RMS_EPS), writes=[self.epsb])

    def norm_tiles(self, src, tile0, ntiles, hT):
        S = self.S
        nx = len(self.xt)
        for i in range(ntiles):
            tile = tile0 + i
            xt = self.xt[i % nx]
            st = self.stat[i % 2]
            S.dma("sp", xt[:], src.ap()[tile * 128:(tile + 1) * 128, :], reads=[self.xbuf(src, tile)], writes=[xt])
            hb = self.hb[i % 2]
            S.op("act", lambda e: e.activation(out=hb[:], in_=xt[:], func=AF.Square, accum_out=st[:, 0:1]),
                 reads=[xt], writes=[hb, st])
            S.op("act", lambda e: e.activation(out=st[:, 1:2], in_=st[:, 0:1], func=AF.Sqrt, scale=1.0 / D, bias=self.epsb[:, 0:1]),
                 reads=[st, self.epsb], writes=[st])
            S.op("dve", lambda e: e.reciprocal(out=st[:, 2:3], in_=st[:, 1:2]), reads=[st], writes=[st])
            hb = self.hb[i % 2]
            S.op("dve", lambda e: e.scalar_tensor_tensor(out=hb[:], in0=xt[:], scalar=st[:, 2:3], in1=self.normw[:],
                                                         op0=ALU.mult, op1=ALU.mult),
                 reads=[xt, st, self.normw], writes=[hb])
            for half in range(2):
                p = self.psum()
                pv = p.t[:].bitcast(BF16)
                for j in range(8):
                    ft = half * 8 + j
                    S.op("pe", lambda e: e.transpose(pv[:, j * 128:(j + 1) * 128], hb[:, ft * 128:(ft + 1) * 128],
                                                     self.ident[:]),
                         reads=[hb, self.ident], writes=[p], signal=(j == 7))
                S.op("act", lambda e: e.copy(out=hT[:, half * 8:half * 8 + 8, i * 128:(i + 1) * 128],
                                             in_=pv.rearrange("p (a c) -> p a c", a=8)),
                     reads=[p], writes=[hT])

    def layer_pool(self, src, dst, last):
        S = self.S
        w_in = self.dram_in("c_w_in", [D, 2 * D]).ap()
        w_grp = self.dram_in("c_w_grp", [4 * 512, 512]).ap()
        w_out = self.dram_in("c_w_out", [D, D]).ap()
        scale_d = self.dram_in("c_scale_pm", [128, NFT]).ap()
        rc_d = self.dram_in("pool_rc", [128, 4 * 16]).ap()
        self.alloc_norm(2, 4)
        self.alloc_wsl()
        hT = self.sb("hT", [128, NFT, TP], BF16)
        scale = self.sb("c_scale", [128, NFT], F32)
        S.dma("sp", scale[:], scale_d[:, :], writes=[scale])
        rc = self.sb("pool_rc", [128, 4, 16], F32)
        S.dma("sp", rc[:].rearrange("p a b -> p (a b)"), rc_d[:, :], writes=[rc])
        wg = self.sb("wgrp", [128, 16, 512], BF16)
        S.dma("pool", wg[:], w_grp.rearrange("(a p) c -> p a c", p=128), writes=[wg])
        hist = self.sb("hist", [128, NFT, 16], F32)
        ub = [self.sb("ub%d" % i, [128, 16 + TP], F32) for i in range(2)]
        tmp = [self.sb("ptmp%d" % i, [128, 16 + TP], F32) for i in range(2)]
        dT = self.sb("dT", [128, NFT, TP], BF16)
        sg = self.sb("sg", [128, NFT, TP], BF16)
        zT = self.sb("zT", [128, NFT, TP], BF16)
        for panel in range(self.n_panels):
            first = (panel % NPB == 0)
            self.norm_tiles(src, panel * 4, 4, hT)
            if first:
                S.op("dve", lambda e: e.memset(hist[:], 0.0), writes=[hist])
            for s in range(8):
                w = self.load_wslab(w_in, s * 512)
                for j in range(4):
                    fo = s * 4 + j
                    p = self.psum()
                    for fi in range(NFT):
                        S.op("pe", lambda e: e.matmul(p[:], w[:, fi, j * 128:(j + 1) * 128], hT[:, fi, :],
                                                      start=(fi == 0), stop=(fi == NFT - 1)),
                             reads=[w, hT], writes=[p], signal=(fi == NFT - 1))
                    if fo < NFT:
                        g = fo // 4
                        win = 2 << g
                        u = ub[fo % 2]
                        S.op("act", lambda e: e.copy(out=u[:, 16:16 + TP], in_=p[:]), reads=[p], writes=[u])
                        S.op("pool", lambda e: e.tensor_copy(out=u[:, 0:16], in_=hist[:, fo, :]), reads=[hist], writes=[u])
                        S.op("pool", lambda e: e.tensor_copy(out=hist[:, fo, :], in_=u[:, TP:TP + 16]), reads=[u], writes=[hist])
                        cur = u
                        sh = 1
                        k = 0
                        while sh < win:
                            nxt = tmp[k % 2]
                            lo = 2 * sh - 1
                            S.op("dve", lambda e: e.tensor_tensor(out=nxt[:, lo:16 + TP], in0=cur[:, lo:16 + TP],
                                                                  in1=cur[:, lo - sh:16 + TP - sh], op=ALU.add),
                                 reads=[cur], writes=[nxt])
                            cur = nxt
                            sh *= 2
                            k += 1
                        S.op("dve", lambda e: e.scalar_tensor_tensor(out=dT[:, fo, :], in0=cur[:, 16:16 + TP], scalar=1.0 / win,
                                                                     in1=u[:, 16:16 + TP], op0=ALU.mult, op1=ALU.subtract),
                             reads=[cur, u], writes=[dT])
                        if first:
                            t2 = tmp[k % 2]
                            S.op("dve", lambda e: e.tensor_tensor(out=t2[:, 0:16], in0=cur[:, 16:32], in1=rc[:, g, :], op=ALU.mult),
                                 reads=[cur, rc], writes=[t2])
                            S.op("dve", lambda e: e.tensor_tensor(out=dT[:, fo, 0:16], in0=t2[:, 0:16], in1=u[:, 16:32], op=ALU.subtract),
                                 reads=[t2, u], writes=[dT])
                    else:
                        S.op("act", lambda e: e.activation(out=sg[:, fo - NFT, :], in_=p[:], func=AF.Silu), reads=[p], writes=[sg])
            for fo in range(NFT):
                g = fo // 4
                jo = fo % 4
                p = self.psum()
                for fi in range(4):
                    S.op("pe", lambda e: e.matmul(p[:], wg[:, g * 4 + fi, jo * 128:(jo + 1) * 128], dT[:, g * 4 + fi, :],
                                                  start=(fi == 0), stop=(fi == 3)),
                         reads=[wg, dT], writes=[p], signal=(fi == 3))
                S.op("dve", lambda e: e.scalar_tensor_tensor(out=zT[:, fo, :], in0=p[:], scalar=scale[:, fo:fo + 1],
                                                             in1=sg[:, fo, :], op0=ALU.mult, op1=ALU.mult),
                     reads=[p, scale, sg], writes=[zT])
            self.out_proj(w_out, zT, NFT, src, dst, panel * 4, last, reload_x=False)

    def out_proj(self, w_out, zT, nfi, src, dst, tile0, last, reload_x):
        S = self.S
        if reload_x:
            for i in range(4):
                tile = tile0 + i
                S.dma("sp", self.xt[i][:], src.ap()[tile * 128:(tile + 1) * 128, :], reads=[self.xbuf(src, tile)],
                      writes=[self.xt[i]])
        for c in range(4):
            w = self.load_wslab(w_out, c * 512, nrows_t=nfi)
            for i in range(4):
                p = self.psum()
                for fi in range(nfi):
                    S.op("pe", lambda e: e.matmul(p[:], zT[:, fi, i * 128:(i + 1) * 128], w[:, fi, :],
                                                  start=(fi == 0), stop=(fi == nfi - 1)),
                         reads=[zT, w], writes=[p], signal=(fi == nfi - 1))
                xt = self.xt[i]
                S.op("dve", lambda e: e.tensor_tensor(out=xt[:, c * 512:(c + 1) * 512], in0=xt[:, c * 512:(c + 1) * 512],
                                                      in1=p[:], op=ALU.add), reads=[xt, p], writes=[xt])
        for i in range(4):
            tile = tile0 + i
            S.dma("sp", dst.ap()[tile * 128:(tile + 1) * 128, :], self.xt[i][:], reads=[self.xt[i]],
                  writes=[self.xbuf(dst, tile)], is_output=last)

    def layer_attn(self, src, dst, last):
        S = self.S
        nc = self.nc
        w_in = self.dram_in("b_w_in_r", [D, 8 * 1280]).ap()
        w_out = self.dram_in("b_w_out", [1024, D]).ap()
        qkn_d = self.dram_in("b_qkn_pm", [128, 6]).ap()
        pos_d = self.dram_in("positions", [NB_CORE, SEQ], I32).ap()
        invf_d = self.dram_in("rope_invf", [32, 1]).ap()
        rot_d = self.dram_in("rope_rot", [32, 32]).ap()
        mask_d = self.dram_in("attn_mask", [128, 256]).ap()
        zscr = [nc.dram_tensor("zscr%d" % b, [1024, SEQ], BF16, kind="Internal") for b in range(NB_CORE)]
        zbuf = [[TT(None) for hd in range(8)] for b in range(NB_CORE)]
        hTf = self.sb("hTf", [128, NFT, SEQ], BF16)
        mask = self.sb("amask", [128, 256], BF16)
        S.dma("pool", mask[:], mask_d[:, :], writes=[mask])
        rot = self.sb("rot", [32, 32], BF16)
        S.dma("pool", rot[:], rot_d[:, :], writes=[rot])
        qkn = self.sb("qkn", [128, 6], F32)
        S.dma("sp", qkn[:], qkn_d[:, :], writes=[qkn])
        invf = self.sb("invf", [32, 1], F32)
        S.dma("sp", invf[:], invf_d[:, :], writes=[invf])
        ones = self.sb("ones", [128, 128], BF16)
        S.op("dve", lambda e: e.memset(ones[:], 1.0), writes=[ones])
        DILS = (1, 4, 16)
        SCALE = 128.0 ** -0.5
        PI = float(np.pi)
        C1 = 6.28125
        C2 = float(2 * np.pi - 6.28125)
        nb = self.n_panels // NPB
        for b in range(nb):
            with self.scope():
                self.alloc_norm(1, 2)
                self.norm_tiles(src, b * 16, 16, hTf)
            with self.scope():
                self.alloc_wsl()
                epsb = self.sb("epsb2", [128, 1], F32)
                S.op("dve", lambda e: e.memset(epsb[:], RMS_EPS), writes=[epsb])
                Ctab = self.sb("Ctab", [32, SEQ], F32)
                Stab = self.sb("Stab", [32, SEQ], F32)
                posi = self.sb("posi", [32, 512], I32)
                ra = self.sb("ra", [32, 512], F32)
                rb = self.sb("rb", [32, 512], F32)
                rc_ = self.sb("rc", [32, 512], F32)
                rqi = self.sb("rqi", [32, 512], I32)
                for c in range(4):
                    cs = slice(c * 512, (c + 1) * 512)
                    S.dma("sp", posi[:], bcast_rows(pos_d[b:b + 1, cs], 32), writes=[posi])
                    S.op("dve", lambda e: e.tensor_copy(out=ra[:], in_=posi[:]), reads=[posi], writes=[ra])
                    S.op("dve", lambda e: e.tensor_scalar(out=ra[:], in0=ra[:], scalar1=invf[:, 0:1], scalar2=None, op0=ALU.mult),
                         reads=[ra, invf], writes=[ra])
                    S.op("dve", lambda e: e.tensor_scalar(out=rb[:], in0=ra[:], scalar1=1.0 / (2 * PI), scalar2=None, op0=ALU.mult),
                         reads=[ra], writes=[rb])
                    S.op("dve", lambda e: e.tensor_copy(out=rqi[:], in_=rb[:]), reads=[rb], writes=[rqi])
                    S.op("dve", lambda e: e.tensor_copy(out=rb[:], in_=rqi[:]), reads=[rqi], writes=[rb])
                    S.op("dve", lambda e: e.scalar_tensor_tensor(out=rc_[:], in0=rb[:], scalar=-C1, in1=ra[:], op0=ALU.mult, op1=ALU.add),
                         reads=[rb, ra], writes=[rc_])
                    S.op("dve", lambda e: e.scalar_tensor_tensor(out=ra[:], in0=rb[:], scalar=-C2, in1=rc_[:], op0=ALU.mult, op1=ALU.add),
                         reads=[rb, rc_], writes=[ra])
                    S.op("dve", lambda e: e.tensor_scalar(out=rb[:], in0=ra[:], scalar1=PI, scalar2=-PI, op0=ALU.min, op1=ALU.max),
                         reads=[ra], writes=[rb])
                    S.op("act", lambda e: e.activation(out=Stab[:, cs], in_=rb[:], func=AF.Sin), reads=[rb], writes=[Stab])
                    S.op("dve", lambda e: e.tensor_scalar(out=rc_[:], in0=ra[:], scalar1=PI / 2, scalar2=None, op0=ALU.add),
                         reads=[ra], writes=[rc_])
                    S.op("dve", lambda e: e.tensor_scalar(out=rb[:], in0=rc_[:], scalar1=PI, scalar2=-2 * PI, op0=ALU.is_gt, op1=ALU.mult),
                         reads=[rc_], writes=[rb])
                    S.op("dve", lambda e: e.tensor_tensor(out=rc_[:], in0=rc_[:], in1=rb[:], op=ALU.add), reads=[rc_, rb], writes=[rc_])
                    S.op("dve", lambda e: e.tensor_scalar(out=rc_[:], in0=rc_[:], scalar1=PI, scalar2=-PI, op0=ALU.min, op1=ALU.max),
                         reads=[rc_], writes=[rc_])
                    S.op("act", lambda e: e.activation(out=Ctab[:, cs], in_=rc_[:], func=AF.Sin), reads=[rc_], writes=[Ctab])
                qT = self.sb("qT", [128, SEQ], BF16)
                kT = self.sb("kT", [128, SEQ], BF16)
                V = self.sb("V", [128, 16, 128], BF16)
                qf = [self.sb("qf%d" % i, [128, 512], F32) for i in range(2)]
                sq = [self.sb("sq%d" % i, [128, 512], BF16) for i in range(2)]
                rs = [self.sb("rs%d" % i, [128, 512], F32) for i in range(2)]
                rt1 = self.sb("rt1", [32, 512], F32)
                rt2 = self.sb("rt2", [32, 512], F32)
                PT = [self.sb("PT%d" % i, [128, 1024], BF16) for i in range(2)]
                acc = self.sb("acc", [128, SEQ], F32)
                lacc = self.sb("lacc", [128, SEQ], F32)
                sgate = self.sb("sgate", [128, SEQ], BF16)
                zst = self.sb("zst", [128, SEQ], BF16)
                cnt = [0]

                def proj_qk(w, coff, dstT, ncol):
                    for c in range(4):
                        cs = slice(c * 512, (c + 1) * 512)
                        p = self.psum()
                        for fi in range(NFT):
                            S.op("pe", lambda e: e.matmul(p[:], w[:, fi, coff:coff + 128], hTf[:, fi, cs],
                                                          start=(fi == 0), stop=(fi == NFT - 1)),
                                 reads=[w, hTf], writes=[p], signal=(fi == NFT - 1))
                        k = cnt[0] % 2
                        cnt[0] += 1
                        S.op("act", lambda e: e.copy(out=qf[k][:], in_=p[:]), reads=[p], writes=[qf[k]])
                        S.op("act", lambda e: e.activation(out=sq[k][:], in_=p[:], func=AF.Square), reads=[p], writes=[sq[k]])
                        p2 = self.psum()
                        S.op("pe", lambda e: e.matmul(p2[:], ones[:], sq[k][:], start=True, stop=True),
                             reads=[ones, sq[k]], writes=[p2])
                        S.op("act", lambda e: e.activation(out=rs[k][:], in_=p2[:], func=AF.Sqrt, scale=1.0 / 128, bias=epsb[:, 0:1]),
                             reads=[p2, epsb], writes=[rs[k]])
                        S.op("dve", lambda e: e.reciprocal(out=rs[k][:], in_=rs[k][:]), reads=[rs[k]], writes=[rs[k]])
                        S.op("dve", lambda e: e.scalar_tensor_tensor(out=dstT[:, cs], in0=qf[k][:], scalar=qkn[:, ncol:ncol + 1],
                                                                     in1=rs[k][:], op0=ALU.mult, op1=ALU.mult),
                             reads=[qf[k], qkn, rs[k]], writes=[dstT])
                        p3 = self.psum()
                        S.op("pe", lambda e: e.matmul(p3[0:32, :], rot[:], dstT[0:32, cs], start=True, stop=True),
                             reads=[rot, dstT], writes=[p3])
                        S.op("dve", lambda e: e.tensor_tensor(out=rt1[:], in0=p3[0:32, :], in1=Stab[:, cs], op=ALU.mult),
                             reads=[p3, Stab], writes=[rt1])
                        S.op("pool", lambda e: e.tensor_tensor(out=rt2[:], in0=dstT[0:32, cs], in1=Ctab[:, cs], op=ALU.mult),
                             reads=[dstT, Ctab], writes=[rt2])
                        S.op("dve", lambda e: e.tensor_tensor(out=dstT[0:32, cs], in0=rt1[:], in1=rt2[:], op=ALU.add),
                             reads=[rt1, rt2], writes=[dstT])

                def view(ap2d, g):
                    dil = DILS[g]
                    nblk = SEQ // dil // 128
                    return ap2d.rearrange("p (n i r) -> p r n i", n=nblk, i=128, r=dil)

                def blocks(g):
                    dil = DILS[g]
                    nblk = SEQ // dil // 128
                    return [(r, n) for r in range(dil) for n in range(nblk)]

                for hd in range(8):
                    for g in range(3):
                        if g == 0:
                            w = self.load_wslab(w_in, hd * 1280, ncols=512)
                        else:
                            w = self.load_wslab(w_in, hd * 1280 + 512 + (g - 1) * 384, ncols=384)
                        voff = 256
                        proj_qk(w, 0, qT, g)
                        proj_qk(w, 128, kT, 3 + g)
                        bl = blocks(g)
                        for tg in range(4):
                            p = self.psum()
                            for tl in range(4):
                                r, n = bl[tg * 4 + tl]
                                for fi in range(NFT):
                                    S.op("pe", lambda e: e.matmul(p[:, tl * 128:(tl + 1) * 128], view(hTf[:, fi, :], g)[:, r, n, :],
                                                                  w[:, fi, voff:voff + 128], start=(fi == 0), stop=(fi == NFT - 1)),
                                         reads=[w, hTf], writes=[p], signal=(fi == NFT - 1 and tl == 3))
                            S.op("act", lambda e: e.copy(out=V[:, tg * 4:tg * 4 + 4, :], in_=p[:].rearrange("p (a c) -> p a c", a=4)),
                                 reads=[p], writes=[V])
                        if g == 0:
                            for c in range(4):
                                cs = slice(c * 512, (c + 1) * 512)
                                p = self.psum()
                                for fi in range(NFT):
                                    S.op("pe", lambda e: e.matmul(p[:], w[:, fi, 384:512], hTf[:, fi, cs],
                                                                  start=(fi == 0), stop=(fi == NFT - 1)),
                                         reads=[w, hTf], writes=[p], signal=(fi == NFT - 1))
                                S.op("act", lambda e: e.activation(out=sgate[:, cs], in_=p[:], func=AF.Silu), reads=[p], writes=[sgate])
                        qv = view(qT[:, :], g)
                        kv = view(kT[:, :], g)
                        av = view(acc[:, :], g)
                        lv = view(lacc[:, :], g)
                        for j in range(4):
                            pt = PT[j % 2]
                            pss = [self.psum(), self.psum()]
                            for bi in range(4):
                                r, n = bl[j * 4 + bi]
                                bank = pss[bi // 2]
                                off = (bi % 2) * 256
                                if n > 0:
                                    S.op("pe", lambda e: e.matmul(bank[:, off:off + 128], kv[:, r, n - 1, :], qv[:, r, n, :],
                                                                  start=True, stop=True),
                                         reads=[kT, qT], writes=[bank], signal=False)
                                S.op("pe", lambda e: e.matmul(bank[:, off + 128:off + 256], kv[:, r, n, :], qv[:, r, n, :],
                                                              start=True, stop=True),
                                     reads=[kT, qT], writes=[bank], signal=(bi % 2 == 1))
                            for h2 in range(2):
                                S.op("act", lambda e: e.activation(out=pt[:, h2 * 512:(h2 + 1) * 512], in_=pss[h2][:], func=AF.Exp, scale=SCALE),
                                     reads=[pss[h2]], writes=[pt])
                            mb = bass.AP(mask.t, 0, [[256, 128], [0, 4], [1, 256]])
                            S.op("pool", lambda e: e.tensor_tensor(out=pt[:].rearrange("p (a c) -> p a c", a=4),
                                                                   in0=pt[:].rearrange("p (a c) -> p a c", a=4), in1=mb, op=ALU.mult),
                                 reads=[pt, mask], writes=[pt])
                            po = self.psum()
                            pl = self.psum()
                            for (bank, isl) in ((po, False), (pl, True)):
                                for bi in range(4):
                                    r, n = bl[j * 4 + bi]
                                    bidx = j * 4 + bi
                                    osl = slice(bi * 128, (bi + 1) * 128)
                                    lh_cur = ones[:] if isl else V[:, bidx, :]
                                    S.op("pe", lambda e: e.matmul(bank[:, osl], lh_cur, pt[:, bi * 256 + 128:bi * 256 + 256],
                                                                  start=True, stop=(n == 0)),
                                         reads=[V, ones, pt], writes=[bank], signal=(n == 0 and bi == 3))
                                    if n > 0:
                                        lh_prev = ones[:] if isl else V[:, bidx - 1, :]
                                        S.op("pe", lambda e: e.matmul(bank[:, osl], lh_prev, pt[:, bi * 256:bi * 256 + 128],
                                                                      start=False, stop=True),
                                             reads=[V, ones, pt], writes=[bank], signal=(bi == 3))
                            if g == 0:
                                osel = av[:, 0, 4 * j:4 * j + 4, :]
                                lsel = lv[:, 0, 4 * j:4 * j + 4, :]
                            elif g == 1:
                                osel = av[:, j, :, :]
                                lsel = lv[:, j, :, :]
                            else:
                                osel = av[:, 4 * j:4 * j + 4, 0, :]
                                lsel = lv[:, 4 * j:4 * j + 4, 0, :]
                            pov = po[:].rearrange("p (a c) -> p a c", a=4)
                            plv = pl[:].rearrange("p (a c) -> p a c", a=4)
                            if g == 0:
                                S.op("act", lambda e: e.copy(out=osel, in_=pov), reads=[po], writes=[acc])
                                S.op("dve", lambda e: e.tensor_copy(out=lsel, in_=plv), reads=[pl], writes=[lacc])
                            else:
                                S.op("dve", lambda e: e.tensor_tensor(out=osel, in0=osel, in1=pov, op=ALU.add), reads=[po, acc], writes=[acc])
                                S.op("dve", lambda e: e.tensor_tensor(out=lsel, in0=lsel, in1=plv, op=ALU.add), reads=[pl, lacc], writes=[lacc])
                    S.op("dve", lambda e: e.reciprocal(out=lacc[:], in_=lacc[:]), reads=[lacc], writes=[lacc])
                    S.op("pool", lambda e: e.tensor_tensor(out=acc[:], in0=acc[:], in1=lacc[:], op=ALU.mult), reads=[acc, lacc], writes=[acc])
                    S.op("pool", lambda e: e.tensor_tensor(out=zst[:], in0=acc[:], in1=sgate[:], op=ALU.mult), reads=[acc, sgate], writes=[zst])
                    S.dma("sp", zscr[b].ap()[hd * 128:(hd + 1) * 128, :], zst[:], reads=[zst], writes=[zbuf[b][hd]])
            with self.scope():
                self.alloc_wsl()
                self.xt = [self.sb("xt%d" % i, [128, D], F32) for i in range(4)]
                zp = [self.sb("zp%d" % i, [128, 8, 512], BF16) for i in range(2)]
                for pn in range(4):
                    z = zp[pn % 2]
                    S.dma("sp", z[:], zscr[b].ap()[:, pn * 512:(pn + 1) * 512].rearrange("(a p) c -> p a c", p=128),
                          reads=zbuf[b], writes=[z])
                    self.out_proj(w_out, z, 8, src, dst, b * 16 + pn * 4, last, reload_x=True)

    def layer_rwkv(self, L, src, dst, last):
        S = self.S
        nc = self.nc
        ia = 0 if L == 0 else 1
        has_v = (L != 0)
        TR = 256
        NCH = TR // 64
        npan = self.n_panels * TP // TR
        ppb = SEQ // TR
        pre = "a%d_" % ia
        w_in = self.dram_in(pre + "w_in", [4 * D, D]).ap()
        w_out = self.dram_in(pre + "w_out", [D, D]).ap()
        mu_d = self.dram_in(pre + "mu_pm", [128, 6 * NFT]).ap()
        par_d = self.dram_in(pre + "par_pm", [128, 8 * NFT]).ap()
        w1_d = self.dram_in(pre + "w1", [D, 96]).ap()
        a1_d = self.dram_in(pre + "a1", [D, 96]).ap()
        w2_d = self.dram_in(pre + "w2", [96, D]).ap()
        a2_d = self.dram_in(pre + "a2", [96, D]).ap()
        if has_v:
            v1_d = self.dram_in(pre + "v1", [D, 64]).ap()
            v2_d = self.dram_in(pre + "v2", [64, D]).ap()
        cm_d = self.dram_in("rwkv_masks", [128, 512]).ap()
        if not hasattr(self, "vfs"):
            self.vfs_out = False
            if 0 in self.layers:
                self.vfs_out = 3 not in self.layers
                self.vfs = nc.dram_tensor("vfs", [D, NB_CORE * SEQ], F32, kind="ExternalOutput" if self.vfs_out else "Internal")
            else:
                self.vfs = self.dram_in("vfs", [D, NB_CORE * SEQ])
            self.vfs_b = {}
        wsc = nc.dram_tensor(pre + "wsc", [20, 128, NFT * 512], BF16, kind="Internal")
        wsc_b = [TT(None) for _ in range(20)]
        for sl in range(4):
            for j in range(4):
                idx = j * 4 + sl
                S.dma("pool", wsc.ap()[idx].rearrange("p (a c) -> p a c", a=NFT),
                      w_in[j * D:(j + 1) * D, sl * 512:(sl + 1) * 512].rearrange("(a p) c -> p a c", p=128), writes=[wsc_b[idx]])
        for sl in range(4):
            idx = 16 + sl
            S.dma("pool", wsc.ap()[idx].rearrange("p (a c) -> p a c", a=NFT),
                  w_out[:, sl * 512:(sl + 1) * 512].rearrange("(a p) c -> p a c", p=128), writes=[wsc_b[idx]])
        mu = self.sb("mu", [128, 6 * NFT], F32)
        S.dma("sp", mu[:], mu_d[:, :], writes=[mu])
        par = self.sb("par", [128, 8 * NFT], F32)
        S.dma("sp", par[:], par_d[:, :], writes=[par])
        parh = self.sb("parh", [128, 8 * NFT], F32)
        S.op("dve", lambda e: e.tensor_scalar(out=parh[:], in0=par[:], scalar1=0.5, scalar2=None, op0=ALU.mult), reads=[par], writes=[parh])
        omka = self.sb("omka", [128, NFT], F32)
        S.op("dve", lambda e: e.tensor_scalar(out=omka[:], in0=par[:, 3 * NFT:4 * NFT], scalar1=-1.0, scalar2=1.0, op0=ALU.mult, op1=ALU.add),
             reads=[par], writes=[omka])
        P_W0, P_A0, P_KK, P_KA, P_RK, P_LW, P_LB, P_V0 = range(8)

        def pcol(which, ft, t=par):
            return t[:, which * NFT + ft:which * NFT + ft + 1]
        wl = self.sb("wl", [128, NFT, 96], BF16)
        w2 = self.sb("w2", [96, D], BF16)
        S.dma("pool", w2[:], w2_d[:, :], writes=[w2])
        a2 = self.sb("a2", [96, D], BF16)
        S.dma("pool", a2[:], a2_d[:, :], writes=[a2])
        if has_v:
            v2 = self.sb("v2", [64, D], BF16)
            S.dma("pool", v2[:], v2_d[:, :], writes=[v2])
        cmf = self.sb("cmf", [128, 512], F32)
        S.dma("sp", cmf[:], cm_d[:, :], writes=[cmf])
        cmb = self.sb("cmb", [128, 4 * 64], BF16)
        S.op("dve", lambda e: e.tensor_copy(out=cmb[:], in_=cmf[:, 0:256]), reads=[cmf], writes=[cmb])
        id2b = cmb[:, 192:256]
        resetm = cmf[:, 256:256 + TR]
        bo = self.sb("bo", [128, 128], BF16)
        S.op("dve", lambda e: e.memset(bo[:], 0.0), writes=[bo])
        S.op("dve", lambda e: e.memset(bo[0:64, 0:64], 1.0), writes=[bo])
        S.op("dve", lambda e: e.memset(bo[64:128, 64:128], 1.0), writes=[bo])
        epsl = self.sb("epsl", [128, 1], F32)
        S.op("dve", lambda e: e.memset(epsl[:], 64e-5), writes=[epsl])
        self.alloc_norm(L, 2)
        hT = self.sb("hT", [128, NFT, TR], BF16)
        xz = self.sb("xz", [128, NFT, TR], BF16)
        carry = self.sb("carry", [128, NFT], BF16)
        tw = self.sb("tw", [96, TR], BF16)
        la = self.sb("la", [96, TR], BF16)
        lv = self.sb("lv", [64, TR], BF16)
        wsl = [self.sb("rwsl%d" % i, [128, NFT, 512], BF16) for i in range(2)]
        mixb = [self.sb("mixb%d" % i, [128, NFT, TR], BF16) for i in range(4)]
        mixl = self.sb("mixl", [128, NFT, TR], BF16)
        wslc = [0]
        Rs = self.sb("Rs", [128, 4, TR], F32)
        Ks = self.sb("Ks", [128, 4, TR], F32)
        Vs = self.sb("Vs", [128, 4, TR], F32)
        Gr = self.sb("Gr", [128, 4, TR], F32)
        Sf = [self.sb("Sf%d" % i, [128, 64], F32) for i in range(NFT)]
        Sb = [self.sb("Sb%d" % i, [128, 64], BF16) for i in range(NFT)]

        def f32t(name):
            return self.sb(name, [128, TR], F32)

        def bft(name):
            return self.sb(name, [128, TR], BF16)
        Wt, Winv, kk, nrm = [f32t(n) for n in ("Wt", "Winv", "kk", "nrm")]
        thw, tha, thv, thg = Wt, Winv, kk, nrm
        aa, sv, logw, cum, Wend, Wprev, tt, kp, bs, vf_t = [f32t(n) for n in ("aa", "sv", "logw", "cum", "Wend", "Wprev", "tt", "kp", "bs", "vf_t")]
        ksq, bT, kTl, bh, kh, vb = [bft(n) for n in ("ksq", "bT", "kTl", "bh", "kh", "vb")]
        AR = [self.sb("AR%d" % i, [128, 2, TR], BF16) for i in range(2)]
        TM = [self.sb("TM%d" % i, [128, 3, NCH, 64], BF16) for i in range(2)]
        MBs = [self.sb("MBs%d" % i, [128, NCH, 2, 64], BF16) for i in range(2)]
        MKs = [self.sb("MKs%d" % i, [128, NCH, 2, 64], BF16) for i in range(2)]
        TTt = [self.sb("TTt%d" % i, [128, NCH, 64], BF16) for i in range(2)]
        TTf = [self.sb("TTf%d" % i, [128, NCH, 64], F32) for i in range(2)]
        rkb = [self.sb("rkb%d" % i, [128, TR], BF16) for i in range(2)]
        vfl = [self.sb("vfl%d" % i, [128, TR], F32) for i in range(2)]
        Gs = [self.sb("Gs%d" % i, [128, TR], BF16) for i in range(2)]
        WC = [self.sb("WC%d" % i, [128, NCH], F32) for i in range(2)]
        Bp = [self.sb("Bp%d" % i, [128, NCH, 64], BF16) for i in range(6)]
        Ap = [self.sb("Ap%d" % i, [128, NCH, 64], BF16) for i in range(2)]
        Xs = self.sb("Xs", [128, 64], BF16)
        Us = self.sb("Us", [128, 64], BF16)
        yc, bon, lnv, ysb = [f32t(n) for n in ("yc", "bon", "lnv", "ysb")]
        ybf, sqb = [bft(n) for n in ("ybf", "sqb")]
        EXPM05 = float(np.exp(-0.5))
        PRS = (slice(0, 64), slice(64, 128))
        pYb = [self.ps[7], self.ps[6]]

        def gen_mix(j, m):
            for fi in range(NFT):
                S.op("dve", lambda e: e.scalar_tensor_tensor(out=m[:, fi, :], in0=xz[:, fi, :], scalar=mu[:, j * NFT + fi:j * NFT + fi + 1],
                                                           in1=hT[:, fi, :], op0=ALU.mult, op1=ALU.add),
                     reads=[xz, mu, hT], writes=[m])

        wlsc = nc.dram_tensor(pre + "wlsc", [3, 128, NFT * 96], BF16, kind="Internal")
        wlsc_b = [TT(None) for _ in range(3)]
        lora_srcs = [(w1_d, 96), (a1_d, 96)] + ([(v1_d, 64)] if has_v else [])
        for li, (wd_, ncol_) in enumerate(lora_srcs):
            S.dma("pool", wlsc.ap()[li].rearrange("p (a c) -> p a c", a=NFT)[:, :, 0:ncol_],
                  wd_.rearrange("(a p) c -> p a c", p=128), writes=[wlsc_b[li]])

        def lora_down(m, li, ncol, dst_t, func):
            S.dma("sp", wl[:].rearrange("p a c -> p (a c)"), wlsc.ap()[li], reads=[wlsc_b[li]], writes=[wl])
            p = self.psum()
            for fi in range(NFT):
                S.op("pe", lambda e: e.matmul(p[0:ncol, 0:TR], wl[:, fi, 0:ncol], m[:, fi, :], start=(fi == 0), stop=(fi == NFT - 1)),
                     reads=[wl, m], writes=[p], signal=(fi == NFT - 1))
            S.op("act", lambda e: e.activation(out=dst_t[:], in_=p[0:ncol, 0:TR], func=func), reads=[p], writes=[dst_t])

        def gemm_slab(sl):
            for j in range(4):
                m = mixb[j]
                w = wsl[wslc[0] % 2]
                wslc[0] += 1
                idx = j * 4 + sl
                S.dma("sp", w[:].rearrange("p a c -> p (a c)"), wsc.ap()[idx], reads=[wsc_b[idx]], writes=[w])
                dstt = (Rs, Ks, Vs, Gr)[j]
                for ftl in range(4):
                    p = self.psum()
                    for fi in range(NFT):
                        S.op("pe", lambda e: e.matmul(p[:, 0:TR], w[:, fi, ftl * 128:(ftl + 1) * 128], m[:, fi, :],
                                                      start=(fi == 0), stop=(fi == NFT - 1)),
                             reads=[w, m], writes=[p], signal=(fi == NFT - 1))
                    S.op("act", lambda e: e.copy(out=dstt[:, ftl, :], in_=p[:, 0:TR]), reads=[p], writes=[dstt])
                    yield

        def stage_a(pn, ft, slot):
            tok0 = pn * TR
            ftl = ft % 4
            r_ = Rs[:, ftl, :]
            k_ = Ks[:, ftl, :]
            v_ = Vs[:, ftl, :]
            g_ = Gr[:, ftl, :]
            fs = slice(ft * 128, (ft + 1) * 128)
            ar, tm, mbs, mks, ttt, ttf = AR[slot], TM[slot], MBs[slot], MKs[slot], TTt[slot], TTf[slot]
            p1 = self.psum()
            S.op("pe", lambda e: e.matmul(p1[:, 0:TR], w2[:, fs], tw[:], start=True, stop=True), reads=[w2, tw], writes=[p1])
            p2 = self.psum()
            S.op("pe", lambda e: e.matmul(p2[:, 0:TR], a2[:, fs], la[:], start=True, stop=True), reads=[a2, la], writes=[p2])
            if has_v:
                p3 = self.psum()
                S.op("pe", lambda e: e.matmul(p3[:, 0:TR], v2[:, fs], lv[:], start=True, stop=True), reads=[v2, lv], writes=[p3])
            S.op("act", lambda e: e.activation(out=thw[:], in_=p1[:, 0:TR], func=AF.Tanh, scale=0.5, bias=pcol(P_W0, ft, parh)),
                 reads=[p1, parh], writes=[thw])
            S.op("act", lambda e: e.activation(out=tha[:], in_=p2[:, 0:TR], func=AF.Tanh, scale=0.5, bias=pcol(P_A0, ft, parh)),
                 reads=[p2, parh], writes=[tha])
            if has_v:
                S.op("act", lambda e: e.activation(out=thv[:], in_=p3[:, 0:TR], func=AF.Tanh, scale=0.5, bias=pcol(P_V0, ft, parh)),
                     reads=[p3, parh], writes=[thv])
            S.op("act", lambda e: e.activation(out=thg[:], in_=g_, func=AF.Tanh, scale=0.5), reads=[Gr], writes=[thg])
            yield
            S.op("dve", lambda e: e.tensor_scalar(out=logw[:], in0=thw[:], scalar1=-0.5 * EXPM05, scalar2=-0.5 * EXPM05, op0=ALU.mult, op1=ALU.add),
                 reads=[thw], writes=[logw])
            S.op("dve", lambda e: e.tensor_scalar(out=aa[:], in0=tha[:], scalar1=0.5, scalar2=0.5, op0=ALU.mult, op1=ALU.add),
                 reads=[tha], writes=[aa])
            S.op("dve", lambda e: e.scalar_tensor_tensor(out=Gs[slot][:], in0=thg[:], scalar=1.0, in1=g_, op0=ALU.add, op1=ALU.mult),
                 reads=[thg, Gr], writes=[Gs[slot]])
            vkey = (ft, pn)
            if has_v:
                S.op("dve", lambda e: e.tensor_scalar(out=sv[:], in0=thv[:], scalar1=0.5, scalar2=0.5, op0=ALU.mult, op1=ALU.add),
                     reads=[thv], writes=[sv])
                rd = [self.vfs_b[vkey]] if vkey in self.vfs_b else []
                S.dma("sp", vf_t[:], self.vfs.ap()[fs, tok0:tok0 + TR], reads=rd, writes=[vf_t])
                S.op("pool", lambda e: e.tensor_tensor(out=vf_t[:], in0=vf_t[:], in1=v_, op=ALU.subtract), reads=[vf_t, Vs], writes=[vf_t])
                S.op("pool", lambda e: e.tensor_tensor(out=vf_t[:], in0=vf_t[:], in1=sv[:], op=ALU.mult), reads=[vf_t, sv], writes=[vf_t])
                S.op("dve", lambda e: e.tensor_tensor(out=vfl[slot][:], in0=v_, in1=vf_t[:], op=ALU.add), reads=[vf_t, Vs], writes=[vfl[slot]])
            else:
                if vkey not in self.vfs_b:
                    self.vfs_b[vkey] = TT(None)
                S.dma("sp", self.vfs.ap()[fs, tok0:tok0 + TR], v_, reads=[Vs], writes=[self.vfs_b[vkey]], is_output=self.vfs_out)
                S.op("pool", lambda e: e.tensor_copy(out=vfl[slot][:], in_=v_), reads=[Vs], writes=[vfl[slot]])
            vv = vfl[slot]
            S.op("dve", lambda e: e.tensor_tensor_scan(out=cum[:], data0=resetm, data1=logw[:], initial=0.0, op0=ALU.mult, op1=ALU.add),
                 reads=[logw, cmf], writes=[cum])
            cumC_bc = bass.AP(cum.t, 63, [[TR, 128], [64, NCH], [0, 64]])
            cumC = bass.AP(cum.t, 63, [[TR, 128], [64, NCH]])
            c3 = cum[:].rearrange("p (a c) -> p a c", a=NCH)
            S.op("pool", lambda e: e.tensor_tensor(out=Wend[:].rearrange("p (a c) -> p a c", a=NCH), in0=cumC_bc, in1=c3, op=ALU.subtract),
                 reads=[cum], writes=[Wend])
            S.op("dve", lambda e: e.tensor_tensor(out=Wprev[:], in0=cum[:], in1=logw[:], op=ALU.subtract), reads=[cum, logw], writes=[Wprev])
            S.op("dve", lambda e: e.tensor_scalar(out=kk[:], in0=k_, scalar1=pcol(P_KK, ft), scalar2=None, op0=ALU.mult),
                 reads=[Ks, par], writes=[kk])
            S.op("pool", lambda e: e.tensor_tensor(out=ksq[:], in0=kk[:], in1=kk[:], op=ALU.mult), reads=[kk], writes=[ksq])
            pk = self.psum()
            S.op("pe", lambda e: e.matmul(pk[:, 0:TR], bo[:], ksq[:], start=True, stop=True), reads=[bo, ksq], writes=[pk])
            S.op("act", lambda e: e.activation(out=Wt[:], in_=cum[:], func=AF.Exp), reads=[cum], writes=[Wt])
            S.op("act", lambda e: e.activation(out=Winv[:], in_=cum[:], func=AF.Exp, scale=-1.0), reads=[cum], writes=[Winv])
            S.op("act", lambda e: e.activation(out=WC[slot][:], in_=cumC, func=AF.Exp), reads=[cum], writes=[WC[slot]])
            S.op("act", lambda e: e.activation(out=Wprev[:], in_=Wprev[:], func=AF.Exp), reads=[Wprev], writes=[Wprev])
            S.op("act", lambda e: e.activation(out=Wend[:], in_=Wend[:], func=AF.Exp), reads=[Wend], writes=[Wend])
            S.op("dve", lambda e: e.tensor_scalar(out=nrm[:], in0=pk[:, 0:TR], scalar1=1e-24, scalar2=None, op0=ALU.max), reads=[pk], writes=[nrm])
            S.op("act", lambda e: e.activation(out=nrm[:], in_=nrm[:], func=AF.Ln), reads=[nrm], writes=[nrm])
            S.op("act", lambda e: e.activation(out=nrm[:], in_=nrm[:], func=AF.Exp, scale=-0.5), reads=[nrm], writes=[nrm])
            S.op("act", lambda e: e.copy(out=vb[:], in_=vv[:]), reads=[vv], writes=[vb])
            S.op("dve", lambda e: e.tensor_tensor(out=kk[:], in0=kk[:], in1=nrm[:], op=ALU.mult), reads=[kk, nrm], writes=[kk])
            S.op("dve", lambda e: e.tensor_scalar(out=tt[:], in0=aa[:], scalar1=pcol(P_KA, ft), scalar2=omka[:, ft:ft + 1],
                                                  op0=ALU.mult, op1=ALU.add), reads=[aa, par, omka], writes=[tt])
            S.op("dve", lambda e: e.tensor_tensor(out=kp[:], in0=k_, in1=tt[:], op=ALU.mult), reads=[Ks, tt], writes=[kp])
            S.op("pool", lambda e: e.tensor_tensor(out=bs[:], in0=kk[:], in1=aa[:], op=ALU.mult), reads=[kk, aa], writes=[bs])
            S.op("dve", lambda e: e.scalar_tensor_tensor(out=ar[:, 0, :], in0=kk[:], scalar=-1.0, in1=Wprev[:], op0=ALU.mult, op1=ALU.mult),
                 reads=[kk, Wprev], writes=[ar])
            S.op("pool", lambda e: e.tensor_tensor(out=ar[:, 1, :], in0=r_, in1=Wt[:], op=ALU.mult), reads=[Rs, Wt], writes=[ar])
            S.op("dve", lambda e: e.tensor_tensor(out=bT[:], in0=bs[:], in1=Winv[:], op=ALU.mult), reads=[bs, Winv], writes=[bT])
            S.op("dve", lambda e: e.tensor_tensor(out=kTl[:], in0=kp[:], in1=Winv[:], op=ALU.mult), reads=[kp, Winv], writes=[kTl])
            S.op("dve", lambda e: e.tensor_tensor(out=bh[:], in0=bs[:], in1=Wend[:], op=ALU.mult), reads=[bs, Wend], writes=[bh])
            S.op("pool", lambda e: e.tensor_tensor(out=kh[:], in0=kp[:], in1=Wend[:], op=ALU.mult), reads=[kp, Wend], writes=[kh])
            S.op("dve", lambda e: e.scalar_tensor_tensor(out=rkb[slot][:], in0=r_, scalar=pcol(P_RK, ft), in1=kp[:], op0=ALU.mult, op1=ALU.mult),
                 reads=[Rs, par, kp], writes=[rkb[slot]])
            pT = self.psum()
            pTb = pT.t[:].bitcast(BF16)
            for wi, srct in enumerate((vb, bh, kh)):
                for c in range(NCH):
                    for hh in range(2):
                        pr = PRS[hh]
                        o0 = (wi * NCH + c) * 64
                        S.op("pe", lambda e: e.transpose(pTb[pr, o0:o0 + 64], srct[pr, c * 64:(c + 1) * 64], id2b[pr, :]),
                             reads=[srct, cmb], writes=[pT], signal=(wi == 2 and c == NCH - 1 and hh == 1))
            S.op("act", lambda e: e.copy(out=tm[:].rearrange("p a b c -> p (a b c)"), in_=pTb[:, 0:3 * NCH * 64]), reads=[pT], writes=[tm])
            yield
            pMB = self.psum()
            pMK = self.psum()
            pN = self.psum()
            for c in range(NCH):
                cs = slice(c * 64, (c + 1) * 64)
                for hh in range(2):
                    pr = PRS[hh]
                    lastm = (c == NCH - 1 and hh == 1)
                    S.op("pe", lambda e: e.matmul(pMB[pr, c * 128:(c + 1) * 128].rearrange("p (a b) -> p a b", a=2), bT[pr, cs], ar[pr, :, cs],
                                                  start=True, stop=True), reads=[bT, ar], writes=[pMB], signal=lastm)
                    S.op("pe", lambda e: e.matmul(pMK[pr, c * 128:(c + 1) * 128].rearrange("p (a b) -> p a b", a=2), kTl[pr, cs], ar[pr, :, cs],
                                                  start=True, stop=True), reads=[kTl, ar], writes=[pMK], signal=lastm)
                    S.op("pe", lambda e: e.matmul(pN[pr, c * 64:(c + 1) * 64], ar[pr, 0, cs], bT[pr, cs],
                                                  start=True, stop=True), reads=[bT, ar], writes=[pN], signal=lastm)
            mk2 = bass.AP(cmf.t, 0, [[512, 128], [0, NCH], [1, 128]])
            mgt = bass.AP(cmf.t, 128, [[512, 128], [0, NCH], [1, 64]])
            idb = bass.AP(cmf.t, 192, [[512, 128], [0, NCH], [1, 64]])
            S.op("dve", lambda e: e.tensor_tensor(out=mbs[:].rearrange("p c a b -> p c (a b)"),
                                                  in0=pMB[:, 0:NCH * 128].rearrange("p (c x) -> p c x", c=NCH), in1=mk2, op=ALU.mult),
                 reads=[pMB, cmf], writes=[mbs])
            S.op("dve", lambda e: e.tensor_tensor(out=Bp[0][:], in0=pN[:, 0:NCH * 64].rearrange("p (c x) -> p c x", c=NCH), in1=mgt, op=ALU.mult),
                 reads=[pN, cmf], writes=[Bp[0]])
            S.op("dve", lambda e: e.tensor_tensor(out=mks[:].rearrange("p c a b -> p c (a b)"),
                                                  in0=pMK[:, 0:NCH * 128].rearrange("p (c x) -> p c x", c=NCH), in1=mk2, op=ALU.mult),
                 reads=[pMK, cmf], writes=[mks])
            S.op("dve", lambda e: e.tensor_tensor(out=ttf[:], in0=mbs[:, :, 0, :], in1=idb, op=ALU.add), reads=[mbs, cmf], writes=[ttf])
            S.op("act", lambda e: e.copy(out=ttt[:], in_=ttf[:]), reads=[ttf], writes=[ttt])
            yield

            def a_pow(l):
                if l == 0:
                    return mbs, (lambda c, pr: mbs[pr, c, 0, :])
                t = Ap[(l - 1) % 2]
                return t, (lambda c, pr: t[pr, c, :])

            def sq(l):
                At, Aap = a_pow(l - 1)
                Bo = Bp[l - 1]
                pB = self.psum()
                pA = self.psum() if l < 5 else None
                for c in range(NCH):
                    for hh in range(2):
                        pr = PRS[hh]
                        lastm = (c == NCH - 1 and hh == 1)
                        S.op("pe", lambda e: e.matmul(pB[pr, c * 64:(c + 1) * 64], Aap(c, pr), Bo[pr, c, :], start=True, stop=True),
                             reads=[At, Bo], writes=[pB], signal=lastm)
                        if pA is not None:
                            S.op("pe", lambda e: e.matmul(pA[pr, c * 64:(c + 1) * 64], Bo[pr, c, :], Aap(c, pr), start=True, stop=True),
                                 reads=[At, Bo], writes=[pA], signal=lastm)
                S.op("act", lambda e: e.copy(out=Bp[l][:].rearrange("p c x -> p (c x)"), in_=pB[:, 0:NCH * 64]), reads=[pB], writes=[Bp[l]])
                if pA is not None:
                    An = Ap[(l - 1) % 2]
                    S.op("dve", lambda e: e.tensor_copy(out=An[:].rearrange("p c x -> p (c x)"), in_=pA[:, 0:NCH * 64]), reads=[pA], writes=[An])

            def tup(l):
                pTu = self.psum()
                Bn = Bp[l]
                for c in range(NCH):
                    for hh in range(2):
                        pr = PRS[hh]
                        lastm = (c == NCH - 1 and hh == 1)
                        S.op("pe", lambda e: e.matmul(pTu[pr, c * 64:(c + 1) * 64], Bn[pr, c, :], ttt[pr, c, :], start=True, stop=True),
                             reads=[Bn, ttt], writes=[pTu], signal=lastm)
                S.op("dve", lambda e: e.tensor_tensor(out=ttf[:].rearrange("p c x -> p (c x)"), in0=ttf[:].rearrange("p c x -> p (c x)"),
                                                      in1=pTu[:, 0:NCH * 64], op=ALU.add), reads=[pTu, ttf], writes=[ttf])
                S.op("act", lambda e: e.copy(out=ttt[:], in_=ttf[:]), reads=[ttf], writes=[ttt])
            for step in (("s", 1), ("s", 2), ("t", 1), ("s", 3), ("t", 2), ("s", 4), ("t", 3), ("s", 5), ("t", 4), ("t", 5)):
                if step[0] == "s":
                    sq(step[1])
                else:
                    tup(step[1])
                yield

        def stage_b(pn, ft, slot):
            ar, tm, mbs, mks, ttt = AR[slot], TM[slot], MBs[slot], MKs[slot], TTt[slot]
            pY = pYb[slot]
            for c in range(NCH):
                cs = slice(c * 64, (c + 1) * 64)
                pX = self.psum()
                for hh in range(2):
                    pr = PRS[hh]
                    S.op("pe", lambda e: e.matmul(pX[pr, 0:64], ar[pr, 0, cs], Sb[ft][pr, :], start=True, stop=False),
                         reads=[ar, Sb[ft]], writes=[pX], signal=False)
                    S.op("pe", lambda e: e.matmul(pX[pr, 0:64], mks[pr, c, 0, :], tm[pr, 0, c, :], start=False, stop=True),
                         reads=[mks, tm], writes=[pX], signal=(hh == 1))
                S.op("act", lambda e: e.copy(out=Xs[:], in_=pX[:, 0:64]), reads=[pX], writes=[Xs])
                yield
                pU = self.psum()
                for hh in range(2):
                    pr = PRS[hh]
                    S.op("pe", lambda e: e.matmul(pU[pr, 0:64], ttt[pr, c, :], Xs[pr, :], start=True, stop=True),
                         reads=[ttt, Xs], writes=[pU], signal=(hh == 1))
                S.op("dve", lambda e: e.tensor_copy(out=Us[:], in_=pU[:, 0:64]), reads=[pU], writes=[Us])
                yield
                for hh in range(2):
                    pr = PRS[hh]
                    S.op("pe", lambda e: e.matmul(pY[pr, cs], Sb[ft][pr, :], ar[pr, 1, cs], start=True, stop=False),
                         reads=[Sb[ft], ar], writes=[pY], signal=False)
                    S.op("pe", lambda e: e.matmul(pY[pr, cs], Us[pr, :], mbs[pr, c, 1, :], start=False, stop=False),
                         reads=[Us, mbs], writes=[pY], signal=False)
                    S.op("pe", lambda e: e.matmul(pY[pr, cs], tm[pr, 0, c, :], mks[pr, c, 1, :], start=False, stop=True),
                         reads=[tm, mks], writes=[pY], signal=(hh == 1))
                pS = self.psum()
                for hh in range(2):
                    pr = PRS[hh]
                    S.op("pe", lambda e: e.matmul(pS[pr, 0:64], tm[pr, 1, c, :], Us[pr, :], start=True, stop=False),
                         reads=[tm, Us], writes=[pS], signal=False)
                    S.op("pe", lambda e: e.matmul(pS[pr, 0:64], tm[pr, 2, c, :], tm[pr, 0, c, :], start=False, stop=True),
                         reads=[tm], writes=[pS], signal=(hh == 1))
                S.op("dve", lambda e: e.scalar_tensor_tensor(out=Sf[ft][:], in0=Sf[ft][:], scalar=WC[slot][:, c:c + 1], in1=pS[:, 0:64],
                                                             op0=ALU.mult, op1=ALU.add), reads=[Sf[ft], WC[slot], pS], writes=[Sf[ft]])
                S.op("act", lambda e: e.copy(out=Sb[ft][:], in_=Sf[ft][:]), reads=[Sf[ft]], writes=[Sb[ft]])
                yield
            S.op("act", lambda e: e.copy(out=ybf[:], in_=pY[:, 0:TR]), reads=[pY], writes=[ybf])
            S.op("dve", lambda e: e.tensor_copy(out=ysb[:], in_=pY[:, 0:TR]), reads=[pY], writes=[ysb])
            p = self.psum()
            S.op("pe", lambda e: e.matmul(p[:, 0:TR], bo[:], ybf[:], start=True, stop=True), reads=[bo, ybf], writes=[p])
            S.op("pool", lambda e: e.tensor_copy(out=yc[:], in_=p[:, 0:TR]) if False else e.tensor_copy(out=bon[:], in_=vfl[slot][:]),
                 reads=[vfl[slot]], writes=[bon])
            S.op("dve", lambda e: e.scalar_tensor_tensor(out=yc[:], in0=p[:, 0:TR], scalar=-1.0 / 64, in1=ysb[:], op0=ALU.mult, op1=ALU.add),
                 reads=[p, ysb], writes=[yc])
            S.op("act", lambda e: e.activation(out=sqb[:], in_=yc[:], func=AF.Square), reads=[yc], writes=[sqb])
            yield
            p2 = self.psum()
            S.op("pe", lambda e: e.matmul(p2[:, 0:TR], bo[:], sqb[:], start=True, stop=True), reads=[bo, sqb], writes=[p2])
            pb = self.psum()
            S.op("pe", lambda e: e.matmul(pb[:, 0:TR], bo[:], rkb[slot][:], start=True, stop=True), reads=[bo, rkb[slot]], writes=[pb])
            S.op("dve", lambda e: e.tensor_tensor(out=bon[:], in0=pb[:, 0:TR], in1=bon[:], op=ALU.mult), reads=[pb, bon], writes=[bon])
            S.op("act", lambda e: e.activation(out=lnv[:], in_=p2[:, 0:TR], func=AF.Ln, scale=1.0 / 64, bias=epsl[:, 0:1]),
                 reads=[p2, epsl], writes=[lnv])
            S.op("act", lambda e: e.activation(out=lnv[:], in_=lnv[:], func=AF.Exp, scale=-0.5), reads=[lnv], writes=[lnv])
            S.op("dve", lambda e: e.tensor_tensor(out=yc[:], in0=yc[:], in1=lnv[:], op=ALU.mult), reads=[yc, lnv], writes=[yc])
            S.op("dve", lambda e: e.tensor_scalar(out=yc[:], in0=yc[:], scalar1=pcol(P_LW, ft), scalar2=pcol(P_LB, ft),
                                                  op0=ALU.mult, op1=ALU.add), reads=[yc, par], writes=[yc])
            S.op("pool", lambda e: e.tensor_tensor(out=yc[:], in0=yc[:], in1=bon[:], op=ALU.add), reads=[yc, bon], writes=[yc])
            S.op("dve", lambda e: e.scalar_tensor_tensor(out=xz[:, ft, :], in0=yc[:], scalar=0.5, in1=Gs[slot][:], op0=ALU.mult, op1=ALU.mult),
                 reads=[yc, Gs[slot]], writes=[xz])
            yield

        def weave(*gens):
            gens = [g for g in gens if g is not None]
            while gens:
                for g in list(gens):
                    try:
                        next(g)
                    except StopIteration:
                        gens.remove(g)

        for pn in range(npan):
            first = (pn % ppb == 0)
            self.norm_tiles(src, pn * 2, 2, hT)
            if first:
                S.op("dve", lambda e: e.memset(carry[:], 0.0), writes=[carry])
                for ft in range(NFT):
                    S.op("pool", lambda e: e.memset(Sf[ft][:], 0.0), writes=[Sf[ft]])
                    S.op("pool", lambda e: e.memset(Sb[ft][:], 0.0), writes=[Sb[ft]])
            S.op("dve", lambda e: e.tensor_tensor(out=xz[:, :, 1:TR], in0=hT[:, :, 0:TR - 1], in1=hT[:, :, 1:TR], op=ALU.subtract),
                 reads=[hT], writes=[xz])
            S.op("dve", lambda e: e.tensor_tensor(out=xz[:, :, 0], in0=carry[:], in1=hT[:, :, 0], op=ALU.subtract),
                 reads=[hT, carry], writes=[xz])
            S.op("dve", lambda e: e.tensor_copy(out=carry[:], in_=hT[:, :, TR - 1]), reads=[hT], writes=[carry])
            gen_mix(4, mixl)
            lora_down(mixl, 0, 96, tw, AF.Tanh)
            gen_mix(0, mixb[0])
            gen_mix(5, mixl)
            lora_down(mixl, 1, 96, la, AF.Copy)
            for j in range(1, 4):
                gen_mix(j, mixb[j])
            if has_v:
                lora_down(mixb[2], 2, 64, lv, AF.Copy)
            prevB = None
            for sl in range(4):
                weave(gemm_slab(sl), prevB)
                prevB = None
                for ftl in range(4):
                    ft = sl * 4 + ftl
                    weave(stage_a(pn, ft, ft % 2), prevB)
                    prevB = stage_b(pn, ft, ft % 2)
            weave(prevB)
            for i in range(2):
                tile = pn * 2 + i
                S.dma("sp", self.xt[i][:], src.ap()[tile * 128:(tile + 1) * 128, :], reads=[self.xbuf(src, tile)], writes=[self.xt[i]])
            for c in range(4):
                w = wsl[wslc[0] % 2]
                wslc[0] += 1
                idx = 16 + c
                S.dma("sp", w[:].rearrange("p a c -> p (a c)"), wsc.ap()[idx], reads=[wsc_b[idx]], writes=[w])
                for i in range(2):
                    p = self.psum()
                    for fi in range(NFT):
                        S.op("pe", lambda e: e.matmul(p[:], xz[:, fi, i * 128:(i + 1) * 128], w[:, fi, :],
                                                      start=(fi == 0), stop=(fi == NFT - 1)),
                             reads=[xz, w], writes=[p], signal=(fi == NFT - 1))
                    xt = self.xt[i]
                    S.op("dve", lambda e: e.tensor_tensor(out=xt[:, c * 512:(c + 1) * 512], in0=xt[:, c * 512:(c + 1) * 512],
                                                          in1=p[:], op=ALU.add), reads=[xt, p], writes=[xt])
            for i in range(2):
                tile = pn * 2 + i
                S.dma("sp", dst.ap()[tile * 128:(tile + 1) * 128, :], self.xt[i][:], reads=[self.xt[i]],
                      writes=[self.xbuf(dst, tile)], is_output=last)


def pm(v):
    return np.ascontiguousarray(np.asarray(v, np.float32).reshape(NFT, 128).T)


def host_consts():
    c = {}
    c["ident"] = np.eye(128, dtype=np.float32)
    rc = np.zeros((4, 16), np.float32)
    for g in range(4):
        win = 2 << g
        for t in range(16):
            rc[g, t] = 1.0 / min(t + 1, win)
    invf = (500000.0 ** (-np.arange(0, 32, 2, dtype=np.float32) / 32)).astype(np.float32)
    c["rope_invf"] = np.concatenate([invf, invf]).reshape(32, 1).astype(np.float32)
    rt = np.zeros((32, 32), np.float32)
    for i in range(16):
        rt[i + 16, i] = -1.0
        rt[i, i + 16] = 1.0
    c["rope_rot"] = rt
    kk = np.arange(128)[:, None]
    qq = np.arange(128)[None, :]
    c["attn_mask"] = np.concatenate([(qq <= kk), (kk <= qq)], axis=1).astype(np.float32)
    s_ = np.arange(64)[:, None]
    t_ = np.arange(64)[None, :]
    lt = (s_ < t_).astype(np.float32)
    le = (s_ <= t_).astype(np.float32)
    gt = (s_ > t_).astype(np.float32)
    eye = np.eye(64, dtype=np.float32)
    rm = np.ones((64, 256), np.float32)
    rm[:, ::64] = 0.0
    blk = np.concatenate([lt, le, gt, eye, rm], axis=1)
    c["rwkv_masks"] = np.ascontiguousarray(np.concatenate([blk, blk], axis=0))
    c["pool_rc"] = np.ascontiguousarray(np.broadcast_to(rc.reshape(1, 64), (128, 64)))
    return c


def shared_inputs(inp, layers):
    m = dict(host_consts())
    m["norm_w"] = np.ascontiguousarray(inp["norm_w"], dtype=np.float32)
    for L, ia in ((0, 0), (3, 1)):
        if L not in layers:
            continue
        pre = "a%d_" % ia
        m[pre + "w_in"] = np.ascontiguousarray(np.asarray(inp["a_w_in"][ia]).reshape(4 * D, D))
        m[pre + "w_out"] = np.ascontiguousarray(inp["a_w_out"][ia])
        m[pre + "mu_pm"] = np.ascontiguousarray(np.concatenate([pm(inp["a_mu"][ia][j]) for j in range(6)], axis=1))
        plist = [inp["a_w0"][ia], inp["a_a0"][ia], inp["a_k_k"][ia], inp["a_k_a"][ia], np.asarray(inp["a_r_k"][ia]).reshape(-1),
                 inp["a_lnx_w"][ia], inp["a_lnx_b"][ia], inp["a_v0"][ia - 1] if ia > 0 else np.zeros(D, np.float32)]
        m[pre + "par_pm"] = np.ascontiguousarray(np.concatenate([pm(v) for v in plist], axis=1))
        m[pre + "w1"] = np.ascontiguousarray(inp["a_w1"][ia])
        m[pre + "a1"] = np.ascontiguousarray(inp["a_a1"][ia])
        m[pre + "w2"] = np.ascontiguousarray(inp["a_w2"][ia])
        m[pre + "a2"] = np.ascontiguousarray(inp["a_a2"][ia])
        if ia > 0:
            m[pre + "v1"] = np.ascontiguousarray(inp["a_v1"][ia - 1])
            m[pre + "v2"] = np.ascontiguousarray(inp["a_v2"][ia - 1])
    if 1 in layers:
        w = np.asarray(inp["b_w_in"][0])
        cols = []
        for hd in range(8):
            order = [(0, 0), (1, 0), (2, 0), None, (0, 1), (1, 1), (2, 1), (0, 2), (1, 2), (2, 2)]
            for o in order:
                if o is None:
                    c0 = 9216 + hd * 128
                else:
                    sidx, g = o
                    c0 = ((sidx * 3 + g) * 8 + hd) * 128
                cols.append(np.arange(c0, c0 + 128))
        cols = np.concatenate(cols)
        m["b_w_in_r"] = np.ascontiguousarray(w[:, cols])
        m["b_w_out"] = np.ascontiguousarray(inp["b_w_out"][0])
        m["b_qkn_pm"] = np.ascontiguousarray(np.concatenate([np.asarray(inp["b_qn_w"][0]).T, np.asarray(inp["b_kn_w"][0]).T], axis=1).astype(np.float32))
    if 2 in layers:
        m["c_w_in"] = np.ascontiguousarray(inp["c_w_in"][0])
        m["c_w_grp"] = np.ascontiguousarray(inp["c_w_grp"][0].reshape(4 * 512, 512))
        m["c_w_out"] = np.ascontiguousarray(inp["c_w_out"][0])
        m["c_scale_pm"] = pm(inp["c_scale"][0])
    return m


_CACHE = {}
FUSED = True


def _prog(layers):
    if layers not in _CACHE:
        pr = Prog(layers)
        pr.build()
        _CACHE[layers] = pr
    return _CACHE[layers]


def kernel(**inp):
    all_layers = (0, 1, 2, 3)
    groups = [all_layers] if FUSED else [(0,), (1,), (2,), (3,)]
    x = np.asarray(inp["x"], np.float32)
    xs = [np.ascontiguousarray(x[c * NB_CORE:(c + 1) * NB_CORE].reshape(NB_CORE * SEQ, D)) for c in range(8)]
    pos = np.asarray(inp["positions"])
    vfs = None
    for layers in groups:
        pr = _prog(layers)
        shared = shared_inputs(inp, layers)
        in_maps = []
        for c in range(8):
            m = dict(shared)
            m["x"] = xs[c]
            m["positions"] = np.ascontiguousarray(pos[c * NB_CORE:(c + 1) * NB_CORE].astype(np.int32))
            if vfs is not None:
                m["vfs"] = vfs[c]
            m = {k: v for k, v in m.items() if k in pr.inputs}
            in_maps.append(m)
        res = run_bass_kernel_spmd(pr.nc, in_maps, core_ids=list(range(8)))
        xs = [np.asarray(res.results[c]["out"]) for c in range(8)]
        if "vfs" in res.results[0]:
            vfs = [np.asarray(res.results[c]["vfs"]) for c in range(8)]
    out = np.stack([xs[c].reshape(NB_CORE, SEQ, D) for c in range(8)], axis=0)
    return out.reshape(16, SEQ, D).astype(np.float32)
```

```python
import itertools
import numpy as np
from contextlib import ExitStack
import concourse.bass as bass
import concourse.mybir as mybir
from concourse.bass_utils import run_bass_kernel_spmd

F32 = mybir.dt.float32
BF16 = mybir.dt.bfloat16
I32 = mybir.dt.int32
AF = mybir.ActivationFunctionType
ALU = mybir.AluOpType
AX = mybir.AxisListType

D = 2048
NFT = 16
SEQ = 2048
NB_CORE = 2
TP = 512
NPB = SEQ // TP
RMS_EPS = 1e-6
NDMA = 24
NDMA_SW = 8
SAME_ENGINE_SYNC = True


class Buf:
    __slots__ = ("w", "r")

    def __init__(self):
        self.w = None
        self.r = {}


class TT:
    def __init__(self, t):
        self.t = t
        self.b = Buf()

    def __getitem__(self, k):
        return self.t[k]


class Sched:
    def __init__(self, nc, es):
        self.nc = nc
        self.eng = {"pe": nc.tensor, "act": nc.scalar, "dve": nc.vector, "pool": nc.gpsimd, "sp": nc.sync}
        self.sem = {}
        self.cnt = {}
        for e in ("pe", "act", "dve", "pool"):
            self.sem[e] = es.enter_context(nc.semaphore("s_" + e))
            self.cnt[e] = 0
        for i in range(NDMA):
            self.sem[("dma", i)] = es.enter_context(nc.semaphore("s_dma%d" % i))
            self.cnt[("dma", i)] = 0
        self.seen = {e: {} for e in self.eng}
        self.rr = 0
        self.rr_sw = 0
        self.out_tokens = []
        self.nwait = 0

    def wait(self, e, tok):
        if tok is None:
            return
        key, val = tok
        if key == e and (e == "pe" or not SAME_ENGINE_SYNC):
            return
        if self.seen[e].get(key, 0) >= val:
            return
        self.eng[e].wait_ge(self.sem[key], val)
        self.seen[e][key] = val
        self.nwait += 1

    def _deps(self, e, reads, writes):
        for b in reads:
            self.wait(e, b.b.w)
        for b in writes:
            self.wait(e, b.b.w)
            for k, v in b.b.r.items():
                self.wait(e, (k, v))

    def _commit(self, tok, reads, writes):
        k, v = tok
        for b in reads:
            if b.b.r.get(k, 0) < v:
                b.b.r[k] = v
        for b in writes:
            b.b.w = tok
            b.b.r = {}

    def op(self, e, fn, reads=(), writes=(), signal=True):
        self._deps(e, reads, writes)
        ins = fn(self.eng[e])
        if signal:
            self.cnt[e] += 1
            ins.then_inc(self.sem[e], 1)
            tok = (e, self.cnt[e])
        else:
            tok = (e, self.cnt[e] + 1)
        self._commit(tok, reads, writes)
        return tok

    def dma(self, q, out, in_, reads=(), writes=(), is_output=False):
        self._deps(q, reads, writes)
        if q == "pool":
            i = self.rr_sw
            self.rr_sw = (self.rr_sw + 1) % NDMA_SW
        else:
            i = NDMA_SW + self.rr
            self.rr = (self.rr + 1) % (NDMA - NDMA_SW)
        key = ("dma", i)
        if self.cnt[key] > 0:
            self.wait(q, (key, self.cnt[key]))
        self.cnt[key] += 16
        self.eng[q].dma_start(out=out, in_=in_).then_inc(self.sem[key], 16)
        tok = (key, self.cnt[key])
        self._commit(tok, reads, writes)
        if is_output:
            self.out_tokens.append(tok)
        return tok

    def barrier(self):
        for e in ("pe", "act", "dve", "pool", "sp"):
            for k in list(self.cnt.keys()):
                if self.cnt[k] > 0 and k != e:
                    self.wait(e, (k, self.cnt[k]))

    def finish(self):
        for tok in self.out_tokens:
            self.wait("sp", tok)
        for e in ("pe", "act", "dve", "pool"):
            if self.cnt[e] > 0:
                self.wait("sp", (e, self.cnt[e]))


def bcast_rows(ap_row, nparts):
    t = ap_row.tensor
    pairs = list(ap_row.ap)
    return bass.AP(t, ap_row.offset, [[0, nparts]] + [list(p) for p in pairs[1:]])


class Prog:
    def __init__(self, layers=(0, 1, 2, 3), n_panels=NB_CORE * NPB):
        self.layers = layers
        self.n_panels = n_panels
        self.nc = bass.Bass("TRN2", target_bir_lowering=False)
        self.es = ExitStack()
        self.inputs = {}
        self.uid = 0
        self.scopes = []

    def dram_in(self, name, shape, dtype=F32):
        if name in self.inputs:
            return self.inputs[name]
        t = self.nc.dram_tensor(name, list(shape), dtype, kind="ExternalInput")
        self.inputs[name] = t
        return t

    def sb(self, name, shape, dtype):
        self.uid += 1
        st = self.scopes[-1] if self.scopes else self.es
        return TT(st.enter_context(self.nc.sbuf_tensor("sb%d_%s" % (self.uid, name), list(shape), dtype)))

    def scope(self):
        prog = self

        class _Scope:
            def __enter__(self_):
                self_.st = ExitStack()
                prog.scopes.append(self_.st)
                return self_

            def __exit__(self_, *a):
                prog.S.barrier()
                prog.scopes.pop()
                self_.st.close()
                return False
        return _Scope()

    def build(self):
        nc = self.nc
        es = self.es
        S = self.S = Sched(nc, es)
        x_in = self.dram_in("x", [NB_CORE * SEQ, D])
        self.ident_d = self.dram_in("ident", [128, 128])
        self.normw_d = self.dram_in("norm_w", [4, D])
        out_d = nc.dram_tensor("out", [NB_CORE * SEQ, D], F32, kind="ExternalOutput")
        nl = len(self.layers)
        scr = [nc.dram_tensor("xs%d" % i, [NB_CORE * SEQ, D], F32, kind="Internal") for i in range(2)]
        chain = [x_in]
        for i in range(nl - 1):
            chain.append(scr[i % 2])
        chain.append(out_d)
        self.xbufs = {}

        def xbuf(t, tile):
            k = (t.name, tile)
            if k not in self.xbufs:
                self.xbufs[k] = TT(None)
            return self.xbufs[k]
        self.xbuf = xbuf

        self.ident = self.sb("ident", [128, 128], BF16)
        S.dma("pool", self.ident[:], self.ident_d.ap()[:, :], writes=[self.ident])
        self.ps = [TT(es.enter_context(nc.psum_tensor("ps%d" % i, [128, 512], F32))) for i in range(8)]
        self.ps_i = 0

        for li, L in enumerate(self.layers):
            src, dst = chain[li], chain[li + 1]
            last = li == nl - 1
            with self.scope():
                if L == 2:
                    self.layer_pool(src, dst, last)
                elif L == 1:
                    self.layer_attn(src, dst, last)
                else:
                    self.layer_rwkv(L, src, dst, last)
        S.finish()
        return nc

    def psum(self):
        p = self.ps[self.ps_i]
        self.ps_i = (self.ps_i + 1) % 6
        return p

    def alloc_wsl(self, n=2):
        self.wsl = [self.sb("wsl%d" % i, [128, NFT, 512], BF16) for i in range(n)]
        self.wsl_i = 0

    def wslab(self):
        w = self.wsl[self.wsl_i]
        self.wsl_i = (self.wsl_i + 1) % len(self.wsl)
        return w

    def load_wslab(self, w_ap2d, c0, ncols=512, nrows_t=NFT):
        w = self.wslab()
        src = w_ap2d[:, c0:c0 + ncols].rearrange("(a p) c -> p a c", p=128)
        self.S.dma("pool", w[:, 0:nrows_t, 0:ncols], src, writes=[w])
        return w

    def alloc_norm(self, layer, nxt):
        self.normw = self.sb("normw", [128, D], F32)
        row = self.normw_d.ap()[layer:layer + 1, :]
        self.S.dma("sp", self.normw[:], bcast_rows(row, 128), writes=[self.normw])
        self.xt = [self.sb("xt%d" % i, [128, D], F32) for i in range(nxt)]
        self.hb = [self.sb("hb%d" % i, [128, D], BF16) for i in range(2)]
        self.stat = [self.sb("stat%d" % i, [128, 4], F32) for i in range(2)]
        self.epsb = self.sb("epsb", [128, 1], F32)
        self.S.op("dve", lambda e: e.memset(self.epsb[:], RMS_EPS), writes=[self.epsb])

    def norm_tiles(self, src, tile0, ntiles, hT):
        S = self.S
        nx = len(self.xt)
        for i in range(ntiles):
            tile = tile0 + i
            xt = self.xt[i % nx]
            st = self.stat[i % 2]
            S.dma("sp", xt[:], src.ap()[tile * 128:(tile + 1) * 128, :], reads=[self.xbuf(src, tile)], writes=[xt])
            hb = self.hb[i % 2]
            S.op("act", lambda e: e.activation(out=hb[:], in_=xt[:], func=AF.Square, accum_out=st[:, 0:1]),
                 reads=[xt], writes=[hb, st])
            S.op("act", lambda e: e.activation(out=st[:, 1:2], in_=st[:, 0:1], func=AF.Sqrt, scale=1.0 / D, bias=self.epsb[:, 0:1]),
                 reads=[st, self.epsb], writes=[st])
            S.op("dve", lambda e: e.reciprocal(out=st[:, 2:3], in_=st[:, 1:2]), reads=[st], writes=[st])
            hb = self.hb[i % 2]
            S.op("dve", lambda e: e.scalar_tensor_tensor(out=hb[:], in0=xt[:], scalar=st[:, 2:3], in1=self.normw[:],
                                                         op0=ALU.mult, op1=ALU.mult),
                 reads=[xt, st, self.normw], writes=[hb])
            for half in range(2):
                p = self.psum()
                pv = p.t[:].bitcast(BF16)
                for j in range(8):
                    ft = half * 8 + j
                    S.op("pe", lambda e: e.transpose(pv[:, j * 128:(j + 1) * 128], hb[:, ft * 128:(ft + 1) * 128],
                                                     self.ident[:]),
                         reads=[hb, self.ident], writes=[p], signal=(j == 7))
                S.op("act", lambda e: e.copy(out=hT[:, half * 8:half * 8 + 8, i * 128:(i + 1) * 128],
                                             in_=pv.rearrange("p (a c) -> p a c", a=8)),
                     reads=[p], writes=[hT])

    def layer_pool(self, src, dst, last):
        S = self.S
        w_in = self.dram_in("c_w_in", [D, 2 * D]).ap()
        w_grp = self.dram_in("c_w_grp", [4 * 512, 512]).ap()
        w_out = self.dram_in("c_w_out", [D, D]).ap()
        scale_d = self.dram_in("c_scale_pm", [128, NFT]).ap()
        rc_d = self.dram_in("pool_rc", [128, 4 * 16]).ap()
        self.alloc_norm(2, 4)
        self.alloc_wsl()
        hT = self.sb("hT", [128, NFT, TP], BF16)
        scale = self.sb("c_scale", [128, NFT], F32)
        S.dma("sp", scale[:], scale_d[:, :], writes=[scale])
        rc = self.sb("pool_rc", [128, 4, 16], F32)
        S.dma("sp", rc[:].rearrange("p a b -> p (a b)"), rc_d[:, :], writes=[rc])
        wg = self.sb("wgrp", [128, 16, 512], BF16)
        S.dma("pool", wg[:], w_grp.rearrange("(a p) c -> p a c", p=128), writes=[wg])
        hist = self.sb("hist", [128, NFT, 16], F32)
        ub = [self.sb("ub%d" % i, [128, 16 + TP], F32) for i in range(2)]
        tmp = [self.sb("ptmp%d" % i, [128, 16 + TP], F32) for i in range(2)]
        dT = self.sb("dT", [128, NFT, TP], BF16)
        sg = self.sb("sg", [128, NFT, TP], BF16)
        zT = self.sb("zT", [128, NFT, TP], BF16)
        for panel in range(self.n_panels):
            first = (panel % NPB == 0)
            self.norm_tiles(src, panel * 4, 4, hT)
            if first:
                S.op("dve", lambda e: e.memset(hist[:], 0.0), writes=[hist])
            for s in range(8):
                w = self.load_wslab(w_in, s * 512)
                for j in range(4):
                    fo = s * 4 + j
                    p = self.psum()
                    for fi in range(NFT):
                        S.op("pe", lambda e: e.matmul(p[:], w[:, fi, j * 128:(j + 1) * 128], hT[:, fi, :],
                                                      start=(fi == 0), stop=(fi == NFT - 1)),
                             reads=[w, hT], writes=[p], signal=(fi == NFT - 1))
                    if fo < NFT:
                        g = fo // 4
                        win = 2 << g
                        u = ub[fo % 2]
                        S.op("act", lambda e: e.copy(out=u[:, 16:16 + TP], in_=p[:]), reads=[p], writes=[u])
                        S.op("pool", lambda e: e.tensor_copy(out=u[:, 0:16], in_=hist[:, fo, :]), reads=[hist], writes=[u])
                        S.op("pool", lambda e: e.tensor_copy(out=hist[:, fo, :], in_=u[:, TP:TP + 16]), reads=[u], writes=[hist])
                        cur = u
                        sh = 1
                        k = 0
                        while sh < win:
                            nxt = tmp[k % 2]
                            lo = 2 * sh - 1
                            S.op("dve", lambda e: e.tensor_tensor(out=nxt[:, lo:16 + TP], in0=cur[:, lo:16 + TP],
                                                                  in1=cur[:, lo - sh:16 + TP - sh], op=ALU.add),
                                 reads=[cur], writes=[nxt])
                            cur = nxt
                            sh *= 2
                            k += 1
                        S.op("dve", lambda e: e.scalar_tensor_tensor(out=dT[:, fo, :], in0=cur[:, 16:16 + TP], scalar=1.0 / win,
                                                                     in1=u[:, 16:16 + TP], op0=ALU.mult, op1=ALU.subtract),
                             reads=[cur, u], writes=[dT])
                        if first:
                            t2 = tmp[k % 2]
                            S.op("dve", lambda e: e.tensor_tensor(out=t2[:, 0:16], in0=cur[:, 16:32], in1=rc[:, g, :], op=ALU.mult),
                                 reads=[cur, rc], writes=[t2])
                            S.op("dve", lambda e: e.tensor_tensor(out=dT[:, fo, 0:16], in0=t2[:, 0:16], in1=u[:, 16:32], op=ALU.subtract),
                                 reads=[t2, u], writes=[dT])
                    else:
                        S.op("act", lambda e: e.activation(out=sg[:, fo - NFT, :], in_=p[:], func=AF.Silu), reads=[p], writes=[sg])
            for fo in range(NFT):
                g = fo // 4
                jo = fo % 4
                p = self.psum()
                for fi in range(4):
                    S.op("pe", lambda e: e.matmul(p[:], wg[:, g * 4 + fi, jo * 128:(jo + 1) * 128], dT[:, g * 4 + fi, :],
                                                  start=(fi == 0), stop=(fi == 3)),
                         reads=[wg, dT], writes=[p], signal=(fi == 3))
                S.op("dve", lambda e: e.scalar_tensor_tensor(out=zT[:, fo, :], in0=p[:], scalar=scale[:, fo:fo + 1],
                                                             in1=sg[:, fo, :], op0=ALU.mult, op1=ALU.mult),
                     reads=[p, scale, sg], writes=[zT])
            self.out_proj(w_out, zT, NFT, src, dst, panel * 4, last, reload_x=False)

    def out_proj(self, w_out, zT, nfi, src, dst, tile0, last, reload_x):
        S = self.S
        if reload_x:
            for i in range(4):
                tile = tile0 + i
                S.dma("sp", self.xt[i][:], src.ap()[tile * 128:(tile + 1) * 128, :], reads=[self.xbuf(src, tile)],
                      writes=[self.xt[i]])
        for c in range(4):
            w = self.load_wslab(w_out, c * 512, nrows_t=nfi)
            for i in range(4):
                p = self.psum()
                for fi in range(nfi):
                    S.op("pe", lambda e: e.matmul(p[:], zT[:, fi, i * 128:(i + 1) * 128], w[:, fi, :],
                                                  start=(fi == 0), stop=(fi == nfi - 1)),
                         reads=[zT, w], writes=[p], signal=(fi == nfi - 1))
                xt = self.xt[i]
                S.op("dve", lambda e: e.tensor_tensor(out=xt[:, c * 512:(c + 1) * 512], in0=xt[:, c * 512:(c + 1) * 512],
                                                      in1=p[:], op=ALU.add), reads=[xt, p], writes=[xt])
        for i in range(4):
            tile = tile0 + i
            S.dma("sp", dst.ap()[tile * 128:(tile + 1) * 128, :], self.xt[i][:], reads=[self.xt[i]],
                  writes=[self.xbuf(dst, tile)], is_output=last)

    def layer_attn(self, src, dst, last):
        S = self.S
        nc = self.nc
        w_in = self.dram_in("b_w_in_r", [D, 8 * 1280]).ap()
        w_out = self.dram_in("b_w_out", [1024, D]).ap()
        qkn_d = self.dram_in("b_qkn_pm", [128, 6]).ap()
        pos_d = self.dram_in("positions", [NB_CORE, SEQ], I32).ap()
        invf_d = self.dram_in("rope_invf", [32, 1]).ap()
        rot_d = self.dram_in("rope_rot", [32, 32]).ap()
        mask_d = self.dram_in("attn_mask", [128, 256]).ap()
        zscr = [nc.dram_tensor("zscr%d" % b, [1024, SEQ], BF16, kind="Internal") for b in range(NB_CORE)]
        zbuf = [[TT(None) for hd in range(8)] for b in range(NB_CORE)]
        hTf = self.sb("hTf", [128, NFT, SEQ], BF16)
        mask = self.sb("amask", [128, 256], BF16)
        S.dma("pool", mask[:], mask_d[:, :], writes=[mask])
        rot = self.sb("rot", [32, 32], BF16)
        S.dma("pool", rot[:], rot_d[:, :], writes=[rot])
        qkn = self.sb("qkn", [128, 6], F32)
        S.dma("sp", qkn[:], qkn_d[:, :], writes=[qkn])
        invf = self.sb("invf", [32, 1], F32)
        S.dma("sp", invf[:], invf_d[:, :], writes=[invf])
        ones = self.sb("ones", [128, 128], BF16)
        S.op("dve", lambda e: e.memset(ones[:], 1.0), writes=[ones])
        DILS = (1, 4, 16)
        SCALE = 128.0 ** -0.5
        PI = float(np.pi)
        C1 = 6.28125
        C2 = float(2 * np.pi - 6.28125)
        nb = self.n_panels // NPB
        for b in range(nb):
            with self.scope():
                self.alloc_norm(1, 2)
                self.norm_tiles(src, b * 16, 16, hTf)
            with self.scope():
                self.alloc_wsl()
                epsb = self.sb("epsb2", [128, 1], F32)
                S.op("dve", lambda e: e.memset(epsb[:], RMS_EPS), writes=[epsb])
                Ctab = self.sb("Ctab", [32, SEQ], F32)
                Stab = self.sb("Stab", [32, SEQ], F32)
                posi = self.sb("posi", [32, 512], I32)
                ra = self.sb("ra", [32, 512], F32)
                rb = self.sb("rb", [32, 512], F32)
                rc_ = self.sb("rc", [32, 512], F32)
                rqi = self.sb("rqi", [32, 512], I32)
                for c in range(4):
                    cs = slice(c * 512, (c + 1) * 512)
                    S.dma("sp", posi[:], bcast_rows(pos_d[b:b + 1, cs], 32), writes=[posi])
                    S.op("dve", lambda e: e.tensor_copy(out=ra[:], in_=posi[:]), reads=[posi], writes=[ra])
                    S.op("dve", lambda e: e.tensor_scalar(out=ra[:], in0=ra[:], scalar1=invf[:, 0:1], scalar2=None, op0=ALU.mult),
                         reads=[ra, invf], writes=[ra])
                    S.op("dve", lambda e: e.tensor_scalar(out=rb[:], in0=ra[:], scalar1=1.0 / (2 * PI), scalar2=None, op0=ALU.mult),
                         reads=[ra], writes=[rb])
                    S.op("dve", lambda e: e.tensor_copy(out=rqi[:], in_=rb[:]), reads=[rb], writes=[rqi])
                    S.op("dve", lambda e: e.tensor_copy(out=rb[:], in_=rqi[:]), reads=[rqi], writes=[rb])
                    S.op("dve", lambda e: e.scalar_tensor_tensor(out=rc_[:], in0=rb[:], scalar=-C1, in1=ra[:], op0=ALU.mult, op1=ALU.add),
                         reads=[rb, ra], writes=[rc_])
                    S.op("dve", lambda e: e.scalar_tensor_tensor(out=ra[:], in0=rb[:], scalar=-C2, in1=rc_[:], op0=ALU.mult, op1=ALU.add),
                         reads=[rb, rc_], writes=[ra])
                    S.op("dve", lambda e: e.tensor_scalar(out=rb[:], in0=ra[:], scalar1=PI, scalar2=-PI, op0=ALU.min, op1=ALU.max),
                         reads=[ra], writes=[rb])
                    S.op("act", lambda e: e.activation(out=Stab[:, cs], in_=rb[:], func=AF.Sin), reads=[rb], writes=[Stab])
                    S.op("dve", lambda e: e.tensor_scalar(out=rc_[:], in0=ra[:], scalar1=PI / 2, scalar2=None, op0=ALU.add),
                         reads=[ra], writes=[rc_])
                    S.op("dve", lambda e: e.tensor_scalar(out=rb[:], in0=rc_[:], scalar1=PI, scalar2=-2 * PI, op0=ALU.is_gt, op1=ALU.mult),
                         reads=[rc_], writes=[rb])
                    S.op("dve", lambda e: e.tensor_tensor(out=rc_[:], in0=rc_[:], in1=rb[:], op=ALU.add), reads=[rc_, rb], writes=[rc_])
                    S.op("dve", lambda e: e.tensor_scalar(out=rc_[:], in0=rc_[:], scalar1=PI, scalar2=-PI, op0=ALU.min, op1=ALU.max),
                         reads=[rc_], writes=[rc_])
                    S.op("act", lambda e: e.activation(out=Ctab[:, cs], in_=rc_[:], func=AF.Sin), reads=[rc_], writes=[Ctab])
                qTb = [self.sb("qT%d" % i, [128, SEQ], BF16) for i in range(2)]
                kTb = [self.sb("kT%d" % i, [128, SEQ], BF16) for i in range(2)]
                Vb = [self.sb("V%d" % i, [128, 16, 128], BF16) for i in range(2)]
                qf = [self.sb("qf%d" % i, [128, 512], F32) for i in range(2)]
                sq = [self.sb("sq%d" % i, [128, 512], BF16) for i in range(2)]
                rs = [self.sb("rs%d" % i, [128, 512], F32) for i in range(2)]
                rt1 = self.sb("rt1", [32, 512], F32)
                rt2 = self.sb("rt2", [32, 512], F32)
                PT = [self.sb("PT%d" % i, [128, 1024], BF16) for i in range(2)]
                acc = self.sb("acc", [128, SEQ], F32)
                lacc = self.sb("lacc", [128, SEQ], F32)
                sgateb = [self.sb("sgate%d" % i, [128, SEQ], BF16) for i in range(2)]
                zst = self.sb("zst", [128, SEQ], BF16)
                cnt = [0]

                def proj_qk(w, coff, dstT, ncol):
                    for c in range(4):
                        cs = slice(c * 512, (c + 1) * 512)
                        p = self.psum()
                        for fi in range(NFT):
                            S.op("pe", lambda e: e.matmul(p[:], w[:, fi, coff:coff + 128], hTf[:, fi, cs],
                                                          start=(fi == 0), stop=(fi == NFT - 1)),
                                 reads=[w, hTf], writes=[p], signal=(fi == NFT - 1))
                        k = cnt[0] % 2
                        cnt[0] += 1
                        S.op("act", lambda e: e.copy(out=qf[k][:], in_=p[:]), reads=[p], writes=[qf[k]])
                        S.op("act", lambda e: e.activation(out=sq[k][:], in_=p[:], func=AF.Square), reads=[p], writes=[sq[k]])
                        yield
                        p2 = self.psum()
                        S.op("pe", lambda e: e.matmul(p2[:], ones[:], sq[k][:], start=True, stop=True),
                             reads=[ones, sq[k]], writes=[p2])
                        S.op("act", lambda e: e.activation(out=rs[k][:], in_=p2[:], func=AF.Ln, scale=1.0 / 128, bias=epsb[:, 0:1]),
                             reads=[p2, epsb], writes=[rs[k]])
                        S.op("act", lambda e: e.activation(out=rs[k][:], in_=rs[k][:], func=AF.Exp, scale=-0.5), reads=[rs[k]], writes=[rs[k]])
                        S.op("dve", lambda e: e.scalar_tensor_tensor(out=dstT[:, cs], in0=qf[k][:], scalar=qkn[:, ncol:ncol + 1],
                                                                     in1=rs[k][:], op0=ALU.mult, op1=ALU.mult),
                             reads=[qf[k], qkn, rs[k]], writes=[dstT])
                        yield
                        p3 = self.psum()
                        S.op("pe", lambda e: e.matmul(p3[0:32, :], rot[:], dstT[0:32, cs], start=True, stop=True),
                             reads=[rot, dstT], writes=[p3])
                        S.op("dve", lambda e: e.tensor_tensor(out=rt1[:], in0=p3[0:32, :], in1=Stab[:, cs], op=ALU.mult),
                             reads=[p3, Stab], writes=[rt1])
                        S.op("pool", lambda e: e.tensor_tensor(out=rt2[:], in0=dstT[0:32, cs], in1=Ctab[:, cs], op=ALU.mult),
                             reads=[dstT, Ctab], writes=[rt2])
                        S.op("dve", lambda e: e.tensor_tensor(out=dstT[0:32, cs], in0=rt1[:], in1=rt2[:], op=ALU.add),
                             reads=[rt1, rt2], writes=[dstT])
                        yield

                def view(ap2d, g):
                    dil = DILS[g]
                    nblk = SEQ // dil // 128
                    return ap2d.rearrange("p (n i r) -> p r n i", n=nblk, i=128, r=dil)

                def blocks(g):
                    dil = DILS[g]
                    nblk = SEQ // dil // 128
                    return [(r, n) for r in range(dil) for n in range(nblk)]

                def proj_item(it, hd, g):
                    qT, kT, V = qTb[it % 2], kTb[it % 2], Vb[it % 2]
                    sgate = sgateb[hd % 2]
                    if g == 0:
                        w = self.load_wslab(w_in, hd * 1280, ncols=512)
                    else:
                        w = self.load_wslab(w_in, hd * 1280 + 512 + (g - 1) * 384, ncols=384)
                    voff = 256
                    yield from proj_qk(w, 0, qT, g)
                    yield from proj_qk(w, 128, kT, 3 + g)
                    bl = blocks(g)
                    for tg in range(4):
                        p = self.psum()
                        for tl in range(4):
                            r, n = bl[tg * 4 + tl]
                            for fi in range(NFT):
                                S.op("pe", lambda e: e.matmul(p[:, tl * 128:(tl + 1) * 128], view(hTf[:, fi, :], g)[:, r, n, :],
                                                              w[:, fi, voff:voff + 128], start=(fi == 0), stop=(fi == NFT - 1)),
                                     reads=[w, hTf], writes=[p], signal=(fi == NFT - 1 and tl == 3))
                        S.op("act", lambda e: e.copy(out=V[:, tg * 4:tg * 4 + 4, :], in_=p[:].rearrange("p (a c) -> p a c", a=4)),
                             reads=[p], writes=[V])
                        yield
                    if g == 0:
                        for c in range(4):
                            cs = slice(c * 512, (c + 1) * 512)
                            p = self.psum()
                            for fi in range(NFT):
                                S.op("pe", lambda e: e.matmul(p[:], w[:, fi, 384:512], hTf[:, fi, cs],
                                                              start=(fi == 0), stop=(fi == NFT - 1)),
                                     reads=[w, hTf], writes=[p], signal=(fi == NFT - 1))
                            S.op("act", lambda e: e.activation(out=sgate[:, cs], in_=p[:], func=AF.Silu), reads=[p], writes=[sgate])
                            yield

                def attn_item(it, hd, g):
                    qT, kT, V = qTb[it % 2], kTb[it % 2], Vb[it % 2]
                    sgate = sgateb[hd % 2]
                    bl = blocks(g)
                    qv = view(qT[:, :], g)
                    kv = view(kT[:, :], g)
                    av = view(acc[:, :], g)
                    lv = view(lacc[:, :], g)
                    for j in range(4):
                        pt = PT[j % 2]
                        pss = [self.psum(), self.psum()]
                        for bi in range(4):
                            r, n = bl[j * 4 + bi]
                            bank = pss[bi // 2]
                            off = (bi % 2) * 256
                            if n > 0:
                                S.op("pe", lambda e: e.matmul(bank[:, off:off + 128], kv[:, r, n - 1, :], qv[:, r, n, :],
                                                              start=True, stop=False),
                                     reads=[kT, qT], writes=[bank], signal=False)
                                S.op("pe", lambda e: e.matmul(bank[:, off:off + 128], self.ident[:], mask[:, 0:128],
                                                              start=False, stop=True),
                                     reads=[self.ident, mask], writes=[bank], signal=False)
                            S.op("pe", lambda e: e.matmul(bank[:, off + 128:off + 256], kv[:, r, n, :], qv[:, r, n, :],
                                                          start=True, stop=False),
                                 reads=[kT, qT], writes=[bank], signal=False)
                            S.op("pe", lambda e: e.matmul(bank[:, off + 128:off + 256], self.ident[:], mask[:, 128:256],
                                                          start=False, stop=True),
                                 reads=[self.ident, mask], writes=[bank], signal=(bi % 2 == 1))
                        for h2 in range(2):
                            S.op("act", lambda e: e.activation(out=pt[:, h2 * 512:(h2 + 1) * 512], in_=pss[h2][:], func=AF.Exp, scale=SCALE),
                                 reads=[pss[h2]], writes=[pt])
                        yield
                        po = self.psum()
                        pl = self.psum()
                        for (bank, isl) in ((po, False), (pl, True)):
                            for bi in range(4):
                                r, n = bl[j * 4 + bi]
                                bidx = j * 4 + bi
                                osl = slice(bi * 128, (bi + 1) * 128)
                                lh_cur = ones[:] if isl else V[:, bidx, :]
                                S.op("pe", lambda e: e.matmul(bank[:, osl], lh_cur, pt[:, bi * 256 + 128:bi * 256 + 256],
                                                              start=True, stop=(n == 0)),
                                     reads=[V, ones, pt], writes=[bank], signal=(n == 0 and bi == 3))
                                if n > 0:
                                    lh_prev = ones[:] if isl else V[:, bidx - 1, :]
                                    S.op("pe", lambda e: e.matmul(bank[:, osl], lh_prev, pt[:, bi * 256:bi * 256 + 128],
                                                                  start=False, stop=True),
                                         reads=[V, ones, pt], writes=[bank], signal=(bi == 3))
                        if g == 0:
                            osel = av[:, 0, 4 * j:4 * j + 4, :]
                            lsel = lv[:, 0, 4 * j:4 * j + 4, :]
                        elif g == 1:
                            osel = av[:, j, :, :]
                            lsel = lv[:, j, :, :]
                        else:
                            osel = av[:, 4 * j:4 * j + 4, 0, :]
                            lsel = lv[:, 4 * j:4 * j + 4, 0, :]
                        pov = po[:].rearrange("p (a c) -> p a c", a=4)
                        plv = pl[:].rearrange("p (a c) -> p a c", a=4)
                        if g == 0:
                            S.op("act", lambda e: e.copy(out=osel, in_=pov), reads=[po], writes=[acc])
                            S.op("dve", lambda e: e.tensor_copy(out=lsel, in_=plv), reads=[pl], writes=[lacc])
                        else:
                            S.op("dve", lambda e: e.tensor_tensor(out=osel, in0=osel, in1=pov, op=ALU.add), reads=[po, acc], writes=[acc])
                            S.op("dve", lambda e: e.tensor_tensor(out=lsel, in0=lsel, in1=plv, op=ALU.add), reads=[pl, lacc], writes=[lacc])
                        yield
                    if g == 2:
                        S.op("act", lambda e: e.activation(out=lacc[:], in_=lacc[:], func=AF.Ln), reads=[lacc], writes=[lacc])
                        S.op("act", lambda e: e.activation(out=lacc[:], in_=lacc[:], func=AF.Exp, scale=-1.0), reads=[lacc], writes=[lacc])
                        S.op("pool", lambda e: e.tensor_tensor(out=acc[:], in0=acc[:], in1=lacc[:], op=ALU.mult), reads=[acc, lacc], writes=[acc])
                        S.op("dve", lambda e: e.tensor_tensor(out=zst[:], in0=acc[:], in1=sgate[:], op=ALU.mult), reads=[acc, sgate], writes=[zst])
                        S.dma("sp", zscr[b].ap()[hd * 128:(hd + 1) * 128, :], zst[:], reads=[zst], writes=[zbuf[b][hd]])

                def weave2(*gens):
                    gens = [g_ for g_ in gens if g_ is not None]
                    while gens:
                        for g_ in list(gens):
                            try:
                                next(g_)
                            except StopIteration:
                                gens.remove(g_)

                prev_attn = None
                it = 0
                for hd in range(8):
                    for g in range(3):
                        weave2(proj_item(it, hd, g), prev_attn)
                        prev_attn = attn_item(it, hd, g)
                        it += 1
                weave2(prev_attn)
            with self.scope():
                self.alloc_wsl()
                self.xt = [self.sb("xt%d" % i, [128, D], F32) for i in range(4)]
                zp = [self.sb("zp%d" % i, [128, 8, 512], BF16) for i in range(2)]
                for pn in range(4):
                    z = zp[pn % 2]
                    S.dma("sp", z[:], zscr[b].ap()[:, pn * 512:(pn + 1) * 512].rearrange("(a p) c -> p a c", p=128),
                          reads=zbuf[b], writes=[z])
                    self.out_proj(w_out, z, 8, src, dst, b * 16 + pn * 4, last, reload_x=True)

    def layer_rwkv(self, L, src, dst, last):
        S = self.S
        nc = self.nc
        ia = 0 if L == 0 else 1
        has_v = (L != 0)
        TR = 256
        NCH = TR // 64
        npan = self.n_panels * TP // TR
        ppb = SEQ // TR
        pre = "a%d_" % ia
        w_in = self.dram_in(pre + "w_in", [4 * D, D]).ap()
        w_out = self.dram_in(pre + "w_out", [D, D]).ap()
        mu_d = self.dram_in(pre + "mu_pm", [128, 6 * NFT]).ap()
        par_d = self.dram_in(pre + "par_pm", [128, 8 * NFT]).ap()
        w1_d = self.dram_in(pre + "w1", [D, 96]).ap()
        a1_d = self.dram_in(pre + "a1", [D, 96]).ap()
        w2_d = self.dram_in(pre + "w2", [96, D]).ap()
        a2_d = self.dram_in(pre + "a2", [96, D]).ap()
        if has_v:
            v1_d = self.dram_in(pre + "v1", [D, 64]).ap()
            v2_d = self.dram_in(pre + "v2", [64, D]).ap()
        cm_d = self.dram_in("rwkv_masks", [128, 512]).ap()
        if not hasattr(self, "vfs"):
            self.vfs_out = False
            if 0 in self.layers:
                self.vfs_out = 3 not in self.layers
                self.vfs = nc.dram_tensor("vfs", [D, NB_CORE * SEQ], F32, kind="ExternalOutput" if self.vfs_out else "Internal")
            else:
                self.vfs = self.dram_in("vfs", [D, NB_CORE * SEQ])
            self.vfs_b = {}
        wsc = nc.dram_tensor(pre + "wsc", [20, 128, NFT * 512], BF16, kind="Internal")
        wsc_b = [TT(None) for _ in range(20)]
        for sl in range(4):
            for j in range(4):
                idx = j * 4 + sl
                S.dma("pool", wsc.ap()[idx].rearrange("p (a c) -> p a c", a=NFT),
                      w_in[j * D:(j + 1) * D, sl * 512:(sl + 1) * 512].rearrange("(a p) c -> p a c", p=128), writes=[wsc_b[idx]])
        for sl in range(4):
            idx = 16 + sl
            S.dma("pool", wsc.ap()[idx].rearrange("p (a c) -> p a c", a=NFT),
                  w_out[:, sl * 512:(sl + 1) * 512].rearrange("(a p) c -> p a c", p=128), writes=[wsc_b[idx]])
        mu = self.sb("mu", [128, 6 * NFT], F32)
        S.dma("sp", mu[:], mu_d[:, :], writes=[mu])
        par = self.sb("par", [128, 8 * NFT], F32)
        S.dma("sp", par[:], par_d[:, :], writes=[par])
        parh = self.sb("parh", [128, 8 * NFT], F32)
        S.op("dve", lambda e: e.tensor_scalar(out=parh[:], in0=par[:], scalar1=0.5, scalar2=None, op0=ALU.mult), reads=[par], writes=[parh])
        omka = self.sb("omka", [128, NFT], F32)
        S.op("dve", lambda e: e.tensor_scalar(out=omka[:], in0=par[:, 3 * NFT:4 * NFT], scalar1=-1.0, scalar2=1.0, op0=ALU.mult, op1=ALU.add),
             reads=[par], writes=[omka])
        P_W0, P_A0, P_KK, P_KA, P_RK, P_LW, P_LB, P_V0 = range(8)

        def pcol(which, ft, t=par):
            return t[:, which * NFT + ft:which * NFT + ft + 1]
        wl = self.sb("wl", [128, NFT, 96], BF16)
        w2 = self.sb("w2", [96, D], BF16)
        S.dma("pool", w2[:], w2_d[:, :], writes=[w2])
        a2 = self.sb("a2", [96, D], BF16)
        S.dma("pool", a2[:], a2_d[:, :], writes=[a2])
        if has_v:
            v2 = self.sb("v2", [64, D], BF16)
            S.dma("pool", v2[:], v2_d[:, :], writes=[v2])
        cmf = self.sb("cmf", [128, 512], F32)
        S.dma("sp", cmf[:], cm_d[:, :], writes=[cmf])
        cmb = self.sb("cmb", [128, 4 * 64], BF16)
        S.op("dve", lambda e: e.tensor_copy(out=cmb[:], in_=cmf[:, 0:256]), reads=[cmf], writes=[cmb])
        id2b = cmb[:, 192:256]
        resetm = cmf[:, 256:256 + TR]
        bo = self.sb("bo", [128, 128], BF16)
        S.op("dve", lambda e: e.memset(bo[:], 0.0), writes=[bo])
        S.op("dve", lambda e: e.memset(bo[0:64, 0:64], 1.0), writes=[bo])
        S.op("dve", lambda e: e.memset(bo[64:128, 64:128], 1.0), writes=[bo])
        epsl = self.sb("epsl", [128, 1], F32)
        S.op("dve", lambda e: e.memset(epsl[:], 64e-5), writes=[epsl])
        self.alloc_norm(L, 2)
        hT = self.sb("hT", [128, NFT, TR], BF16)
        xz = self.sb("xz", [128, NFT, TR], BF16)
        carry = self.sb("carry", [128, NFT], BF16)
        tw = self.sb("tw", [96, TR], BF16)
        la = self.sb("la", [96, TR], BF16)
        lv = self.sb("lv", [64, TR], BF16)
        wsl = [self.sb("rwsl%d" % i, [128, NFT, 512], BF16) for i in range(2)]
        mixb = [self.sb("mixb%d" % i, [128, NFT, TR], BF16) for i in range(4)]
        mixl = self.sb("mixl", [128, NFT, TR], BF16)
        wslc = [0]
        Rs0 = self.sb("Rs", [128, 4, TR], F32)
        Ks0 = self.sb("Ks", [128, 4, TR], F32)
        Vs0 = self.sb("Vs", [128, 4, TR], F32)
        Gr0 = self.sb("Gr", [128, 4, TR], F32)
        Sf = [self.sb("Sf%d" % i, [128, 64], F32) for i in range(NFT)]
        Sb = [self.sb("Sb%d" % i, [128, 64], BF16) for i in range(NFT)]

        def f32t(name):
            return self.sb(name, [128, TR], F32)

        def bft(name):
            return self.sb(name, [128, TR], BF16)
        Wt, Winv, kk, nrm = [f32t(n) for n in ("Wt", "Winv", "kk", "nrm")]
        thw, tha, thv, thg = Wt, Winv, kk, nrm
        aa, sv, logw, cum, Wend, Wprev, tt, kp, bs, vf_t = [f32t(n) for n in ("aa", "sv", "logw", "cum", "Wend", "Wprev", "tt", "kp", "bs", "vf_t")]
        ksq, bT, kTl, bh, kh, vb = [bft(n) for n in ("ksq", "bT", "kTl", "bh", "kh", "vb")]
        AR = [self.sb("AR%d" % i, [128, 2, TR], BF16) for i in range(2)]
        TM = [self.sb("TM%d" % i, [128, 3, NCH, 64], BF16) for i in range(2)]
        MBs = [self.sb("MBs%d" % i, [128, NCH, 2, 64], BF16) for i in range(2)]
        MKs = [self.sb("MKs%d" % i, [128, NCH, 2, 64], BF16) for i in range(2)]
        TTt = [self.sb("TTt%d" % i, [128, NCH, 64], BF16) for i in range(2)]
        TTf = [self.sb("TTf%d" % i, [128, NCH, 64], F32) for i in range(2)]
        rkb = [self.sb("rkb%d" % i, [128, TR], BF16) for i in range(2)]
        vfl = [self.sb("vfl%d" % i, [128, TR], F32) for i in range(2)]
        Gs = [self.sb("Gs%d" % i, [128, TR], BF16) for i in range(2)]
        WC = [self.sb("WC%d" % i, [128, NCH], F32) for i in range(2)]
        Bp = [self.sb("Bp%d" % i, [128, NCH, 64], BF16) for i in range(6)]
        Ap = [self.sb("Ap%d" % i, [128, NCH, 64], BF16) for i in range(2)]
        Xs = self.sb("Xs", [128, 64], BF16)
        Us = self.sb("Us", [128, 64], BF16)
        yc, bon, lnv, ysb = [f32t(n) for n in ("yc", "bon", "lnv", "ysb")]
        ybf, sqb = [bft(n) for n in ("ybf", "sqb")]
        EXPM05 = float(np.exp(-0.5))
        PRS = (slice(0, 64), slice(64, 128))
        pYb = [self.ps[7], self.ps[6]]

        def gen_mix(j, m):
            for fi in range(NFT):
                S.op("dve", lambda e: e.scalar_tensor_tensor(out=m[:, fi, :], in0=xz[:, fi, :], scalar=mu[:, j * NFT + fi:j * NFT + fi + 1],
                                                           in1=hT[:, fi, :], op0=ALU.mult, op1=ALU.add),
                     reads=[xz, mu, hT], writes=[m])

        wlsc = nc.dram_tensor(pre + "wlsc", [3, 128, NFT * 96], BF16, kind="Internal")
        wlsc_b = [TT(None) for _ in range(3)]
        lora_srcs = [(w1_d, 96), (a1_d, 96)] + ([(v1_d, 64)] if has_v else [])
        for li, (wd_, ncol_) in enumerate(lora_srcs):
            S.dma("pool", wlsc.ap()[li].rearrange("p (a c) -> p a c", a=NFT)[:, :, 0:ncol_],
                  wd_.rearrange("(a p) c -> p a c", p=128), writes=[wlsc_b[li]])

        def lora_down(m, li, ncol, dst_t, func):
            S.dma("sp", wl[:].rearrange("p a c -> p (a c)"), wlsc.ap()[li], reads=[wlsc_b[li]], writes=[wl])
            p = self.psum()
            for fi in range(NFT):
                S.op("pe", lambda e: e.matmul(p[0:ncol, 0:TR], wl[:, fi, 0:ncol], m[:, fi, :], start=(fi == 0), stop=(fi == NFT - 1)),
                     reads=[wl, m], writes=[p], signal=(fi == NFT - 1))
            S.op("act", lambda e: e.activation(out=dst_t[:], in_=p[0:ncol, 0:TR], func=func), reads=[p], writes=[dst_t])

        def slab_tt(sl, j):
            return (Rs0, Ks0, Vs0, Gr0)[j] if sl % 2 == 0 else self.xt[j // 2]

        def slab_ap(sl, j, ftl):
            if sl % 2 == 0:
                return (Rs0, Ks0, Vs0, Gr0)[j][:, ftl, :]
            o = (j % 2) * 4 * TR + ftl * TR
            return self.xt[j // 2][:, o:o + TR]

        def gemm_slab(sl):
            for j in range(4):
                m = mixb[j]
                w = wsl[wslc[0] % 2]
                wslc[0] += 1
                idx = j * 4 + sl
                S.dma("sp", w[:].rearrange("p a c -> p (a c)"), wsc.ap()[idx], reads=[wsc_b[idx]], writes=[w])
                dstt = slab_tt(sl, j)
                for ftl in range(4):
                    p = self.psum()
                    for fi in range(NFT):
                        S.op("pe", lambda e: e.matmul(p[:, 0:TR], w[:, fi, ftl * 128:(ftl + 1) * 128], m[:, fi, :],
                                                      start=(fi == 0), stop=(fi == NFT - 1)),
                             reads=[w, m], writes=[p], signal=(fi == NFT - 1))
                    S.op("act", lambda e: e.copy(out=slab_ap(sl, j, ftl), in_=p[:, 0:TR]), reads=[p], writes=[dstt])
                    yield

        def stage_a(pn, ft, slot):
            tok0 = pn * TR
            ftl = ft % 4
            sl_ = ft // 4
            r_, k_, v_, g_ = [slab_ap(sl_, j, ftl) for j in range(4)]
            Rs, Ks, Vs, Gr = [slab_tt(sl_, j) for j in range(4)]
            fs = slice(ft * 128, (ft + 1) * 128)
            ar, tm, mbs, mks, ttt, ttf = AR[slot], TM[slot], MBs[slot], MKs[slot], TTt[slot], TTf[slot]
            p1 = self.psum()
            S.op("pe", lambda e: e.matmul(p1[:, 0:TR], w2[:, fs], tw[:], start=True, stop=True), reads=[w2, tw], writes=[p1])
            p2 = self.psum()
            S.op("pe", lambda e: e.matmul(p2[:, 0:TR], a2[:, fs], la[:], start=True, stop=True), reads=[a2, la], writes=[p2])
            if has_v:
                p3 = self.psum()
                S.op("pe", lambda e: e.matmul(p3[:, 0:TR], v2[:, fs], lv[:], start=True, stop=True), reads=[v2, lv], writes=[p3])
            S.op("act", lambda e: e.activation(out=thw[:], in_=p1[:, 0:TR], func=AF.Tanh, scale=0.5, bias=pcol(P_W0, ft, parh)),
                 reads=[p1, parh], writes=[thw])
            S.op("act", lambda e: e.activation(out=tha[:], in_=p2[:, 0:TR], func=AF.Tanh, scale=0.5, bias=pcol(P_A0, ft, parh)),
                 reads=[p2, parh], writes=[tha])
            if has_v:
                S.op("act", lambda e: e.activation(out=thv[:], in_=p3[:, 0:TR], func=AF.Tanh, scale=0.5, bias=pcol(P_V0, ft, parh)),
                     reads=[p3, parh], writes=[thv])
            S.op("act", lambda e: e.activation(out=thg[:], in_=g_, func=AF.Tanh, scale=0.5), reads=[Gr], writes=[thg])
            yield
            S.op("dve", lambda e: e.tensor_scalar(out=logw[:], in0=thw[:], scalar1=-0.5 * EXPM05, scalar2=-0.5 * EXPM05, op0=ALU.mult, op1=ALU.add),
                 reads=[thw], writes=[logw])
            S.op("dve", lambda e: e.tensor_scalar(out=aa[:], in0=tha[:], scalar1=0.5, scalar2=0.5, op0=ALU.mult, op1=ALU.add),
                 reads=[tha], writes=[aa])
            S.op("dve", lambda e: e.scalar_tensor_tensor(out=Gs[slot][:], in0=thg[:], scalar=1.0, in1=g_, op0=ALU.add, op1=ALU.mult),
                 reads=[thg, Gr], writes=[Gs[slot]])
            vkey = (ft, pn)
            if has_v:
                S.op("dve", lambda e: e.tensor_scalar(out=sv[:], in0=thv[:], scalar1=0.5, scalar2=0.5, op0=ALU.mult, op1=ALU.add),
                     reads=[thv], writes=[sv])
                rd = [self.vfs_b[vkey]] if vkey in self.vfs_b else []
                S.dma("sp", vf_t[:], self.vfs.ap()[fs, tok0:tok0 + TR], reads=rd, writes=[vf_t])
                S.op("pool", lambda e: e.tensor_tensor(out=vf_t[:], in0=vf_t[:], in1=v_, op=ALU.subtract), reads=[vf_t, Vs], writes=[vf_t])
                S.op("pool", lambda e: e.tensor_tensor(out=vf_t[:], in0=vf_t[:], in1=sv[:], op=ALU.mult), reads=[vf_t, sv], writes=[vf_t])
                S.op("dve", lambda e: e.tensor_tensor(out=vfl[slot][:], in0=v_, in1=vf_t[:], op=ALU.add), reads=[vf_t, Vs], writes=[vfl[slot]])
            else:
                if vkey not in self.vfs_b:
                    self.vfs_b[vkey] = TT(None)
                S.dma("sp", self.vfs.ap()[fs, tok0:tok0 + TR], v_, reads=[Vs], writes=[self.vfs_b[vkey]], is_output=self.vfs_out)
                S.op("pool", lambda e: e.tensor_copy(out=vfl[slot][:], in_=v_), reads=[Vs], writes=[vfl[slot]])
            vv = vfl[slot]
            S.op("dve", lambda e: e.tensor_tensor_scan(out=cum[:], data0=resetm, data1=logw[:], initial=0.0, op0=ALU.mult, op1=ALU.add),
                 reads=[logw, cmf], writes=[cum])
            cumC_bc = bass.AP(cum.t, 63, [[TR, 128], [64, NCH], [0, 64]])
            cumC = bass.AP(cum.t, 63, [[TR, 128], [64, NCH]])
            c3 = cum[:].rearrange("p (a c) -> p a c", a=NCH)
            S.op("pool", lambda e: e.tensor_tensor(out=Wend[:].rearrange("p (a c) -> p a c", a=NCH), in0=cumC_bc, in1=c3, op=ALU.subtract),
                 reads=[cum], writes=[Wend])
            S.op("dve", lambda e: e.tensor_tensor(out=Wprev[:], in0=cum[:], in1=logw[:], op=ALU.subtract), reads=[cum, logw], writes=[Wprev])
            S.op("dve", lambda e: e.tensor_scalar(out=kk[:], in0=k_, scalar1=pcol(P_KK, ft), scalar2=None, op0=ALU.mult),
                 reads=[Ks, par], writes=[kk])
            S.op("pool", lambda e: e.tensor_tensor(out=ksq[:], in0=kk[:], in1=kk[:], op=ALU.mult), reads=[kk], writes=[ksq])
            pk = self.psum()
            S.op("pe", lambda e: e.matmul(pk[:, 0:TR], bo[:], ksq[:], start=True, stop=True), reads=[bo, ksq], writes=[pk])
            S.op("act", lambda e: e.activation(out=Wt[:], in_=cum[:], func=AF.Exp), reads=[cum], writes=[Wt])
            S.op("act", lambda e: e.activation(out=Winv[:], in_=cum[:], func=AF.Exp, scale=-1.0), reads=[cum], writes=[Winv])
            S.op("act", lambda e: e.activation(out=WC[slot][:], in_=cumC, func=AF.Exp), reads=[cum], writes=[WC[slot]])
            S.op("act", lambda e: e.activation(out=Wprev[:], in_=Wprev[:], func=AF.Exp), reads=[Wprev], writes=[Wprev])
            S.op("act", lambda e: e.activation(out=Wend[:], in_=Wend[:], func=AF.Exp), reads=[Wend], writes=[Wend])
            S.op("dve", lambda e: e.tensor_scalar(out=nrm[:], in0=pk[:, 0:TR], scalar1=1e-24, scalar2=None, op0=ALU.max), reads=[pk], writes=[nrm])
            S.op("act", lambda e: e.activation(out=nrm[:], in_=nrm[:], func=AF.Ln), reads=[nrm], writes=[nrm])
            S.op("act", lambda e: e.activation(out=nrm[:], in_=nrm[:], func=AF.Exp, scale=-0.5), reads=[nrm], writes=[nrm])
            S.op("act", lambda e: e.copy(out=vb[:], in_=vv[:]), reads=[vv], writes=[vb])
            S.op("dve", lambda e: e.tensor_tensor(out=kk[:], in0=kk[:], in1=nrm[:], op=ALU.mult), reads=[kk, nrm], writes=[kk])
            S.op("dve", lambda e: e.tensor_scalar(out=tt[:], in0=aa[:], scalar1=pcol(P_KA, ft), scalar2=omka[:, ft:ft + 1],
                                                  op0=ALU.mult, op1=ALU.add), reads=[aa, par, omka], writes=[tt])
            S.op("dve", lambda e: e.tensor_tensor(out=kp[:], in0=k_, in1=tt[:], op=ALU.mult), reads=[Ks, tt], writes=[kp])
            S.op("pool", lambda e: e.tensor_tensor(out=bs[:], in0=kk[:], in1=aa[:], op=ALU.mult), reads=[kk, aa], writes=[bs])
            S.op("dve", lambda e: e.scalar_tensor_tensor(out=ar[:, 0, :], in0=kk[:], scalar=-1.0, in1=Wprev[:], op0=ALU.mult, op1=ALU.mult),
                 reads=[kk, Wprev], writes=[ar])
            S.op("pool", lambda e: e.tensor_tensor(out=ar[:, 1, :], in0=r_, in1=Wt[:], op=ALU.mult), reads=[Rs, Wt], writes=[ar])
            S.op("dve", lambda e: e.tensor_tensor(out=bT[:], in0=bs[:], in1=Winv[:], op=ALU.mult), reads=[bs, Winv], writes=[bT])
            S.op("dve", lambda e: e.tensor_tensor(out=kTl[:], in0=kp[:], in1=Winv[:], op=ALU.mult), reads=[kp, Winv], writes=[kTl])
            S.op("dve", lambda e: e.tensor_tensor(out=bh[:], in0=bs[:], in1=Wend[:], op=ALU.mult), reads=[bs, Wend], writes=[bh])
            S.op("pool", lambda e: e.tensor_tensor(out=kh[:], in0=kp[:], in1=Wend[:], op=ALU.mult), reads=[kp, Wend], writes=[kh])
            S.op("dve", lambda e: e.scalar_tensor_tensor(out=rkb[slot][:], in0=r_, scalar=pcol(P_RK, ft), in1=kp[:], op0=ALU.mult, op1=ALU.mult),
                 reads=[Rs, par, kp], writes=[rkb[slot]])
            pT = self.psum()
            pTb = pT.t[:].bitcast(BF16)
            for wi, srct in enumerate((vb, bh, kh)):
                for c in range(NCH):
                    for hh in range(2):
                        pr = PRS[hh]
                        o0 = (wi * NCH + c) * 64
                        S.op("pe", lambda e: e.transpose(pTb[pr, o0:o0 + 64], srct[pr, c * 64:(c + 1) * 64], id2b[pr, :]),
                             reads=[srct, cmb], writes=[pT], signal=(wi == 2 and c == NCH - 1 and hh == 1))
            S.op("act", lambda e: e.copy(out=tm[:].rearrange("p a b c -> p (a b c)"), in_=pTb[:, 0:3 * NCH * 64]), reads=[pT], writes=[tm])
            yield
            pMB = self.psum()
            pMK = self.psum()
            pN = self.psum()
            for c in range(NCH):
                cs = slice(c * 64, (c + 1) * 64)
                for hh in range(2):
                    pr = PRS[hh]
                    lastm = (c == NCH - 1 and hh == 1)
                    S.op("pe", lambda e: e.matmul(pMB[pr, c * 128:(c + 1) * 128].rearrange("p (a b) -> p a b", a=2), bT[pr, cs], ar[pr, :, cs],
                                                  start=True, stop=True), reads=[bT, ar], writes=[pMB], signal=lastm)
                    S.op("pe", lambda e: e.matmul(pMK[pr, c * 128:(c + 1) * 128].rearrange("p (a b) -> p a b", a=2), kTl[pr, cs], ar[pr, :, cs],
                                                  start=True, stop=True), reads=[kTl, ar], writes=[pMK], signal=lastm)
                    S.op("pe", lambda e: e.matmul(pN[pr, c * 64:(c + 1) * 64], ar[pr, 0, cs], bT[pr, cs],
                                                  start=True, stop=True), reads=[bT, ar], writes=[pN], signal=lastm)
            mk2 = bass.AP(cmf.t, 0, [[512, 128], [0, NCH], [1, 128]])
            mgt = bass.AP(cmf.t, 128, [[512, 128], [0, NCH], [1, 64]])
            idb = bass.AP(cmf.t, 192, [[512, 128], [0, NCH], [1, 64]])
            S.op("dve", lambda e: e.tensor_tensor(out=mbs[:].rearrange("p c a b -> p c (a b)"),
                                                  in0=pMB[:, 0:NCH * 128].rearrange("p (c x) -> p c x", c=NCH), in1=mk2, op=ALU.mult),
                 reads=[pMB, cmf], writes=[mbs])
            S.op("dve", lambda e: e.tensor_tensor(out=Bp[0][:], in0=pN[:, 0:NCH * 64].rearrange("p (c x) -> p c x", c=NCH), in1=mgt, op=ALU.mult),
                 reads=[pN, cmf], writes=[Bp[0]])
            S.op("dve", lambda e: e.tensor_tensor(out=mks[:].rearrange("p c a b -> p c (a b)"),
                                                  in0=pMK[:, 0:NCH * 128].rearrange("p (c x) -> p c x", c=NCH), in1=mk2, op=ALU.mult),
                 reads=[pMK, cmf], writes=[mks])
            S.op("dve", lambda e: e.tensor_tensor(out=ttf[:], in0=mbs[:, :, 0, :], in1=idb, op=ALU.add), reads=[mbs, cmf], writes=[ttf])
            S.op("act", lambda e: e.copy(out=ttt[:], in_=ttf[:]), reads=[ttf], writes=[ttt])
            yield

            def a_pow(l):
                if l == 0:
                    return mbs, (lambda c, pr: mbs[pr, c, 0, :])
                t = Ap[(l - 1) % 2]
                return t, (lambda c, pr: t[pr, c, :])

            def sq(l):
                At, Aap = a_pow(l - 1)
                Bo = Bp[l - 1]
                pB = self.psum()
                pA = self.psum() if l < 5 else None
                for c in range(NCH):
                    for hh in range(2):
                        pr = PRS[hh]
                        lastm = (c == NCH - 1 and hh == 1)
                        S.op("pe", lambda e: e.matmul(pB[pr, c * 64:(c + 1) * 64], Aap(c, pr), Bo[pr, c, :], start=True, stop=True),
                             reads=[At, Bo], writes=[pB], signal=lastm)
                        if pA is not None:
                            S.op("pe", lambda e: e.matmul(pA[pr, c * 64:(c + 1) * 64], Bo[pr, c, :], Aap(c, pr), start=True, stop=True),
                                 reads=[At, Bo], writes=[pA], signal=lastm)
                S.op("act", lambda e: e.copy(out=Bp[l][:].rearrange("p c x -> p (c x)"), in_=pB[:, 0:NCH * 64]), reads=[pB], writes=[Bp[l]])
                if pA is not None:
                    An = Ap[(l - 1) % 2]
                    S.op("dve", lambda e: e.tensor_copy(out=An[:].rearrange("p c x -> p (c x)"), in_=pA[:, 0:NCH * 64]), reads=[pA], writes=[An])

            def tup(l):
                pTu = self.psum()
                Bn = Bp[l]
                for c in range(NCH):
                    for hh in range(2):
                        pr = PRS[hh]
                        lastm = (c == NCH - 1 and hh == 1)
                        S.op("pe", lambda e: e.matmul(pTu[pr, c * 64:(c + 1) * 64], Bn[pr, c, :], ttt[pr, c, :], start=True, stop=True),
                             reads=[Bn, ttt], writes=[pTu], signal=lastm)
                S.op("dve", lambda e: e.tensor_tensor(out=ttf[:].rearrange("p c x -> p (c x)"), in0=ttf[:].rearrange("p c x -> p (c x)"),
                                                      in1=pTu[:, 0:NCH * 64], op=ALU.add), reads=[pTu, ttf], writes=[ttf])
                S.op("act", lambda e: e.copy(out=ttt[:], in_=ttf[:]), reads=[ttf], writes=[ttt])
            for step in (("s", 1), ("s", 2), ("t", 1), ("s", 3), ("t", 2), ("s", 4), ("t", 3), ("s", 5), ("t", 4), ("t", 5)):
                if step[0] == "s":
                    sq(step[1])
                else:
                    tup(step[1])
                yield

        def stage_b(pn, ft, slot):
            ar, tm, mbs, mks, ttt = AR[slot], TM[slot], MBs[slot], MKs[slot], TTt[slot]
            pY = pYb[slot]
            for c in range(NCH):
                cs = slice(c * 64, (c + 1) * 64)
                pX = self.psum()
                for hh in range(2):
                    pr = PRS[hh]
                    S.op("pe", lambda e: e.matmul(pX[pr, 0:64], ar[pr, 0, cs], Sb[ft][pr, :], start=True, stop=False),
                         reads=[ar, Sb[ft]], writes=[pX], signal=False)
                    S.op("pe", lambda e: e.matmul(pX[pr, 0:64], mks[pr, c, 0, :], tm[pr, 0, c, :], start=False, stop=True),
                         reads=[mks, tm], writes=[pX], signal=(hh == 1))
                S.op("act", lambda e: e.copy(out=Xs[:], in_=pX[:, 0:64]), reads=[pX], writes=[Xs])
                yield
                pU = self.psum()
                for hh in range(2):
                    pr = PRS[hh]
                    S.op("pe", lambda e: e.matmul(pU[pr, 0:64], ttt[pr, c, :], Xs[pr, :], start=True, stop=True),
                         reads=[ttt, Xs], writes=[pU], signal=(hh == 1))
                S.op("dve", lambda e: e.tensor_copy(out=Us[:], in_=pU[:, 0:64]), reads=[pU], writes=[Us])
                yield
                for hh in range(2):
                    pr = PRS[hh]
                    S.op("pe", lambda e: e.matmul(pY[pr, cs], Sb[ft][pr, :], ar[pr, 1, cs], start=True, stop=False),
                         reads=[Sb[ft], ar], writes=[pY], signal=False)
                    S.op("pe", lambda e: e.matmul(pY[pr, cs], Us[pr, :], mbs[pr, c, 1, :], start=False, stop=False),
                         reads=[Us, mbs], writes=[pY], signal=False)
                    S.op("pe", lambda e: e.matmul(pY[pr, cs], tm[pr, 0, c, :], mks[pr, c, 1, :], start=False, stop=True),
                         reads=[tm, mks], writes=[pY], signal=(hh == 1))
                pS = self.psum()
                for hh in range(2):
                    pr = PRS[hh]
                    S.op("pe", lambda e: e.matmul(pS[pr, 0:64], tm[pr, 1, c, :], Us[pr, :], start=True, stop=False),
                         reads=[tm, Us], writes=[pS], signal=False)
                    S.op("pe", lambda e: e.matmul(pS[pr, 0:64], tm[pr, 2, c, :], tm[pr, 0, c, :], start=False, stop=True),
                         reads=[tm], writes=[pS], signal=(hh == 1))
                S.op("dve", lambda e: e.scalar_tensor_tensor(out=Sf[ft][:], in0=Sf[ft][:], scalar=WC[slot][:, c:c + 1], in1=pS[:, 0:64],
                                                             op0=ALU.mult, op1=ALU.add), reads=[Sf[ft], WC[slot], pS], writes=[Sf[ft]])
                S.op("act", lambda e: e.copy(out=Sb[ft][:], in_=Sf[ft][:]), reads=[Sf[ft]], writes=[Sb[ft]])
                yield
            S.op("act", lambda e: e.copy(out=ybf[:], in_=pY[:, 0:TR]), reads=[pY], writes=[ybf])
            S.op("dve", lambda e: e.tensor_copy(out=ysb[:], in_=pY[:, 0:TR]), reads=[pY], writes=[ysb])
            p = self.psum()
            S.op("pe", lambda e: e.matmul(p[:, 0:TR], bo[:], ybf[:], start=True, stop=True), reads=[bo, ybf], writes=[p])
            S.op("pool", lambda e: e.tensor_copy(out=yc[:], in_=p[:, 0:TR]) if False else e.tensor_copy(out=bon[:], in_=vfl[slot][:]),
                 reads=[vfl[slot]], writes=[bon])
            S.op("dve", lambda e: e.scalar_tensor_tensor(out=yc[:], in0=p[:, 0:TR], scalar=-1.0 / 64, in1=ysb[:], op0=ALU.mult, op1=ALU.add),
                 reads=[p, ysb], writes=[yc])
            S.op("act", lambda e: e.activation(out=sqb[:], in_=yc[:], func=AF.Square), reads=[yc], writes=[sqb])
            yield
            p2 = self.psum()
            S.op("pe", lambda e: e.matmul(p2[:, 0:TR], bo[:], sqb[:], start=True, stop=True), reads=[bo, sqb], writes=[p2])
            pb = self.psum()
            S.op("pe", lambda e: e.matmul(pb[:, 0:TR], bo[:], rkb[slot][:], start=True, stop=True), reads=[bo, rkb[slot]], writes=[pb])
            S.op("dve", lambda e: e.tensor_tensor(out=bon[:], in0=pb[:, 0:TR], in1=bon[:], op=ALU.mult), reads=[pb, bon], writes=[bon])
            S.op("act", lambda e: e.activation(out=lnv[:], in_=p2[:, 0:TR], func=AF.Ln, scale=1.0 / 64, bias=epsl[:, 0:1]),
                 reads=[p2, epsl], writes=[lnv])
            S.op("act", lambda e: e.activation(out=lnv[:], in_=lnv[:], func=AF.Exp, scale=-0.5), reads=[lnv], writes=[lnv])
            S.op("dve", lambda e: e.tensor_tensor(out=yc[:], in0=yc[:], in1=lnv[:], op=ALU.mult), reads=[yc, lnv], writes=[yc])
            S.op("dve", lambda e: e.tensor_scalar(out=yc[:], in0=yc[:], scalar1=pcol(P_LW, ft), scalar2=pcol(P_LB, ft),
                                                  op0=ALU.mult, op1=ALU.add), reads=[yc, par], writes=[yc])
            S.op("pool", lambda e: e.tensor_tensor(out=yc[:], in0=yc[:], in1=bon[:], op=ALU.add), reads=[yc, bon], writes=[yc])
            S.op("dve", lambda e: e.scalar_tensor_tensor(out=xz[:, ft, :], in0=yc[:], scalar=0.5, in1=Gs[slot][:], op0=ALU.mult, op1=ALU.mult),
                 reads=[yc, Gs[slot]], writes=[xz])
            yield

        def weave(*gens):
            gens = [g for g in gens if g is not None]
            while gens:
                for g in list(gens):
                    try:
                        next(g)
                    except StopIteration:
                        gens.remove(g)

        def weave3(ga, gb, gg):
            chains = [g for g in (ga, gb) if g is not None]
            rnd = 0
            while chains or gg is not None:
                for g in list(chains):
                    try:
                        next(g)
                    except StopIteration:
                        chains.remove(g)
                if gg is not None and (rnd % 3 == 1 or not chains):
                    try:
                        next(gg)
                    except StopIteration:
                        gg = None
                rnd += 1

        for pn in range(npan):
            first = (pn % ppb == 0)
            self.norm_tiles(src, pn * 2, 2, hT)
            if first:
                S.op("dve", lambda e: e.memset(carry[:], 0.0), writes=[carry])
                for ft in range(NFT):
                    S.op("pool", lambda e: e.memset(Sf[ft][:], 0.0), writes=[Sf[ft]])
                    S.op("pool", lambda e: e.memset(Sb[ft][:], 0.0), writes=[Sb[ft]])
            S.op("dve", lambda e: e.tensor_tensor(out=xz[:, :, 1:TR], in0=hT[:, :, 0:TR - 1], in1=hT[:, :, 1:TR], op=ALU.subtract),
                 reads=[hT], writes=[xz])
            S.op("dve", lambda e: e.tensor_tensor(out=xz[:, :, 0], in0=carry[:], in1=hT[:, :, 0], op=ALU.subtract),
                 reads=[hT, carry], writes=[xz])
            S.op("dve", lambda e: e.tensor_copy(out=carry[:], in_=hT[:, :, TR - 1]), reads=[hT], writes=[carry])
            gen_mix(4, mixl)
            lora_down(mixl, 0, 96, tw, AF.Tanh)
            gen_mix(0, mixb[0])
            gen_mix(5, mixl)
            lora_down(mixl, 1, 96, la, AF.Copy)
            for j in range(1, 4):
                gen_mix(j, mixb[j])
            if has_v:
                lora_down(mixb[2], 2, 64, lv, AF.Copy)
            prevB = None
            for sl in range(4):
                weave(gemm_slab(sl), prevB)
                prevB = None
                for ftl in range(4):
                    ft = sl * 4 + ftl
                    weave(stage_a(pn, ft, ft % 2), prevB)
                    prevB = stage_b(pn, ft, ft % 2)
            weave(prevB)
            for i in range(2):
                tile = pn * 2 + i
                S.dma("sp", self.xt[i][:], src.ap()[tile * 128:(tile + 1) * 128, :], reads=[self.xbuf(src, tile)], writes=[self.xt[i]])
            for c in range(4):
                w = wsl[wslc[0] % 2]
                wslc[0] += 1
                idx = 16 + c
                S.dma("sp", w[:].rearrange("p a c -> p (a c)"), wsc.ap()[idx], reads=[wsc_b[idx]], writes=[w])
                for i in range(2):
                    p = self.psum()
                    for fi in range(NFT):
                        S.op("pe", lambda e: e.matmul(p[:], xz[:, fi, i * 128:(i + 1) * 128], w[:, fi, :],
                                                      start=(fi == 0), stop=(fi == NFT - 1)),
                             reads=[xz, w], writes=[p], signal=(fi == NFT - 1))
                    xt = self.xt[i]
                    S.op("dve", lambda e: e.tensor_tensor(out=xt[:, c * 512:(c + 1) * 512], in0=xt[:, c * 512:(c + 1) * 512],
                                                          in1=p[:], op=ALU.add), reads=[xt, p], writes=[xt])
            for i in range(2):
                tile = pn * 2 + i
                S.dma("sp", dst.ap()[tile * 128:(tile + 1) * 128, :], self.xt[i][:], reads=[self.xt[i]],
                      writes=[self.xbuf(dst, tile)], is_output=last)


def pm(v):
    return np.ascontiguousarray(np.asarray(v, np.float32).reshape(NFT, 128).T)


def host_consts():
    c = {}
    c["ident"] = np.eye(128, dtype=np.float32)
    rc = np.zeros((4, 16), np.float32)
    for g in range(4):
        win = 2 << g
        for t in range(16):
            rc[g, t] = 1.0 / min(t + 1, win)
    invf = (500000.0 ** (-np.arange(0, 32, 2, dtype=np.float32) / 32)).astype(np.float32)
    c["rope_invf"] = np.concatenate([invf, invf]).reshape(32, 1).astype(np.float32)
    rt = np.zeros((32, 32), np.float32)
    for i in range(16):
        rt[i + 16, i] = -1.0
        rt[i, i + 16] = 1.0
    c["rope_rot"] = rt
    kk = np.arange(128)[:, None]
    qq = np.arange(128)[None, :]
    c["attn_mask"] = np.where(np.concatenate([(qq <= kk), (kk <= qq)], axis=1), 0.0, -30000.0).astype(np.float32)
    s_ = np.arange(64)[:, None]
    t_ = np.arange(64)[None, :]
    lt = (s_ < t_).astype(np.float32)
    le = (s_ <= t_).astype(np.float32)
    gt = (s_ > t_).astype(np.float32)
    eye = np.eye(64, dtype=np.float32)
    rm = np.ones((64, 256), np.float32)
    rm[:, ::64] = 0.0
    blk = np.concatenate([lt, le, gt, eye, rm], axis=1)
    c["rwkv_masks"] = np.ascontiguousarray(np.concatenate([blk, blk], axis=0))
    c["pool_rc"] = np.ascontiguousarray(np.broadcast_to(rc.reshape(1, 64), (128, 64)))
    return c


def shared_inputs(inp, layers):
    m = dict(host_consts())
    m["norm_w"] = np.ascontiguousarray(inp["norm_w"], dtype=np.float32)
    for L, ia in ((0, 0), (3, 1)):
        if L not in layers:
            continue
        pre = "a%d_" % ia
        m[pre + "w_in"] = np.ascontiguousarray(np.asarray(inp["a_w_in"][ia]).reshape(4 * D, D))
        m[pre + "w_out"] = np.ascontiguousarray(inp["a_w_out"][ia])
        m[pre + "mu_pm"] = np.ascontiguousarray(np.concatenate([pm(inp["a_mu"][ia][j]) for j in range(6)], axis=1))
        plist = [inp["a_w0"][ia], inp["a_a0"][ia], inp["a_k_k"][ia], inp["a_k_a"][ia], np.asarray(inp["a_r_k"][ia]).reshape(-1),
                 inp["a_lnx_w"][ia], inp["a_lnx_b"][ia], inp["a_v0"][ia - 1] if ia > 0 else np.zeros(D, np.float32)]
        m[pre + "par_pm"] = np.ascontiguousarray(np.concatenate([pm(v) for v in plist], axis=1))
        m[pre + "w1"] = np.ascontiguousarray(inp["a_w1"][ia])
        m[pre + "a1"] = np.ascontiguousarray(inp["a_a1"][ia])
        m[pre + "w2"] = np.ascontiguousarray(inp["a_w2"][ia])
        m[pre + "a2"] = np.ascontiguousarray(inp["a_a2"][ia])
        if ia > 0:
            m[pre + "v1"] = np.ascontiguousarray(inp["a_v1"][ia - 1])
            m[pre + "v2"] = np.ascontiguousarray(inp["a_v2"][ia - 1])
    if 1 in layers:
        w = np.asarray(inp["b_w_in"][0])
        cols = []
        for hd in range(8):
            order = [(0, 0), (1, 0), (2, 0), None, (0, 1), (1, 1), (2, 1), (0, 2), (1, 2), (2, 2)]
            for o in order:
                if o is None:
                    c0 = 9216 + hd * 128
                else:
                    sidx, g = o
                    c0 = ((sidx * 3 + g) * 8 + hd) * 128
                cols.append(np.arange(c0, c0 + 128))
        cols = np.concatenate(cols)
        m["b_w_in_r"] = np.ascontiguousarray(w[:, cols])
        m["b_w_out"] = np.ascontiguousarray(inp["b_w_out"][0])
        m["b_qkn_pm"] = np.ascontiguousarray(np.concatenate([np.asarray(inp["b_qn_w"][0]).T, np.asarray(inp["b_kn_w"][0]).T], axis=1).astype(np.float32))
    if 2 in layers:
        m["c_w_in"] = np.ascontiguousarray(inp["c_w_in"][0])
        m["c_w_grp"] = np.ascontiguousarray(inp["c_w_grp"][0].reshape(4 * 512, 512))
        m["c_w_out"] = np.ascontiguousarray(inp["c_w_out"][0])
        m["c_scale_pm"] = pm(inp["c_scale"][0])
    return m


_CACHE = {}
FUSED = True


def _prog(layers):
    if layers not in _CACHE:
        pr = Prog(layers)
        pr.build()
        _CACHE[layers] = pr
    return _CACHE[layers]


def kernel(**inp):
    all_layers = (0, 1, 2, 3)
    groups = [all_layers] if FUSED else [(0,), (1,), (2,), (3,)]
    x = np.asarray(inp["x"], np.float32)
    xs = [np.ascontiguousarray(x[c * NB_CORE:(c + 1) * NB_CORE].reshape(NB_CORE * SEQ, D)) for c in range(8)]
    pos = np.asarray(inp["positions"])
    vfs = None
    for layers in groups:
        pr = _prog(layers)
        shared = shared_inputs(inp, layers)
        in_maps = []
        for c in range(8):
            m = dict(shared)
            m["x"] = xs[c]
            m["positions"] = np.ascontiguousarray(pos[c * NB_CORE:(c + 1) * NB_CORE].astype(np.int32))
            if vfs is not None:
                m["vfs"] = vfs[c]
            m = {k: v for k, v in m.items() if k in pr.inputs}
            in_maps.append(m)
        res = run_bass_kernel_spmd(pr.nc, in_maps, core_ids=list(range(8)))
        xs = [np.asarray(res.results[c]["out"]) for c in range(8)]
        if "vfs" in res.results[0]:
            vfs = [np.asarray(res.results[c]["vfs"]) for c in range(8)]
    out = np.stack([xs[c].reshape(NB_CORE, SEQ, D) for c in range(8)], axis=0)
    return out.reshape(16, SEQ, D).astype(np.float32)
```

```python
import itertools
import numpy as np
from contextlib import ExitStack
import concourse.bass as bass
import concourse.mybir as mybir
from concourse.bass_utils import run_bass_kernel_spmd

F32 = mybir.dt.float32
BF16 = mybir.dt.bfloat16
I32 = mybir.dt.int32
AF = mybir.ActivationFunctionType
ALU = mybir.AluOpType
AX = mybir.AxisListType

D = 2048
NFT = 16
SEQ = 2048
NB_CORE = 2
TP = 512
NPB = SEQ // TP
RMS_EPS = 1e-6
NDMA = 24
NDMA_SW = 8
SAME_ENGINE_SYNC = True


class Buf:
    __slots__ = ("w", "r")

    def __init__(self):
        self.w = None
        self.r = {}


class TT:
    def __init__(self, t):
        self.t = t
        self.b = Buf()

    def __getitem__(self, k):
        return self.t[k]


class Sched:
    def __init__(self, nc, es):
        self.nc = nc
        self.eng = {"pe": nc.tensor, "act": nc.scalar, "dve": nc.vector, "pool": nc.gpsimd, "sp": nc.sync}
        self.sem = {}
        self.cnt = {}
        for e in ("pe", "act", "dve", "pool"):
            self.sem[e] = es.enter_context(nc.semaphore("s_" + e))
            self.cnt[e] = 0
        for i in range(NDMA):
            self.sem[("dma", i)] = es.enter_context(nc.semaphore("s_dma%d" % i))
            self.cnt[("dma", i)] = 0
        self.seen = {e: {} for e in self.eng}
        self.rr = 0
        self.rr_sw = 0
        self.out_tokens = []
        self.nwait = 0

    def wait(self, e, tok):
        if tok is None:
            return
        key, val = tok
        if key == e and (e == "pe" or not SAME_ENGINE_SYNC):
            return
        if self.seen[e].get(key, 0) >= val:
            return
        self.eng[e].wait_ge(self.sem[key], val)
        self.seen[e][key] = val
        self.nwait += 1

    def _deps(self, e, reads, writes):
        for b in reads:
            self.wait(e, b.b.w)
        for b in writes:
            self.wait(e, b.b.w)
            for k, v in b.b.r.items():
                self.wait(e, (k, v))

    def _commit(self, tok, reads, writes):
        k, v = tok
        for b in reads:
            if b.b.r.get(k, 0) < v:
                b.b.r[k] = v
        for b in writes:
            b.b.w = tok
            b.b.r = {}

    def op(self, e, fn, reads=(), writes=(), signal=True):
        self._deps(e, reads, writes)
        ins = fn(self.eng[e])
        if signal:
            self.cnt[e] += 1
            ins.then_inc(self.sem[e], 1)
            tok = (e, self.cnt[e])
        else:
            tok = (e, self.cnt[e] + 1)
        self._commit(tok, reads, writes)
        return tok

    def dma(self, q, out, in_, reads=(), writes=(), is_output=False):
        self._deps(q, reads, writes)
        if q == "pool":
            i = self.rr_sw
            self.rr_sw = (self.rr_sw + 1) % NDMA_SW
        else:
            i = NDMA_SW + self.rr
            self.rr = (self.rr + 1) % (NDMA - NDMA_SW)
        key = ("dma", i)
        if self.cnt[key] > 0:
            self.wait(q, (key, self.cnt[key]))
        self.cnt[key] += 16
        self.eng[q].dma_start(out=out, in_=in_).then_inc(self.sem[key], 16)
        tok = (key, self.cnt[key])
        self._commit(tok, reads, writes)
        if is_output:
            self.out_tokens.append(tok)
        return tok

    def barrier(self):
        for e in ("pe", "act", "dve", "pool", "sp"):
            for k in list(self.cnt.keys()):
                if self.cnt[k] > 0 and k != e:
                    self.wait(e, (k, self.cnt[k]))

    def finish(self):
        for tok in self.out_tokens:
            self.wait("sp", tok)
        for e in ("pe", "act", "dve", "pool"):
            if self.cnt[e] > 0:
                self.wait("sp", (e, self.cnt[e]))


def bcast_rows(ap_row, nparts):
    t = ap_row.tensor
    pairs = list(ap_row.ap)
    return bass.AP(t, ap_row.offset, [[0, nparts]] + [list(p) for p in pairs[1:]])


class Prog:
    def __init__(self, layers=(0, 1, 2, 3), n_panels=NB_CORE * NPB):
        self.layers = layers
        self.n_panels = n_panels
        self.nc = bass.Bass("TRN2", target_bir_lowering=False)
        self.es = ExitStack()
        self.inputs = {}
        self.uid = 0
        self.scopes = []

    def dram_in(self, name, shape, dtype=F32):
        if name in self.inputs:
            return self.inputs[name]
        t = self.nc.dram_tensor(name, list(shape), dtype, kind="ExternalInput")
        self.inputs[name] = t
        return t

    def sb(self, name, shape, dtype):
        self.uid += 1
        st = self.scopes[-1] if self.scopes else self.es
        return TT(st.enter_context(self.nc.sbuf_tensor("sb%d_%s" % (self.uid, name), list(shape), dtype)))

    def scope(self):
        prog = self

        class _Scope:
            def __enter__(self_):
                self_.st = ExitStack()
                prog.scopes.append(self_.st)
                return self_

            def __exit__(self_, *a):
                prog.S.barrier()
                prog.scopes.pop()
                self_.st.close()
                return False
        return _Scope()

    def build(self):
        nc = self.nc
        es = self.es
        S = self.S = Sched(nc, es)
        x_in = self.dram_in("x", [NB_CORE * SEQ, D])
        self.ident_d = self.dram_in("ident", [128, 128])
        self.normw_d = self.dram_in("norm_w", [4, D])
        out_d = nc.dram_tensor("out", [NB_CORE * SEQ, D], F32, kind="ExternalOutput")
        nl = len(self.layers)
        scr = [nc.dram_tensor("xs%d" % i, [NB_CORE * SEQ, D], F32, kind="Internal") for i in range(2)]
        chain = [x_in]
        for i in range(nl - 1):
            chain.append(scr[i % 2])
        chain.append(out_d)
        self.xbufs = {}

        def xbuf(t, tile):
            k = (t.name, tile)
            if k not in self.xbufs:
                self.xbufs[k] = TT(None)
            return self.xbufs[k]
        self.xbuf = xbuf

        self.ident = self.sb("ident", [128, 128], BF16)
        S.dma("pool", self.ident[:], self.ident_d.ap()[:, :], writes=[self.ident])
        self.ps = [TT(es.enter_context(nc.psum_tensor("ps%d" % i, [128, 512], F32))) for i in range(8)]
        self.ps_i = 0

        for li, L in enumerate(self.layers):
            src, dst = chain[li], chain[li + 1]
            last = li == nl - 1
            with self.scope():
                if L == 2:
                    self.layer_pool(src, dst, last)
                elif L == 1:
                    self.layer_attn(src, dst, last)
                else:
                    self.layer_rwkv(L, src, dst, last)
        S.finish()
        return nc

    def psum(self):
        p = self.ps[self.ps_i]
        self.ps_i = (self.ps_i + 1) % 6
        return p

    def alloc_wsl(self, n=2):
        self.wsl = [self.sb("wsl%d" % i, [128, NFT, 512], BF16) for i in range(n)]
        self.wsl_i = 0

    def wslab(self):
        w = self.wsl[self.wsl_i]
        self.wsl_i = (self.wsl_i + 1) % len(self.wsl)
        return w

    def load_wslab(self, w_ap2d, c0, ncols=512, nrows_t=NFT):
        w = self.wslab()
        src = w_ap2d[:, c0:c0 + ncols].rearrange("(a p) c -> p a c", p=128)
        self.S.dma("pool", w[:, 0:nrows_t, 0:ncols], src, writes=[w])
        return w

    def alloc_norm(self, layer, nxt):
        self.normw = self.sb("normw", [128, D], F32)
        row = self.normw_d.ap()[layer:layer + 1, :]
        self.S.dma("sp", self.normw[:], bcast_rows(row, 128), writes=[self.normw])
        self.xt = [self.sb("xt%d" % i, [128, D], F32) for i in range(nxt)]
        self.hb = [self.sb("hb%d" % i, [128, D], BF16) for i in range(2)]
        self.stat = [self.sb("stat%d" % i, [128, 4], F32) for i in range(2)]
        self.epsb = self.sb("epsb", [128, 1], F32)
        self.S.op("dve", lambda e: e.memset(self.epsb[:], RMS_EPS), writes=[self.epsb])

    def norm_tiles(self, src, tile0, ntiles, hT):
        S = self.S
        nx = len(self.xt)
        for i in range(ntiles):
            tile = tile0 + i
            xt = self.xt[i % nx]
            st = self.stat[i % 2]
            S.dma("sp", xt[:], src.ap()[tile * 128:(tile + 1) * 128, :], reads=[self.xbuf(src, tile)], writes=[xt])
            hb = self.hb[i % 2]
            S.op("act", lambda e: e.activation(out=hb[:], in_=xt[:], func=AF.Square, accum_out=st[:, 0:1]),
                 reads=[xt], writes=[hb, st])
            S.op("act", lambda e: e.activation(out=st[:, 1:2], in_=st[:, 0:1], func=AF.Sqrt, scale=1.0 / D, bias=self.epsb[:, 0:1]),
                 reads=[st, self.epsb], writes=[st])
            S.op("dve", lambda e: e.reciprocal(out=st[:, 2:3], in_=st[:, 1:2]), reads=[st], writes=[st])
            hb = self.hb[i % 2]
            S.op("dve", lambda e: e.scalar_tensor_tensor(out=hb[:], in0=xt[:], scalar=st[:, 2:3], in1=self.normw[:],
                                                         op0=ALU.mult, op1=ALU.mult),
                 reads=[xt, st, self.normw], writes=[hb])
            for half in range(2):
                p = self.psum()
                pv = p.t[:].bitcast(BF16)
                for j in range(8):
                    ft = half * 8 + j
                    S.op("pe", lambda e: e.transpose(pv[:, j * 128:(j + 1) * 128], hb[:, ft * 128:(ft + 1) * 128],
                                                     self.ident[:]),
                         reads=[hb, self.ident], writes=[p], signal=(j == 7))
                S.op("act", lambda e: e.copy(out=hT[:, half * 8:half * 8 + 8, i * 128:(i + 1) * 128],
                                             in_=pv.rearrange("p (a c) -> p a c", a=8)),
                     reads=[p], writes=[hT])

    def layer_pool(self, src, dst, last):
        S = self.S
        w_in = self.dram_in("c_w_in", [D, 2 * D]).ap()
        w_grp = self.dram_in("c_w_grp", [4 * 512, 512]).ap()
        w_out = self.dram_in("c_w_out", [D, D]).ap()
        scale_d = self.dram_in("c_scale_pm", [128, NFT]).ap()
        rc_d = self.dram_in("pool_rc", [128, 4 * 16]).ap()
        self.alloc_norm(2, 4)
        self.alloc_wsl()
        nc = self.nc
        wsc = nc.dram_tensor("c_wsc", [12, 128, NFT * 512], BF16, kind="Internal")
        wsc_b = [TT(None) for _ in range(12)]
        for idx in range(12):
            wsrc = w_in[:, idx * 512:(idx + 1) * 512] if idx < 8 else w_out[:, (idx - 8) * 512:(idx - 7) * 512]
            S.dma("pool", wsc.ap()[idx].rearrange("p (a c) -> p a c", a=NFT), wsrc.rearrange("(a p) c -> p a c", p=128),
                  writes=[wsc_b[idx]])

        def load_conv(idx):
            w = self.wslab()
            S.dma("sp", w[:].rearrange("p a c -> p (a c)"), wsc.ap()[idx], reads=[wsc_b[idx]], writes=[w])
            return w
        hT = self.sb("hT", [128, NFT, TP], BF16)
        scale = self.sb("c_scale", [128, NFT], F32)
        S.dma("sp", scale[:], scale_d[:, :], writes=[scale])
        rc = self.sb("pool_rc", [128, 4, 16], F32)
        S.dma("sp", rc[:].rearrange("p a b -> p (a b)"), rc_d[:, :], writes=[rc])
        wg = self.sb("wgrp", [128, 16, 512], BF16)
        S.dma("pool", wg[:], w_grp.rearrange("(a p) c -> p a c", p=128), writes=[wg])
        hist = self.sb("hist", [128, NFT, 16], F32)
        ub = [self.sb("ub%d" % i, [128, 16 + TP], F32) for i in range(2)]
        tmp = [self.sb("ptmp%d" % i, [128, 16 + TP], F32) for i in range(2)]
        dT = self.sb("dT", [128, NFT, TP], BF16)
        sg = self.sb("sg", [128, NFT, TP], BF16)
        zT = self.sb("zT", [128, NFT, TP], BF16)
        for panel in range(self.n_panels):
            first = (panel % NPB == 0)
            self.norm_tiles(src, panel * 4, 4, hT)
            if first:
                S.op("dve", lambda e: e.memset(hist[:], 0.0), writes=[hist])
            for s in range(8):
                w = load_conv(s)
                for j in range(4):
                    fo = s * 4 + j
                    p = self.psum()
                    for fi in range(NFT):
                        S.op("pe", lambda e: e.matmul(p[:], w[:, fi, j * 128:(j + 1) * 128], hT[:, fi, :],
                                                      start=(fi == 0), stop=(fi == NFT - 1)),
                             reads=[w, hT], writes=[p], signal=(fi == NFT - 1))
                    if fo < NFT:
                        g = fo // 4
                        win = 2 << g
                        u = ub[fo % 2]
                        S.op("act", lambda e: e.copy(out=u[:, 16:16 + TP], in_=p[:]), reads=[p], writes=[u])
                        S.op("pool", lambda e: e.tensor_copy(out=u[:, 0:16], in_=hist[:, fo, :]), reads=[hist], writes=[u])
                        S.op("pool", lambda e: e.tensor_copy(out=hist[:, fo, :], in_=u[:, TP:TP + 16]), reads=[u], writes=[hist])
                        cur = u
                        sh = 1
                        k = 0
                        while sh < win:
                            nxt = tmp[k % 2]
                            lo = 2 * sh - 1
                            S.op("dve", lambda e: e.tensor_tensor(out=nxt[:, lo:16 + TP], in0=cur[:, lo:16 + TP],
                                                                  in1=cur[:, lo - sh:16 + TP - sh], op=ALU.add),
                                 reads=[cur], writes=[nxt])
                            cur = nxt
                            sh *= 2
                            k += 1
                        S.op("dve", lambda e: e.scalar_tensor_tensor(out=dT[:, fo, :], in0=cur[:, 16:16 + TP], scalar=1.0 / win,
                                                                     in1=u[:, 16:16 + TP], op0=ALU.mult, op1=ALU.subtract),
                             reads=[cur, u], writes=[dT])
                        if first:
                            t2 = tmp[k % 2]
                            S.op("dve", lambda e: e.tensor_tensor(out=t2[:, 0:16], in0=cur[:, 16:32], in1=rc[:, g, :], op=ALU.mult),
                                 reads=[cur, rc], writes=[t2])
                            S.op("dve", lambda e: e.tensor_tensor(out=dT[:, fo, 0:16], in0=t2[:, 0:16], in1=u[:, 16:32], op=ALU.subtract),
                                 reads=[t2, u], writes=[dT])
                    else:
                        S.op("act", lambda e: e.activation(out=sg[:, fo - NFT, :], in_=p[:], func=AF.Silu), reads=[p], writes=[sg])
            for fo in range(NFT):
                g = fo // 4
                jo = fo % 4
                p = self.psum()
                for fi in range(4):
                    S.op("pe", lambda e: e.matmul(p[:], wg[:, g * 4 + fi, jo * 128:(jo + 1) * 128], dT[:, g * 4 + fi, :],
                                                  start=(fi == 0), stop=(fi == 3)),
                         reads=[wg, dT], writes=[p], signal=(fi == 3))
                S.op("dve", lambda e: e.scalar_tensor_tensor(out=zT[:, fo, :], in0=p[:], scalar=scale[:, fo:fo + 1],
                                                             in1=sg[:, fo, :], op0=ALU.mult, op1=ALU.mult),
                     reads=[p, scale, sg], writes=[zT])
            self.out_proj(w_out, zT, NFT, src, dst, panel * 4, last, reload_x=False, loader=lambda c: load_conv(8 + c))

    def out_proj(self, w_out, zT, nfi, src, dst, tile0, last, reload_x, loader=None):
        S = self.S
        if reload_x:
            for i in range(4):
                tile = tile0 + i
                S.dma("sp", self.xt[i][:], src.ap()[tile * 128:(tile + 1) * 128, :], reads=[self.xbuf(src, tile)],
                      writes=[self.xt[i]])
        for c in range(4):
            w = loader(c) if loader is not None else self.load_wslab(w_out, c * 512, nrows_t=nfi)
            for i in range(4):
                p = self.psum()
                for fi in range(nfi):
                    S.op("pe", lambda e: e.matmul(p[:], zT[:, fi, i * 128:(i + 1) * 128], w[:, fi, :],
                                                  start=(fi == 0), stop=(fi == nfi - 1)),
                         reads=[zT, w], writes=[p], signal=(fi == nfi - 1))
                xt = self.xt[i]
                S.op("dve", lambda e: e.tensor_tensor(out=xt[:, c * 512:(c + 1) * 512], in0=xt[:, c * 512:(c + 1) * 512],
                                                      in1=p[:], op=ALU.add), reads=[xt, p], writes=[xt])
        for i in range(4):
            tile = tile0 + i
            S.dma("sp", dst.ap()[tile * 128:(tile + 1) * 128, :], self.xt[i][:], reads=[self.xt[i]],
                  writes=[self.xbuf(dst, tile)], is_output=last)

    def layer_attn(self, src, dst, last):
        S = self.S
        nc = self.nc
        w_in = self.dram_in("b_w_in_r", [D, 8 * 1280]).ap()
        w_out = self.dram_in("b_w_out", [1024, D]).ap()
        qkn_d = self.dram_in("b_qkn_pm", [128, 6]).ap()
        pos_d = self.dram_in("positions", [NB_CORE, SEQ], I32).ap()
        invf_d = self.dram_in("rope_invf", [32, 1]).ap()
        rot_d = self.dram_in("rope_rot", [32, 32]).ap()
        mask_d = self.dram_in("attn_mask", [128, 256]).ap()
        zscr = [nc.dram_tensor("zscr%d" % b, [1024, SEQ], BF16, kind="Internal") for b in range(NB_CORE)]
        zbuf = [[TT(None) for hd in range(8)] for b in range(NB_CORE)]
        hTf = self.sb("hTf", [128, NFT, SEQ], BF16)
        mask = self.sb("amask", [128, 256], BF16)
        S.dma("pool", mask[:], mask_d[:, :], writes=[mask])
        rot = self.sb("rot", [32, 32], BF16)
        S.dma("pool", rot[:], rot_d[:, :], writes=[rot])
        qkn = self.sb("qkn", [128, 6], F32)
        S.dma("sp", qkn[:], qkn_d[:, :], writes=[qkn])
        invf = self.sb("invf", [32, 1], F32)
        S.dma("sp", invf[:], invf_d[:, :], writes=[invf])
        ones = self.sb("ones", [128, 128], BF16)
        S.op("dve", lambda e: e.memset(ones[:], 1.0), writes=[ones])
        DILS = (1, 4, 16)
        SCALE = 128.0 ** -0.5
        PI = float(np.pi)
        C1 = 6.28125
        C2 = float(2 * np.pi - 6.28125)
        nb = self.n_panels // NPB
        for b in range(nb):
            with self.scope():
                self.alloc_norm(1, 2)
                self.norm_tiles(src, b * 16, 16, hTf)
            with self.scope():
                self.alloc_wsl()
                epsb = self.sb("epsb2", [128, 1], F32)
                S.op("dve", lambda e: e.memset(epsb[:], RMS_EPS), writes=[epsb])
                Ctab = self.sb("Ctab", [32, SEQ], F32)
                Stab = self.sb("Stab", [32, SEQ], F32)
                posi = self.sb("posi", [32, 512], I32)
                ra = self.sb("ra", [32, 512], F32)
                rb = self.sb("rb", [32, 512], F32)
                rc_ = self.sb("rc", [32, 512], F32)
                rqi = self.sb("rqi", [32, 512], I32)
                for c in range(4):
                    cs = slice(c * 512, (c + 1) * 512)
                    S.dma("sp", posi[:], bcast_rows(pos_d[b:b + 1, cs], 32), writes=[posi])
                    S.op("dve", lambda e: e.tensor_copy(out=ra[:], in_=posi[:]), reads=[posi], writes=[ra])
                    S.op("dve", lambda e: e.tensor_scalar(out=ra[:], in0=ra[:], scalar1=invf[:, 0:1], scalar2=None, op0=ALU.mult),
                         reads=[ra, invf], writes=[ra])
                    S.op("dve", lambda e: e.tensor_scalar(out=rb[:], in0=ra[:], scalar1=1.0 / (2 * PI), scalar2=None, op0=ALU.mult),
                         reads=[ra], writes=[rb])
                    S.op("dve", lambda e: e.tensor_copy(out=rqi[:], in_=rb[:]), reads=[rb], writes=[rqi])
                    S.op("dve", lambda e: e.tensor_copy(out=rb[:], in_=rqi[:]), reads=[rqi], writes=[rb])
                    S.op("dve", lambda e: e.scalar_tensor_tensor(out=rc_[:], in0=rb[:], scalar=-C1, in1=ra[:], op0=ALU.mult, op1=ALU.add),
                         reads=[rb, ra], writes=[rc_])
                    S.op("dve", lambda e: e.scalar_tensor_tensor(out=ra[:], in0=rb[:], scalar=-C2, in1=rc_[:], op0=ALU.mult, op1=ALU.add),
                         reads=[rb, rc_], writes=[ra])
                    S.op("dve", lambda e: e.tensor_scalar(out=rb[:], in0=ra[:], scalar1=PI, scalar2=-PI, op0=ALU.min, op1=ALU.max),
                         reads=[ra], writes=[rb])
                    S.op("act", lambda e: e.activation(out=Stab[:, cs], in_=rb[:], func=AF.Sin), reads=[rb], writes=[Stab])
                    S.op("dve", lambda e: e.tensor_scalar(out=rc_[:], in0=ra[:], scalar1=PI / 2, scalar2=None, op0=ALU.add),
                         reads=[ra], writes=[rc_])
                    S.op("dve", lambda e: e.tensor_scalar(out=rb[:], in0=rc_[:], scalar1=PI, scalar2=-2 * PI, op0=ALU.is_gt, op1=ALU.mult),
                         reads=[rc_], writes=[rb])
                    S.op("dve", lambda e: e.tensor_tensor(out=rc_[:], in0=rc_[:], in1=rb[:], op=ALU.add), reads=[rc_, rb], writes=[rc_])
                    S.op("dve", lambda e: e.tensor_scalar(out=rc_[:], in0=rc_[:], scalar1=PI, scalar2=-PI, op0=ALU.min, op1=ALU.max),
                         reads=[rc_], writes=[rc_])
                    S.op("act", lambda e: e.activation(out=Ctab[:, cs], in_=rc_[:], func=AF.Sin), reads=[rc_], writes=[Ctab])
                qTb = [self.sb("qT%d" % i, [128, SEQ], BF16) for i in range(2)]
                kTb = [self.sb("kT%d" % i, [128, SEQ], BF16) for i in range(2)]
                Vb = [self.sb("V%d" % i, [128, 16, 128], BF16) for i in range(2)]
                qf = [self.sb("qf%d" % i, [128, 512], F32) for i in range(2)]
                sq = [self.sb("sq%d" % i, [128, 512], BF16) for i in range(2)]
                rs = [self.sb("rs%d" % i, [128, 512], F32) for i in range(2)]
                rt1 = self.sb("rt1", [32, 512], F32)
                rt2 = self.sb("rt2", [32, 512], F32)
                PT = [self.sb("PT%d" % i, [128, 1024], BF16) for i in range(2)]
                acc = self.sb("acc", [128, SEQ], F32)
                lacc = self.sb("lacc", [128, SEQ], F32)
                sgateb = [self.sb("sgate%d" % i, [128, SEQ], BF16) for i in range(2)]
                zst = self.sb("zst", [128, SEQ], BF16)
                cnt = [0]

                def proj_qk(w, coff, dstT, ncol):
                    for c in range(4):
                        cs = slice(c * 512, (c + 1) * 512)
                        p = self.psum()
                        for fi in range(NFT):
                            S.op("pe", lambda e: e.matmul(p[:], w[:, fi, coff:coff + 128], hTf[:, fi, cs],
                                                          start=(fi == 0), stop=(fi == NFT - 1)),
                                 reads=[w, hTf], writes=[p], signal=(fi == NFT - 1))
                        k = cnt[0] % 2
                        cnt[0] += 1
                        S.op("act", lambda e: e.copy(out=qf[k][:], in_=p[:]), reads=[p], writes=[qf[k]])
                        S.op("act", lambda e: e.activation(out=sq[k][:], in_=p[:], func=AF.Square), reads=[p], writes=[sq[k]])
                        yield
                        p2 = self.psum()
                        S.op("pe", lambda e: e.matmul(p2[:], ones[:], sq[k][:], start=True, stop=True),
                             reads=[ones, sq[k]], writes=[p2])
                        S.op("act", lambda e: e.activation(out=rs[k][:], in_=p2[:], func=AF.Ln, scale=1.0 / 128, bias=epsb[:, 0:1]),
                             reads=[p2, epsb], writes=[rs[k]])
                        S.op("act", lambda e: e.activation(out=rs[k][:], in_=rs[k][:], func=AF.Exp, scale=-0.5), reads=[rs[k]], writes=[rs[k]])
                        S.op("dve", lambda e: e.scalar_tensor_tensor(out=dstT[:, cs], in0=qf[k][:], scalar=qkn[:, ncol:ncol + 1],
                                                                     in1=rs[k][:], op0=ALU.mult, op1=ALU.mult),
                             reads=[qf[k], qkn, rs[k]], writes=[dstT])
                        yield
                        p3 = self.psum()
                        S.op("pe", lambda e: e.matmul(p3[0:32, :], rot[:], dstT[0:32, cs], start=True, stop=True),
                             reads=[rot, dstT], writes=[p3])
                        S.op("dve", lambda e: e.tensor_tensor(out=rt1[:], in0=p3[0:32, :], in1=Stab[:, cs], op=ALU.mult),
                             reads=[p3, Stab], writes=[rt1])
                        S.op("pool", lambda e: e.tensor_tensor(out=rt2[:], in0=dstT[0:32, cs], in1=Ctab[:, cs], op=ALU.mult),
                             reads=[dstT, Ctab], writes=[rt2])
                        S.op("dve", lambda e: e.tensor_tensor(out=dstT[0:32, cs], in0=rt1[:], in1=rt2[:], op=ALU.add),
                             reads=[rt1, rt2], writes=[dstT])
                        yield

                def view(ap2d, g):
                    dil = DILS[g]
                    nblk = SEQ // dil // 128
                    return ap2d.rearrange("p (n i r) -> p r n i", n=nblk, i=128, r=dil)

                def blocks(g):
                    dil = DILS[g]
                    nblk = SEQ // dil // 128
                    return [(r, n) for r in range(dil) for n in range(nblk)]

                def proj_item(it, hd, g):
                    qT, kT, V = qTb[it % 2], kTb[it % 2], Vb[it % 2]
                    sgate = sgateb[hd % 2]
                    if g == 0:
                        w = self.load_wslab(w_in, hd * 1280, ncols=512)
                    else:
                        w = self.load_wslab(w_in, hd * 1280 + 512 + (g - 1) * 384, ncols=384)
                    voff = 256
                    yield from proj_qk(w, 0, qT, g)
                    yield from proj_qk(w, 128, kT, 3 + g)
                    bl = blocks(g)
                    for tg in range(4):
                        p = self.psum()
                        for tl in range(4):
                            r, n = bl[tg * 4 + tl]
                            for fi in range(NFT):
                                S.op("pe", lambda e: e.matmul(p[:, tl * 128:(tl + 1) * 128], view(hTf[:, fi, :], g)[:, r, n, :],
                                                              w[:, fi, voff:voff + 128], start=(fi == 0), stop=(fi == NFT - 1)),
                                     reads=[w, hTf], writes=[p], signal=(fi == NFT - 1 and tl == 3))
                        S.op("act", lambda e: e.copy(out=V[:, tg * 4:tg * 4 + 4, :], in_=p[:].rearrange("p (a c) -> p a c", a=4)),
                             reads=[p], writes=[V])
                        yield
                    if g == 0:
                        for c in range(4):
                            cs = slice(c * 512, (c + 1) * 512)
                            p = self.psum()
                            for fi in range(NFT):
                                S.op("pe", lambda e: e.matmul(p[:], w[:, fi, 384:512], hTf[:, fi, cs],
                                                              start=(fi == 0), stop=(fi == NFT - 1)),
                                     reads=[w, hTf], writes=[p], signal=(fi == NFT - 1))
                            S.op("act", lambda e: e.activation(out=sgate[:, cs], in_=p[:], func=AF.Silu), reads=[p], writes=[sgate])
                            yield

                def attn_item(it, hd, g):
                    qT, kT, V = qTb[it % 2], kTb[it % 2], Vb[it % 2]
                    sgate = sgateb[hd % 2]
                    bl = blocks(g)
                    qv = view(qT[:, :], g)
                    kv = view(kT[:, :], g)
                    av = view(acc[:, :], g)
                    lv = view(lacc[:, :], g)
                    for j in range(4):
                        pt = PT[j % 2]
                        pss = [self.psum(), self.psum()]
                        for bi in range(4):
                            r, n = bl[j * 4 + bi]
                            bank = pss[bi // 2]
                            off = (bi % 2) * 256
                            if n > 0:
                                S.op("pe", lambda e: e.matmul(bank[:, off:off + 128], kv[:, r, n - 1, :], qv[:, r, n, :],
                                                              start=True, stop=False),
                                     reads=[kT, qT], writes=[bank], signal=False)
                                S.op("pe", lambda e: e.matmul(bank[:, off:off + 128], self.ident[:], mask[:, 0:128],
                                                              start=False, stop=True),
                                     reads=[self.ident, mask], writes=[bank], signal=False)
                            S.op("pe", lambda e: e.matmul(bank[:, off + 128:off + 256], kv[:, r, n, :], qv[:, r, n, :],
                                                          start=True, stop=False),
                                 reads=[kT, qT], writes=[bank], signal=False)
                            S.op("pe", lambda e: e.matmul(bank[:, off + 128:off + 256], self.ident[:], mask[:, 128:256],
                                                          start=False, stop=True),
                                 reads=[self.ident, mask], writes=[bank], signal=(bi % 2 == 1))
                        for h2 in range(2):
                            S.op("act", lambda e: e.activation(out=pt[:, h2 * 512:(h2 + 1) * 512], in_=pss[h2][:], func=AF.Exp, scale=SCALE),
                                 reads=[pss[h2]], writes=[pt])
                        yield
                        po = self.psum()
                        pl = self.psum()
                        for (bank, isl) in ((po, False), (pl, True)):
                            for bi in range(4):
                                r, n = bl[j * 4 + bi]
                                bidx = j * 4 + bi
                                osl = slice(bi * 128, (bi + 1) * 128)
                                lh_cur = ones[:] if isl else V[:, bidx, :]
                                S.op("pe", lambda e: e.matmul(bank[:, osl], lh_cur, pt[:, bi * 256 + 128:bi * 256 + 256],
                                                              start=True, stop=(n == 0)),
                                     reads=[V, ones, pt], writes=[bank], signal=(n == 0 and bi == 3))
                                if n > 0:
                                    lh_prev = ones[:] if isl else V[:, bidx - 1, :]
                                    S.op("pe", lambda e: e.matmul(bank[:, osl], lh_prev, pt[:, bi * 256:bi * 256 + 128],
                                                                  start=False, stop=True),
                                         reads=[V, ones, pt], writes=[bank], signal=(bi == 3))
                        if g == 0:
                            osel = av[:, 0, 4 * j:4 * j + 4, :]
                            lsel = lv[:, 0, 4 * j:4 * j + 4, :]
                        elif g == 1:
                            osel = av[:, j, :, :]
                            lsel = lv[:, j, :, :]
                        else:
                            osel = av[:, 4 * j:4 * j + 4, 0, :]
                            lsel = lv[:, 4 * j:4 * j + 4, 0, :]
                        pov = po[:].rearrange("p (a c) -> p a c", a=4)
                        plv = pl[:].rearrange("p (a c) -> p a c", a=4)
                        if g == 0:
                            S.op("act", lambda e: e.copy(out=osel, in_=pov), reads=[po], writes=[acc])
                            S.op("dve", lambda e: e.tensor_copy(out=lsel, in_=plv), reads=[pl], writes=[lacc])
                        else:
                            S.op("dve", lambda e: e.tensor_tensor(out=osel, in0=osel, in1=pov, op=ALU.add), reads=[po, acc], writes=[acc])
                            S.op("dve", lambda e: e.tensor_tensor(out=lsel, in0=lsel, in1=plv, op=ALU.add), reads=[pl, lacc], writes=[lacc])
                        yield
                    if g == 2:
                        S.op("act", lambda e: e.activation(out=lacc[:], in_=lacc[:], func=AF.Ln), reads=[lacc], writes=[lacc])
                        S.op("act", lambda e: e.activation(out=lacc[:], in_=lacc[:], func=AF.Exp, scale=-1.0), reads=[lacc], writes=[lacc])
                        S.op("pool", lambda e: e.tensor_tensor(out=acc[:], in0=acc[:], in1=lacc[:], op=ALU.mult), reads=[acc, lacc], writes=[acc])
                        S.op("dve", lambda e: e.tensor_tensor(out=zst[:], in0=acc[:], in1=sgate[:], op=ALU.mult), reads=[acc, sgate], writes=[zst])
                        S.dma("sp", zscr[b].ap()[hd * 128:(hd + 1) * 128, :], zst[:], reads=[zst], writes=[zbuf[b][hd]])

                def weave2(*gens):
                    gens = [g_ for g_ in gens if g_ is not None]
                    while gens:
                        for g_ in list(gens):
                            try:
                                next(g_)
                            except StopIteration:
                                gens.remove(g_)

                prev_attn = None
                it = 0
                for hd in range(8):
                    for g in range(3):
                        weave2(proj_item(it, hd, g), prev_attn)
                        prev_attn = attn_item(it, hd, g)
                        it += 1
                weave2(prev_attn)
            with self.scope():
                self.alloc_wsl()
                self.xt = [self.sb("xt%d" % i, [128, D], F32) for i in range(4)]
                zp = [self.sb("zp%d" % i, [128, 8, 512], BF16) for i in range(2)]
                for pn in range(4):
                    z = zp[pn % 2]
                    S.dma("sp", z[:], zscr[b].ap()[:, pn * 512:(pn + 1) * 512].rearrange("(a p) c -> p a c", p=128),
                          reads=zbuf[b], writes=[z])
                    self.out_proj(w_out, z, 8, src, dst, b * 16 + pn * 4, last, reload_x=True)

    def layer_rwkv(self, L, src, dst, last):
        S = self.S
        nc = self.nc
        ia = 0 if L == 0 else 1
        has_v = (L != 0)
        TR = 256
        NCH = TR // 64
        npan = self.n_panels * TP // TR
        ppb = SEQ // TR
        pre = "a%d_" % ia
        w_in = self.dram_in(pre + "w_in", [4 * D, D]).ap()
        w_out = self.dram_in(pre + "w_out", [D, D]).ap()
        mu_d = self.dram_in(pre + "mu_pm", [128, 6 * NFT]).ap()
        par_d = self.dram_in(pre + "par_pm", [128, 8 * NFT]).ap()
        w1_d = self.dram_in(pre + "w1", [D, 96]).ap()
        a1_d = self.dram_in(pre + "a1", [D, 96]).ap()
        w2_d = self.dram_in(pre + "w2", [96, D]).ap()
        a2_d = self.dram_in(pre + "a2", [96, D]).ap()
        if has_v:
            v1_d = self.dram_in(pre + "v1", [D, 64]).ap()
            v2_d = self.dram_in(pre + "v2", [64, D]).ap()
        cm_d = self.dram_in("rwkv_masks", [128, 512]).ap()
        if not hasattr(self, "vfs"):
            self.vfs_out = False
            if 0 in self.layers:
                self.vfs_out = 3 not in self.layers
                self.vfs = nc.dram_tensor("vfs", [D, NB_CORE * SEQ], F32, kind="ExternalOutput" if self.vfs_out else "Internal")
            else:
                self.vfs = self.dram_in("vfs", [D, NB_CORE * SEQ])
            self.vfs_b = {}
        wsc = nc.dram_tensor(pre + "wsc", [20, 128, NFT * 512], BF16, kind="Internal")
        wsc_b = [TT(None) for _ in range(20)]
        for sl in range(4):
            for j in range(4):
                idx = j * 4 + sl
                S.dma("pool", wsc.ap()[idx].rearrange("p (a c) -> p a c", a=NFT),
                      w_in[j * D:(j + 1) * D, sl * 512:(sl + 1) * 512].rearrange("(a p) c -> p a c", p=128), writes=[wsc_b[idx]])
        for sl in range(4):
            idx = 16 + sl
            S.dma("pool", wsc.ap()[idx].rearrange("p (a c) -> p a c", a=NFT),
                  w_out[:, sl * 512:(sl + 1) * 512].rearrange("(a p) c -> p a c", p=128), writes=[wsc_b[idx]])
        mu = self.sb("mu", [128, 6 * NFT], F32)
        S.dma("sp", mu[:], mu_d[:, :], writes=[mu])
        par = self.sb("par", [128, 8 * NFT], F32)
        S.dma("sp", par[:], par_d[:, :], writes=[par])
        parh = self.sb("parh", [128, 8 * NFT], F32)
        S.op("dve", lambda e: e.tensor_scalar(out=parh[:], in0=par[:], scalar1=0.5, scalar2=None, op0=ALU.mult), reads=[par], writes=[parh])
        omka = self.sb("omka", [128, NFT], F32)
        S.op("dve", lambda e: e.tensor_scalar(out=omka[:], in0=par[:, 3 * NFT:4 * NFT], scalar1=-1.0, scalar2=1.0, op0=ALU.mult, op1=ALU.add),
             reads=[par], writes=[omka])
        P_W0, P_A0, P_KK, P_KA, P_RK, P_LW, P_LB, P_V0 = range(8)

        def pcol(which, ft, t=par):
            return t[:, which * NFT + ft:which * NFT + ft + 1]
        wl = self.sb("wl", [128, NFT, 96], BF16)
        w2 = self.sb("w2", [96, D], BF16)
        S.dma("pool", w2[:], w2_d[:, :], writes=[w2])
        a2 = self.sb("a2", [96, D], BF16)
        S.dma("pool", a2[:], a2_d[:, :], writes=[a2])
        if has_v:
            v2 = self.sb("v2", [64, D], BF16)
            S.dma("pool", v2[:], v2_d[:, :], writes=[v2])
        cmf = self.sb("cmf", [128, 512], F32)
        S.dma("sp", cmf[:], cm_d[:, :], writes=[cmf])
        cmb = self.sb("cmb", [128, 4 * 64], BF16)
        S.op("dve", lambda e: e.tensor_copy(out=cmb[:], in_=cmf[:, 0:256]), reads=[cmf], writes=[cmb])
        id2b = cmb[:, 192:256]
        resetm = cmf[:, 256:256 + TR]
        bo = self.sb("bo", [128, 128], BF16)
        S.op("dve", lambda e: e.memset(bo[:], 0.0), writes=[bo])
        S.op("dve", lambda e: e.memset(bo[0:64, 0:64], 1.0), writes=[bo])
        S.op("dve", lambda e: e.memset(bo[64:128, 64:128], 1.0), writes=[bo])
        epsl = self.sb("epsl", [128, 1], F32)
        S.op("dve", lambda e: e.memset(epsl[:], 64e-5), writes=[epsl])
        self.alloc_norm(L, 2)
        hT = self.sb("hT", [128, NFT, TR], BF16)
        xz = self.sb("xz", [128, NFT, TR], BF16)
        carry = self.sb("carry", [128, NFT], BF16)
        tw = self.sb("tw", [96, TR], BF16)
        la = self.sb("la", [96, TR], BF16)
        lv = self.sb("lv", [64, TR], BF16)
        wsl = [self.sb("rwsl%d" % i, [128, NFT, 512], BF16) for i in range(2)]
        mixb = [self.sb("mixb%d" % i, [128, NFT, TR], BF16) for i in range(4)]
        mixl = self.sb("mixl", [128, NFT, TR], BF16)
        wslc = [0]
        Sf = [self.sb("Sf%d" % i, [128, 64], F32) for i in range(NFT)]
        Sb = [self.sb("Sb%d" % i, [128, 64], BF16) for i in range(NFT)]

        def f32t(name):
            return self.sb(name, [128, TR], F32)

        def bft(name):
            return self.sb(name, [128, TR], BF16)
        Wt, Winv, kk, nrm = [f32t(n) for n in ("Wt", "Winv", "kk", "nrm")]
        thw, tha, thv, thg = Wt, Winv, kk, nrm
        aa, sv, logw, cum, Wend, Wprev, tt, kp, bs, vf_t = [f32t(n) for n in ("aa", "sv", "logw", "cum", "Wend", "Wprev", "tt", "kp", "bs", "vf_t")]
        ksq, bT, kTl, bh, kh, vb = [bft(n) for n in ("ksq", "bT", "kTl", "bh", "kh", "vb")]
        AR = [self.sb("AR%d" % i, [128, 2, TR], BF16) for i in range(3)]
        TM = [self.sb("TM%d" % i, [128, 3, NCH, 64], BF16) for i in range(3)]
        MBs = [self.sb("MBs%d" % i, [128, NCH, 2, 64], BF16) for i in range(3)]
        MKs = [self.sb("MKs%d" % i, [128, NCH, 2, 64], BF16) for i in range(3)]
        TTt = [self.sb("TTt%d" % i, [128, NCH, 64], BF16) for i in range(3)]
        TTf = [self.sb("TTf%d" % i, [128, NCH, 64], F32) for i in range(3)]
        rkb = [self.sb("rkb%d" % i, [128, TR], BF16) for i in range(3)]
        vfl = [self.sb("vfl%d" % i, [128, TR], F32) for i in range(3)]
        Gs = [self.sb("Gs%d" % i, [128, TR], BF16) for i in range(3)]
        WC = [self.sb("WC%d" % i, [128, NCH], F32) for i in range(3)]
        Bp = [self.sb("Bp%d" % i, [128, NCH, 64], BF16) for i in range(6)]
        Bp0 = [self.sb("Bp0_%d" % i, [128, NCH, 64], BF16) for i in range(3)]
        Ap = [self.sb("Ap%d" % i, [128, NCH, 64], BF16) for i in range(2)]
        Xs = self.sb("Xs", [128, 64], BF16)
        Us = self.sb("Us", [128, 64], BF16)
        yc, bon, lnv, ysb = [f32t(n) for n in ("yc", "bon", "lnv", "ysb")]
        ybf, sqb = [bft(n) for n in ("ybf", "sqb")]
        EXPM05 = float(np.exp(-0.5))
        PRS = (slice(0, 64), slice(64, 128))
        pYb = [self.ps[7], self.ps[6]]

        def gen_mix(j, m):
            for fi in range(NFT):
                S.op("dve", lambda e: e.scalar_tensor_tensor(out=m[:, fi, :], in0=xz[:, fi, :], scalar=mu[:, j * NFT + fi:j * NFT + fi + 1],
                                                           in1=hT[:, fi, :], op0=ALU.mult, op1=ALU.add),
                     reads=[xz, mu, hT], writes=[m])

        wlsc = nc.dram_tensor(pre + "wlsc", [3, 128, NFT * 96], BF16, kind="Internal")
        wlsc_b = [TT(None) for _ in range(3)]
        lora_srcs = [(w1_d, 96), (a1_d, 96)] + ([(v1_d, 64)] if has_v else [])
        for li, (wd_, ncol_) in enumerate(lora_srcs):
            S.dma("pool", wlsc.ap()[li][:, 0:NFT * ncol_].rearrange("p (a c) -> p a c", a=NFT),
                  wd_.rearrange("(a p) c -> p a c", p=128), writes=[wlsc_b[li]])

        def lora_down(m, li, ncol, dst_t, func):
            wv = wl[:].rearrange("p a c -> p (a c)")[:, 0:NFT * ncol]
            S.dma("sp", wv, wlsc.ap()[li][:, 0:NFT * ncol], reads=[wlsc_b[li]], writes=[wl])
            wv3 = wv.rearrange("p (a c) -> p a c", a=NFT)
            p = self.psum()
            for fi in range(NFT):
                S.op("pe", lambda e: e.matmul(p[0:ncol, 0:TR], wv3[:, fi, :], m[:, fi, :], start=(fi == 0), stop=(fi == NFT - 1)),
                     reads=[wl, m], writes=[p], signal=(fi == NFT - 1))
            S.op("act", lambda e: e.activation(out=dst_t[:], in_=p[0:ncol, 0:TR], func=func), reads=[p], writes=[dst_t])

        def slab_tt(sl, j):
            return self.xt[j // 2]

        def slab_ap(sl, j, ftl):
            o = (j % 2) * 4 * TR + ftl * TR
            return self.xt[j // 2][:, o:o + TR]

        def gemm_slab(sl):
            for j in range(4):
                m = mixb[j]
                w = wsl[wslc[0] % 2]
                wslc[0] += 1
                idx = j * 4 + sl
                S.dma("sp", w[:].rearrange("p a c -> p (a c)"), wsc.ap()[idx], reads=[wsc_b[idx]], writes=[w])
                dstt = slab_tt(sl, j)
                for ftl in range(4):
                    p = self.psum()
                    for fi in range(NFT):
                        S.op("pe", lambda e: e.matmul(p[:, 0:TR], w[:, fi, ftl * 128:(ftl + 1) * 128], m[:, fi, :],
                                                      start=(fi == 0), stop=(fi == NFT - 1)),
                             reads=[w, m], writes=[p], signal=(fi == NFT - 1))
                    S.op("act", lambda e: e.copy(out=slab_ap(sl, j, ftl), in_=p[:, 0:TR]), reads=[p], writes=[dstt])
                    yield

        def stage_a1(pn, ft, slot):
            tok0 = pn * TR
            ftl = ft % 4
            sl_ = ft // 4
            r_, k_, v_, g_ = [slab_ap(sl_, j, ftl) for j in range(4)]
            Rs, Ks, Vs, Gr = [slab_tt(sl_, j) for j in range(4)]
            fs = slice(ft * 128, (ft + 1) * 128)
            ar, tm, mbs, mks, ttt, ttf = AR[slot], TM[slot], MBs[slot], MKs[slot], TTt[slot], TTf[slot]
            p1 = self.psum()
            S.op("pe", lambda e: e.matmul(p1[:, 0:TR], w2[:, fs], tw[:], start=True, stop=True), reads=[w2, tw], writes=[p1])
            p2 = self.psum()
            S.op("pe", lambda e: e.matmul(p2[:, 0:TR], a2[:, fs], la[:], start=True, stop=True), reads=[a2, la], writes=[p2])
            if has_v:
                p3 = self.psum()
                S.op("pe", lambda e: e.matmul(p3[:, 0:TR], v2[:, fs], lv[:], start=True, stop=True), reads=[v2, lv], writes=[p3])
            S.op("act", lambda e: e.activation(out=thw[:], in_=p1[:, 0:TR], func=AF.Tanh, scale=0.5, bias=pcol(P_W0, ft, parh)),
                 reads=[p1, parh], writes=[thw])
            S.op("act", lambda e: e.activation(out=tha[:], in_=p2[:, 0:TR], func=AF.Tanh, scale=0.5, bias=pcol(P_A0, ft, parh)),
                 reads=[p2, parh], writes=[tha])
            if has_v:
                S.op("act", lambda e: e.activation(out=thv[:], in_=p3[:, 0:TR], func=AF.Tanh, scale=0.5, bias=pcol(P_V0, ft, parh)),
                     reads=[p3, parh], writes=[thv])
            S.op("act", lambda e: e.activation(out=thg[:], in_=g_, func=AF.Tanh, scale=0.5), reads=[Gr], writes=[thg])
            yield
            S.op("dve", lambda e: e.tensor_scalar(out=logw[:], in0=thw[:], scalar1=-0.5 * EXPM05, scalar2=-0.5 * EXPM05, op0=ALU.mult, op1=ALU.add),
                 reads=[thw], writes=[logw])
            S.op("dve", lambda e: e.tensor_scalar(out=aa[:], in0=tha[:], scalar1=0.5, scalar2=0.5, op0=ALU.mult, op1=ALU.add),
                 reads=[tha], writes=[aa])
            S.op("dve", lambda e: e.scalar_tensor_tensor(out=Gs[slot][:], in0=thg[:], scalar=1.0, in1=g_, op0=ALU.add, op1=ALU.mult),
                 reads=[thg, Gr], writes=[Gs[slot]])
            yield
            vkey = (ft, pn)
            if has_v:
                S.op("dve", lambda e: e.tensor_scalar(out=sv[:], in0=thv[:], scalar1=0.5, scalar2=0.5, op0=ALU.mult, op1=ALU.add),
                     reads=[thv], writes=[sv])
                rd = [self.vfs_b[vkey]] if vkey in self.vfs_b else []
                S.dma("sp", vf_t[:], self.vfs.ap()[fs, tok0:tok0 + TR], reads=rd, writes=[vf_t])
                S.op("pool", lambda e: e.tensor_tensor(out=vf_t[:], in0=vf_t[:], in1=v_, op=ALU.subtract), reads=[vf_t, Vs], writes=[vf_t])
                S.op("pool", lambda e: e.tensor_tensor(out=vf_t[:], in0=vf_t[:], in1=sv[:], op=ALU.mult), reads=[vf_t, sv], writes=[vf_t])
                S.op("dve", lambda e: e.tensor_tensor(out=vfl[slot][:], in0=v_, in1=vf_t[:], op=ALU.add), reads=[vf_t, Vs], writes=[vfl[slot]])
            else:
                if vkey not in self.vfs_b:
                    self.vfs_b[vkey] = TT(None)
                S.dma("sp", self.vfs.ap()[fs, tok0:tok0 + TR], v_, reads=[Vs], writes=[self.vfs_b[vkey]], is_output=self.vfs_out)
                S.op("pool", lambda e: e.tensor_copy(out=vfl[slot][:], in_=v_), reads=[Vs], writes=[vfl[slot]])
            vv = vfl[slot]
            S.op("dve", lambda e: e.tensor_tensor_scan(out=cum[:], data0=resetm, data1=logw[:], initial=0.0, op0=ALU.mult, op1=ALU.add),
                 reads=[logw, cmf], writes=[cum])
            cumC_bc = bass.AP(cum.t, 63, [[TR, 128], [64, NCH], [0, 64]])
            cumC = bass.AP(cum.t, 63, [[TR, 128], [64, NCH]])
            c3 = cum[:].rearrange("p (a c) -> p a c", a=NCH)
            S.op("pool", lambda e: e.tensor_tensor(out=Wend[:].rearrange("p (a c) -> p a c", a=NCH), in0=cumC_bc, in1=c3, op=ALU.subtract),
                 reads=[cum], writes=[Wend])
            S.op("dve", lambda e: e.tensor_tensor(out=Wprev[:], in0=cum[:], in1=logw[:], op=ALU.subtract), reads=[cum, logw], writes=[Wprev])
            yield
            S.op("dve", lambda e: e.tensor_scalar(out=kk[:], in0=k_, scalar1=pcol(P_KK, ft), scalar2=None, op0=ALU.mult),
                 reads=[Ks, par], writes=[kk])
            S.op("pool", lambda e: e.tensor_tensor(out=ksq[:], in0=kk[:], in1=kk[:], op=ALU.mult), reads=[kk], writes=[ksq])
            S.op("act", lambda e: e.activation(out=Wt[:], in_=cum[:], func=AF.Exp), reads=[cum], writes=[Wt])
            yield
            S.op("act", lambda e: e.activation(out=Winv[:], in_=cum[:], func=AF.Exp, scale=-1.0), reads=[cum], writes=[Winv])
            S.op("act", lambda e: e.activation(out=WC[slot][:], in_=cumC, func=AF.Exp), reads=[cum], writes=[WC[slot]])
            S.op("act", lambda e: e.activation(out=Wprev[:], in_=Wprev[:], func=AF.Exp), reads=[Wprev], writes=[Wprev])
            yield
            S.op("act", lambda e: e.activation(out=Wend[:], in_=Wend[:], func=AF.Exp), reads=[Wend], writes=[Wend])
            pk = self.psum()
            S.op("pe", lambda e: e.matmul(pk[:, 0:TR], bo[:], ksq[:], start=True, stop=True), reads=[bo, ksq], writes=[pk])
            S.op("dve", lambda e: e.tensor_scalar(out=nrm[:], in0=pk[:, 0:TR], scalar1=1e-24, scalar2=None, op0=ALU.max), reads=[pk], writes=[nrm])
            S.op("act", lambda e: e.activation(out=nrm[:], in_=nrm[:], func=AF.Ln), reads=[nrm], writes=[nrm])
            yield
            S.op("act", lambda e: e.activation(out=nrm[:], in_=nrm[:], func=AF.Exp, scale=-0.5), reads=[nrm], writes=[nrm])
            S.op("act", lambda e: e.copy(out=vb[:], in_=vv[:]), reads=[vv], writes=[vb])
            S.op("dve", lambda e: e.tensor_tensor(out=kk[:], in0=kk[:], in1=nrm[:], op=ALU.mult), reads=[kk, nrm], writes=[kk])
            yield
            S.op("dve", lambda e: e.tensor_scalar(out=tt[:], in0=aa[:], scalar1=pcol(P_KA, ft), scalar2=omka[:, ft:ft + 1],
                                                  op0=ALU.mult, op1=ALU.add), reads=[aa, par, omka], writes=[tt])
            S.op("dve", lambda e: e.tensor_tensor(out=kp[:], in0=k_, in1=tt[:], op=ALU.mult), reads=[Ks, tt], writes=[kp])
            S.op("pool", lambda e: e.tensor_tensor(out=bs[:], in0=kk[:], in1=aa[:], op=ALU.mult), reads=[kk, aa], writes=[bs])
            yield
            S.op("dve", lambda e: e.scalar_tensor_tensor(out=ar[:, 0, :], in0=kk[:], scalar=-1.0, in1=Wprev[:], op0=ALU.mult, op1=ALU.mult),
                 reads=[kk, Wprev], writes=[ar])
            S.op("pool", lambda e: e.tensor_tensor(out=ar[:, 1, :], in0=r_, in1=Wt[:], op=ALU.mult), reads=[Rs, Wt], writes=[ar])
            S.op("dve", lambda e: e.tensor_tensor(out=bT[:], in0=bs[:], in1=Winv[:], op=ALU.mult), reads=[bs, Winv], writes=[bT])
            yield
            S.op("dve", lambda e: e.tensor_tensor(out=kTl[:], in0=kp[:], in1=Winv[:], op=ALU.mult), reads=[kp, Winv], writes=[kTl])
            S.op("dve", lambda e: e.tensor_tensor(out=bh[:], in0=bs[:], in1=Wend[:], op=ALU.mult), reads=[bs, Wend], writes=[bh])
            S.op("pool", lambda e: e.tensor_tensor(out=kh[:], in0=kp[:], in1=Wend[:], op=ALU.mult), reads=[kp, Wend], writes=[kh])
            yield
            S.op("dve", lambda e: e.scalar_tensor_tensor(out=rkb[slot][:], in0=r_, scalar=pcol(P_RK, ft), in1=kp[:], op0=ALU.mult, op1=ALU.mult),
                 reads=[Rs, par, kp], writes=[rkb[slot]])
            pT = self.psum()
            pTb = pT.t[:].bitcast(BF16)
            for wi, srct in enumerate((vb, bh, kh)):
                for c in range(NCH):
                    for hh in range(2):
                        pr = PRS[hh]
                        o0 = (wi * NCH + c) * 64
                        S.op("pe", lambda e: e.transpose(pTb[pr, o0:o0 + 64], srct[pr, c * 64:(c + 1) * 64], id2b[pr, :]),
                             reads=[srct, cmb], writes=[pT], signal=(wi == 2 and c == NCH - 1 and hh == 1))
            S.op("act", lambda e: e.copy(out=tm[:].rearrange("p a b c -> p (a b c)"), in_=pTb[:, 0:3 * NCH * 64]), reads=[pT], writes=[tm])
            yield
            pMB = self.psum()
            pMK = self.psum()
            pN = self.psum()
            for c in range(NCH):
                cs = slice(c * 64, (c + 1) * 64)
                for hh in range(2):
                    pr = PRS[hh]
                    lastm = (c == NCH - 1 and hh == 1)
                    S.op("pe", lambda e: e.matmul(pMB[pr, c * 128:(c + 1) * 128].rearrange("p (a b) -> p a b", a=2), bT[pr, cs], ar[pr, :, cs],
                                                  start=True, stop=True), reads=[bT, ar], writes=[pMB], signal=lastm)
                    S.op("pe", lambda e: e.matmul(pMK[pr, c * 128:(c + 1) * 128].rearrange("p (a b) -> p a b", a=2), kTl[pr, cs], ar[pr, :, cs],
                                                  start=True, stop=True), reads=[kTl, ar], writes=[pMK], signal=lastm)
                    S.op("pe", lambda e: e.matmul(pN[pr, c * 64:(c + 1) * 64], ar[pr, 0, cs], bT[pr, cs],
                                                  start=True, stop=True), reads=[bT, ar], writes=[pN], signal=lastm)
            mk2 = bass.AP(cmf.t, 0, [[512, 128], [0, NCH], [1, 128]])
            mgt = bass.AP(cmf.t, 128, [[512, 128], [0, NCH], [1, 64]])
            idb = bass.AP(cmf.t, 192, [[512, 128], [0, NCH], [1, 64]])
            S.op("dve", lambda e: e.tensor_tensor(out=mbs[:].rearrange("p c a b -> p c (a b)"),
                                                  in0=pMB[:, 0:NCH * 128].rearrange("p (c x) -> p c x", c=NCH), in1=mk2, op=ALU.mult),
                 reads=[pMB, cmf], writes=[mbs])
            S.op("dve", lambda e: e.tensor_tensor(out=Bp0[slot][:], in0=pN[:, 0:NCH * 64].rearrange("p (c x) -> p c x", c=NCH), in1=mgt, op=ALU.mult),
                 reads=[pN, cmf], writes=[Bp0[slot]])
            S.op("dve", lambda e: e.tensor_tensor(out=mks[:].rearrange("p c a b -> p c (a b)"),
                                                  in0=pMK[:, 0:NCH * 128].rearrange("p (c x) -> p c x", c=NCH), in1=mk2, op=ALU.mult),
                 reads=[pMK, cmf], writes=[mks])
            S.op("dve", lambda e: e.tensor_tensor(out=ttf[:], in0=mbs[:, :, 0, :], in1=idb, op=ALU.add), reads=[mbs, cmf], writes=[ttf])
            yield
            S.op("act", lambda e: e.copy(out=ttt[:], in_=ttf[:]), reads=[ttf], writes=[ttt])
            yield

        def stage_a2(ft, slot):
            mbs, ttt, ttf = MBs[slot], TTt[slot], TTf[slot]

            def b_pow(l):
                return Bp0[slot] if l == 0 else Bp[l]

            def a_pow(l):
                if l == 0:
                    return mbs, (lambda c, pr: mbs[pr, c, 0, :])
                t = Ap[(l - 1) % 2]
                return t, (lambda c, pr: t[pr, c, :])

            def sq(l):
                At, Aap = a_pow(l - 1)
                Bo = b_pow(l - 1)
                pB = self.psum()
                pA = self.psum() if l < 5 else None
                for c in range(NCH):
                    for hh in range(2):
                        pr = PRS[hh]
                        lastm = (c == NCH - 1 and hh == 1)
                        S.op("pe", lambda e: e.matmul(pB[pr, c * 64:(c + 1) * 64], Aap(c, pr), Bo[pr, c, :], start=True, stop=True),
                             reads=[At, Bo], writes=[pB], signal=lastm)
                        if pA is not None:
                            S.op("pe", lambda e: e.matmul(pA[pr, c * 64:(c + 1) * 64], Bo[pr, c, :], Aap(c, pr), start=True, stop=True),
                                 reads=[At, Bo], writes=[pA], signal=lastm)
                S.op("act", lambda e: e.copy(out=Bp[l][:].rearrange("p c x -> p (c x)"), in_=pB[:, 0:NCH * 64]), reads=[pB], writes=[Bp[l]])
                if pA is not None:
                    An = Ap[(l - 1) % 2]
                    S.op("dve", lambda e: e.tensor_copy(out=An[:].rearrange("p c x -> p (c x)"), in_=pA[:, 0:NCH * 64]), reads=[pA], writes=[An])

            def tup(l):
                pTu = self.psum()
                Bn = Bp[l]
                for c in range(NCH):
                    for hh in range(2):
                        pr = PRS[hh]
                        lastm = (c == NCH - 1 and hh == 1)
                        S.op("pe", lambda e: e.matmul(pTu[pr, c * 64:(c + 1) * 64], Bn[pr, c, :], ttt[pr, c, :], start=True, stop=True),
                             reads=[Bn, ttt], writes=[pTu], signal=lastm)
                S.op("dve", lambda e: e.tensor_tensor(out=ttf[:].rearrange("p c x -> p (c x)"), in0=ttf[:].rearrange("p c x -> p (c x)"),
                                                      in1=pTu[:, 0:NCH * 64], op=ALU.add), reads=[pTu, ttf], writes=[ttf])
                S.op("act", lambda e: e.copy(out=ttt[:], in_=ttf[:]), reads=[ttf], writes=[ttt])
            for step in (("s", 1), ("s", 2), ("t", 1), ("s", 3), ("t", 2), ("s", 4), ("t", 3), ("s", 5), ("t", 4), ("t", 5)):
                if step[0] == "s":
                    sq(step[1])
                else:
                    tup(step[1])
                yield

        def stage_b(pn, ft, slot):
            ar, tm, mbs, mks, ttt = AR[slot], TM[slot], MBs[slot], MKs[slot], TTt[slot]
            pY = pYb[ft % 2]
            for c in range(NCH):
                cs = slice(c * 64, (c + 1) * 64)
                pX = self.psum()
                for hh in range(2):
                    pr = PRS[hh]
                    S.op("pe", lambda e: e.matmul(pX[pr, 0:64], ar[pr, 0, cs], Sb[ft][pr, :], start=True, stop=False),
                         reads=[ar, Sb[ft]], writes=[pX], signal=False)
                    S.op("pe", lambda e: e.matmul(pX[pr, 0:64], mks[pr, c, 0, :], tm[pr, 0, c, :], start=False, stop=True),
                         reads=[mks, tm], writes=[pX], signal=(hh == 1))
                S.op("act", lambda e: e.copy(out=Xs[:], in_=pX[:, 0:64]), reads=[pX], writes=[Xs])
                yield
                pU = self.psum()
                for hh in range(2):
                    pr = PRS[hh]
                    S.op("pe", lambda e: e.matmul(pU[pr, 0:64], ttt[pr, c, :], Xs[pr, :], start=True, stop=True),
                         reads=[ttt, Xs], writes=[pU], signal=(hh == 1))
                S.op("dve", lambda e: e.tensor_copy(out=Us[:], in_=pU[:, 0:64]), reads=[pU], writes=[Us])
                yield
                for hh in range(2):
                    pr = PRS[hh]
                    S.op("pe", lambda e: e.matmul(pY[pr, cs], Sb[ft][pr, :], ar[pr, 1, cs], start=True, stop=False),
                         reads=[Sb[ft], ar], writes=[pY], signal=False)
                    S.op("pe", lambda e: e.matmul(pY[pr, cs], Us[pr, :], mbs[pr, c, 1, :], start=False, stop=False),
                         reads=[Us, mbs], writes=[pY], signal=False)
                    S.op("pe", lambda e: e.matmul(pY[pr, cs], tm[pr, 0, c, :], mks[pr, c, 1, :], start=False, stop=True),
                         reads=[tm, mks], writes=[pY], signal=(hh == 1))
                pS = self.psum()
                for hh in range(2):
                    pr = PRS[hh]
                    S.op("pe", lambda e: e.matmul(pS[pr, 0:64], tm[pr, 1, c, :], Us[pr, :], start=True, stop=False),
                         reads=[tm, Us], writes=[pS], signal=False)
                    S.op("pe", lambda e: e.matmul(pS[pr, 0:64], tm[pr, 2, c, :], tm[pr, 0, c, :], start=False, stop=True),
                         reads=[tm], writes=[pS], signal=(hh == 1))
                S.op("dve", lambda e: e.scalar_tensor_tensor(out=Sf[ft][:], in0=Sf[ft][:], scalar=WC[slot][:, c:c + 1], in1=pS[:, 0:64],
                                                             op0=ALU.mult, op1=ALU.add), reads=[Sf[ft], WC[slot], pS], writes=[Sf[ft]])
                S.op("act", lambda e: e.copy(out=Sb[ft][:], in_=Sf[ft][:]), reads=[Sf[ft]], writes=[Sb[ft]])
                yield
            S.op("act", lambda e: e.copy(out=ybf[:], in_=pY[:, 0:TR]), reads=[pY], writes=[ybf])
            S.op("dve", lambda e: e.tensor_copy(out=ysb[:], in_=pY[:, 0:TR]), reads=[pY], writes=[ysb])
            p = self.psum()
            S.op("pe", lambda e: e.matmul(p[:, 0:TR], bo[:], ybf[:], start=True, stop=True), reads=[bo, ybf], writes=[p])
            S.op("pool", lambda e: e.tensor_copy(out=yc[:], in_=p[:, 0:TR]) if False else e.tensor_copy(out=bon[:], in_=vfl[slot][:]),
                 reads=[vfl[slot]], writes=[bon])
            S.op("dve", lambda e: e.scalar_tensor_tensor(out=yc[:], in0=p[:, 0:TR], scalar=-1.0 / 64, in1=ysb[:], op0=ALU.mult, op1=ALU.add),
                 reads=[p, ysb], writes=[yc])
            S.op("act", lambda e: e.activation(out=sqb[:], in_=yc[:], func=AF.Square), reads=[yc], writes=[sqb])
            yield
            p2 = self.psum()
            S.op("pe", lambda e: e.matmul(p2[:, 0:TR], bo[:], sqb[:], start=True, stop=True), reads=[bo, sqb], writes=[p2])
            pb = self.psum()
            S.op("pe", lambda e: e.matmul(pb[:, 0:TR], bo[:], rkb[slot][:], start=True, stop=True), reads=[bo, rkb[slot]], writes=[pb])
            S.op("dve", lambda e: e.tensor_tensor(out=bon[:], in0=pb[:, 0:TR], in1=bon[:], op=ALU.mult), reads=[pb, bon], writes=[bon])
            S.op("act", lambda e: e.activation(out=lnv[:], in_=p2[:, 0:TR], func=AF.Ln, scale=1.0 / 64, bias=epsl[:, 0:1]),
                 reads=[p2, epsl], writes=[lnv])
            yield
            S.op("act", lambda e: e.activation(out=lnv[:], in_=lnv[:], func=AF.Exp, scale=-0.5), reads=[lnv], writes=[lnv])
            S.op("dve", lambda e: e.tensor_tensor(out=yc[:], in0=yc[:], in1=lnv[:], op=ALU.mult), reads=[yc, lnv], writes=[yc])
            S.op("dve", lambda e: e.tensor_scalar(out=yc[:], in0=yc[:], scalar1=pcol(P_LW, ft), scalar2=pcol(P_LB, ft),
                                                  op0=ALU.mult, op1=ALU.add), reads=[yc, par], writes=[yc])
            yield
            S.op("pool", lambda e: e.tensor_tensor(out=yc[:], in0=yc[:], in1=bon[:], op=ALU.add), reads=[yc, bon], writes=[yc])
            S.op("dve", lambda e: e.scalar_tensor_tensor(out=xz[:, ft, :], in0=yc[:], scalar=0.5, in1=Gs[slot][:], op0=ALU.mult, op1=ALU.mult),
                 reads=[yc, Gs[slot]], writes=[xz])
            yield

        def weave(*gens):
            gens = [g for g in gens if g is not None]
            while gens:
                for g in list(gens):
                    try:
                        next(g)
                    except StopIteration:
                        gens.remove(g)

        def weave3(ga, gb, gg):
            chains = [g for g in (ga, gb) if g is not None]
            rnd = 0
            while chains or gg is not None:
                for g in list(chains):
                    try:
                        next(g)
                    except StopIteration:
                        chains.remove(g)
                if gg is not None and (rnd % 3 == 1 or not chains):
                    try:
                        next(gg)
                    except StopIteration:
                        gg = None
                rnd += 1

        for pn in range(npan):
            first = (pn % ppb == 0)
            self.norm_tiles(src, pn * 2, 2, hT)
            if first:
                S.op("dve", lambda e: e.memset(carry[:], 0.0), writes=[carry])
                for ft in range(NFT):
                    S.op("pool", lambda e: e.memset(Sf[ft][:], 0.0), writes=[Sf[ft]])
                    S.op("pool", lambda e: e.memset(Sb[ft][:], 0.0), writes=[Sb[ft]])
            S.op("dve", lambda e: e.tensor_tensor(out=xz[:, :, 1:TR], in0=hT[:, :, 0:TR - 1], in1=hT[:, :, 1:TR], op=ALU.subtract),
                 reads=[hT], writes=[xz])
            S.op("dve", lambda e: e.tensor_tensor(out=xz[:, :, 0], in0=carry[:], in1=hT[:, :, 0], op=ALU.subtract),
                 reads=[hT, carry], writes=[xz])
            S.op("dve", lambda e: e.tensor_copy(out=carry[:], in_=hT[:, :, TR - 1]), reads=[hT], writes=[carry])
            gen_mix(4, mixl)
            lora_down(mixl, 0, 96, tw, AF.Tanh)
            gen_mix(0, mixb[0])
            gen_mix(5, mixl)
            lora_down(mixl, 1, 96, la, AF.Copy)
            for j in range(1, 4):
                gen_mix(j, mixb[j])
            if has_v:
                lora_down(mixb[2], 2, 64, lv, AF.Copy)
            for t in range(NFT + 2):
                gens = []
                if 0 <= t - 2 < NFT:
                    gens.append(stage_b(pn, t - 2, (t - 2) % 3))
                if 0 <= t - 1 < NFT:
                    gens.append(stage_a2(t - 1, (t - 1) % 3))
                if t < NFT:
                    g1 = stage_a1(pn, t, t % 3)
                    if t % 4 == 0:
                        g1 = itertools.chain(gemm_slab(t // 4), g1)
                    gens.append(g1)
                weave(*gens)
            for i in range(2):
                tile = pn * 2 + i
                S.dma("sp", self.xt[i][:], src.ap()[tile * 128:(tile + 1) * 128, :], reads=[self.xbuf(src, tile)], writes=[self.xt[i]])
            for c in range(4):
                w = wsl[wslc[0] % 2]
                wslc[0] += 1
                idx = 16 + c
                S.dma("sp", w[:].rearrange("p a c -> p (a c)"), wsc.ap()[idx], reads=[wsc_b[idx]], writes=[w])
                for i in range(2):
                    p = self.psum()
                    for fi in range(NFT):
                        S.op("pe", lambda e: e.matmul(p[:], xz[:, fi, i * 128:(i + 1) * 128], w[:, fi, :],
                                                      start=(fi == 0), stop=(fi == NFT - 1)),
                             reads=[xz, w], writes=[p], signal=(fi == NFT - 1))
                    xt = self.xt[i]
                    S.op("dve", lambda e: e.tensor_tensor(out=xt[:, c * 512:(c + 1) * 512], in0=xt[:, c * 512:(c + 1) * 512],
                                                          in1=p[:], op=ALU.add), reads=[xt, p], writes=[xt])
            for i in range(2):
                tile = pn * 2 + i
                S.dma("sp", dst.ap()[tile * 128:(tile + 1) * 128, :], self.xt[i][:], reads=[self.xt[i]],
                      writes=[self.xbuf(dst, tile)], is_output=last)


def pm(v):
    return np.ascontiguousarray(np.asarray(v, np.float32).reshape(NFT, 128).T)


def host_consts():
    c = {}
    c["ident"] = np.eye(128, dtype=np.float32)
    rc = np.zeros((4, 16), np.float32)
    for g in range(4):
        win = 2 << g
        for t in range(16):
            rc[g, t] = 1.0 / min(t + 1, win)
    invf = (500000.0 ** (-np.arange(0, 32, 2, dtype=np.float32) / 32)).astype(np.float32)
    c["rope_invf"] = np.concatenate([invf, invf]).reshape(32, 1).astype(np.float32)
    rt = np.zeros((32, 32), np.float32)
    for i in range(16):
        rt[i + 16, i] = -1.0
        rt[i, i + 16] = 1.0
    c["rope_rot"] = rt
    kk = np.arange(128)[:, None]
    qq = np.arange(128)[None, :]
    c["attn_mask"] = np.where(np.concatenate([(qq <= kk), (kk <= qq)], axis=1), 0.0, -30000.0).astype(np.float32)
    s_ = np.arange(64)[:, None]
    t_ = np.arange(64)[None, :]
    lt = (s_ < t_).astype(np.float32)
    le = (s_ <= t_).astype(np.float32)
    gt = (s_ > t_).astype(np.float32)
    eye = np.eye(64, dtype=np.float32)
    rm = np.ones((64, 256), np.float32)
    rm[:, ::64] = 0.0
    blk = np.concatenate([lt, le, gt, eye, rm], axis=1)
    c["rwkv_masks"] = np.ascontiguousarray(np.concatenate([blk, blk], axis=0))
    c["pool_rc"] = np.ascontiguousarray(np.broadcast_to(rc.reshape(1, 64), (128, 64)))
    return c


def shared_inputs(inp, layers):
    m = dict(host_consts())
    m["norm_w"] = np.ascontiguousarray(inp["norm_w"], dtype=np.float32)
    for L, ia in ((0, 0), (3, 1)):
        if L not in layers:
            continue
        pre = "a%d_" % ia
        m[pre + "w_in"] = np.ascontiguousarray(np.asarray(inp["a_w_in"][ia]).reshape(4 * D, D))
        m[pre + "w_out"] = np.ascontiguousarray(inp["a_w_out"][ia])
        m[pre + "mu_pm"] = np.ascontiguousarray(np.concatenate([pm(inp["a_mu"][ia][j]) for j in range(6)], axis=1))
        plist = [inp["a_w0"][ia], inp["a_a0"][ia], inp["a_k_k"][ia], inp["a_k_a"][ia], np.asarray(inp["a_r_k"][ia]).reshape(-1),
                 inp["a_lnx_w"][ia], inp["a_lnx_b"][ia], inp["a_v0"][ia - 1] if ia > 0 else np.zeros(D, np.float32)]
        m[pre + "par_pm"] = np.ascontiguousarray(np.concatenate([pm(v) for v in plist], axis=1))
        m[pre + "w1"] = np.ascontiguousarray(inp["a_w1"][ia])
        m[pre + "a1"] = np.ascontiguousarray(inp["a_a1"][ia])
        m[pre + "w2"] = np.ascontiguousarray(inp["a_w2"][ia])
        m[pre + "a2"] = np.ascontiguousarray(inp["a_a2"][ia])
        if ia > 0:
            m[pre + "v1"] = np.ascontiguousarray(inp["a_v1"][ia - 1])
            m[pre + "v2"] = np.ascontiguousarray(inp["a_v2"][ia - 1])
    if 1 in layers:
        w = np.asarray(inp["b_w_in"][0])
        cols = []
        for hd in range(8):
            order = [(0, 0), (1, 0), (2, 0), None, (0, 1), (1, 1), (2, 1), (0, 2), (1, 2), (2, 2)]
            for o in order:
                if o is None:
                    c0 = 9216 + hd * 128
                else:
                    sidx, g = o
                    c0 = ((sidx * 3 + g) * 8 + hd) * 128
                cols.append(np.arange(c0, c0 + 128))
        cols = np.concatenate(cols)
        m["b_w_in_r"] = np.ascontiguousarray(w[:, cols])
        m["b_w_out"] = np.ascontiguousarray(inp["b_w_out"][0])
        m["b_qkn_pm"] = np.ascontiguousarray(np.concatenate([np.asarray(inp["b_qn_w"][0]).T, np.asarray(inp["b_kn_w"][0]).T], axis=1).astype(np.float32))
    if 2 in layers:
        m["c_w_in"] = np.ascontiguousarray(inp["c_w_in"][0])
        m["c_w_grp"] = np.ascontiguousarray(inp["c_w_grp"][0].reshape(4 * 512, 512))
        m["c_w_out"] = np.ascontiguousarray(inp["c_w_out"][0])
        m["c_scale_pm"] = pm(inp["c_scale"][0])
    return m


_CACHE = {}
FUSED = True


def _prog(layers):
    if layers not in _CACHE:
        pr = Prog(layers)
        pr.build()
        _CACHE[layers] = pr
    return _CACHE[layers]


def kernel(**inp):
    all_layers = (0, 1, 2, 3)
    groups = [all_layers] if FUSED else [(0,), (1,), (2,), (3,)]
    x = np.asarray(inp["x"], np.float32)
    xs = [np.ascontiguousarray(x[c * NB_CORE:(c + 1) * NB_CORE].reshape(NB_CORE * SEQ, D)) for c in range(8)]
    pos = np.asarray(inp["positions"])
    vfs = None
    for layers in groups:
        pr = _prog(layers)
        shared = shared_inputs(inp, layers)
        in_maps = []
        for c in range(8):
            m = dict(shared)
            m["x"] = xs[c]
            m["positions"] = np.ascontiguousarray(pos[c * NB_CORE:(c + 1) * NB_CORE].astype(np.int32))
            if vfs is not None:
                m["vfs"] = vfs[c]
            m = {k: v for k, v in m.items() if k in pr.inputs}
            in_maps.append(m)
        res = run_bass_kernel_spmd(pr.nc, in_maps, core_ids=list(range(8)))
        xs = [np.asarray(res.results[c]["out"]) for c in range(8)]
        if "vfs" in res.results[0]:
            vfs = [np.asarray(res.results[c]["vfs"]) for c in range(8)]
    out = np.stack([xs[c].reshape(NB_CORE, SEQ, D) for c in range(8)], axis=0)
    return out.reshape(16, SEQ, D).astype(np.float32)
```
